# Optimizing a Trainium2 kernel written in Bass

```python
import math
import jax
import jax.numpy as jnp
from jax import lax
import numpy as np


D_MODEL = 2048
BATCH = 8
SEQ = 2048
DEPTH = 2

GRID_W = 64
CTX_LEN = 256
EPS = 1e-6
N_SUB = 3
N_MOD = 3 * N_SUB
MACARON_W = 0.5
D_FF = 5632
D_RNN = D_MODEL // 2
RNN_BLOCKS = 16
RNN_BW = D_RNN // RNN_BLOCKS
RNN_CONV = 4
RG_C = 8.0
D_CONV = D_MODEL // 2
CONV_K = 31
N_HEADS = 8
D_QK = 64
D_V = 128
D_Q = N_HEADS * 2 * D_QK
D_ATT = N_HEADS * D_V
ROPE_BASE = 10000.0
ROPE_N = D_QK // 4
Q_BLOCK = 128
N_BRANCH = 3
IN_SPLITS = (D_RNN, D_RNN, 2 * D_CONV, D_Q, D_Q, D_ATT, N_BRANCH * D_MODEL)
D_IN = sum(IN_SPLITS)
F32 = jnp.float32

kernel_name = 'hybrid_rglru_conformer_diffattn_dit'


def rms_norm(x, g):
    xf = x.astype(F32)
    y = xf * lax.rsqrt(jnp.mean(xf * xf, axis=-1, keepdims=True) + EPS)
    return (y * g.astype(F32)).astype(x.dtype)


def layer_norm(x, g, b):
    xf = x.astype(F32)
    mu = jnp.mean(xf, axis=-1, keepdims=True)
    var = jnp.mean(jnp.square(xf - mu), axis=-1, keepdims=True)
    y = (xf - mu) * lax.rsqrt(var + EPS)
    return (y * g.astype(F32) + b.astype(F32)).astype(x.dtype)


def modulate(h, shift, scale):
    return h * (1 + scale) + shift


def swiglu(h, w_gate, w_up, w_down):
    return (jax.nn.silu(h @ w_gate) * (h @ w_up)) @ w_down


def ffn_half_step(s, mods, k, norm_gain, w_gate, w_up, w_down):
    h = modulate(rms_norm(s, norm_gain), mods[:, :, 3 * k], mods[:, :, 3 * k + 1])
    return s + MACARON_W * mods[:, :, 3 * k + 2] * swiglu(h, w_gate, w_up, w_down)


def split_columns(z):
    offs = []
    acc = 0
    for w in IN_SPLITS[:-1]:
        acc += w
        offs.append(acc)
    return jnp.split(z, offs, axis=-1)


def axial_rope_tables(seq_len):
    rows = seq_len // GRID_W
    row = jnp.repeat(jnp.arange(rows, dtype=jnp.int32), GRID_W).astype(F32)
    col = jnp.tile(jnp.arange(GRID_W, dtype=jnp.int32), rows).astype(F32)
    inv = ROPE_BASE ** (-jnp.arange(ROPE_N, dtype=F32) * 2.0 / (2 * ROPE_N))
    ang = jnp.stack([row[:, None] * inv, col[:, None] * inv], axis=1)
    return jnp.cos(ang), jnp.sin(ang)


def axial_rope(x, cos, sin):
    xs = x.astype(F32).reshape(x.shape[:-1] + (2, 2, ROPE_N))
    x1, x2 = xs[..., 0, :], xs[..., 1, :]
    c = cos[None, :, None, None]
    s = sin[None, :, None, None]
    out = jnp.stack([x1 * c - x2 * s, x2 * c + x1 * s], axis=-2)
    return out.reshape(x.shape).astype(x.dtype)


def depthwise_conv(u, w, b):
    out = lax.conv_general_dilated(
        u, w[:, None, :].astype(u.dtype), window_strides=(1,), padding='SAME',
        dimension_numbers=('NWC', 'WIO', 'NWC'), feature_group_count=u.shape[-1])
    return out + b


def linear_scan(a, b):
    def combine(left, right):
        a_l, b_l = left
        a_r, b_r = right
        return a_l * a_r, a_r * b_l + b_r
    return lax.associative_scan(combine, (a, b), axis=1)[1]


def rglru_coeffs(u, w_r, b_r, w_i, b_i, lam):
    nb, t, _ = u.shape
    ub = u.reshape(nb, t, RNN_BLOCKS, RNN_BW)
    r = jax.nn.sigmoid((jnp.einsum('bthi,hij->bthj', ub, w_r).reshape(nb, t, D_RNN) + b_r).astype(F32))
    i = jax.nn.sigmoid((jnp.einsum('bthi,hij->bthj', ub, w_i).reshape(nb, t, D_RNN) + b_i).astype(F32))
    log_a = -RG_C * r * jax.nn.softplus(-lam.astype(F32))
    a = jnp.exp(log_a)
    b = jnp.sqrt(-jnp.expm1(2.0 * log_a)) * i * u.astype(F32)
    return a, b


def rglru_branch(xl, yl, xc, yc, p, need_ctx):
    n_ctx = xc.shape[1]
    ul = depthwise_conv(xl, p['rnn_conv_w'], p['rnn_conv_b'])
    uc = depthwise_conv(xc, p['rnn_conv_w'], p['rnn_conv_b'])
    hs_ctx = []
    hs_lat = []
    for d in range(2):
        if d == 0:
            seq = jnp.concatenate([uc, ul], axis=1)
        else:
            seq = jnp.concatenate([jnp.flip(uc, 1), jnp.flip(ul, 1)], axis=1)
        a, b = rglru_coeffs(seq, p['rg_w_r'][d], p['rg_b_r'][d], p['rg_w_i'][d],
                            p['rg_b_i'][d], p['rg_lam'][d])
        h = linear_scan(a, b)
        hc_d, hl_d = h[:, :n_ctx], h[:, n_ctx:]
        if d == 1:
            hc_d, hl_d = jnp.flip(hc_d, 1), jnp.flip(hl_d, 1)
        hs_ctx.append(hc_d)
        hs_lat.append(hl_d)
    h_lat = hs_lat[0] + hs_lat[1]
    out_l = (jax.nn.gelu(yl) * h_lat.astype(yl.dtype)) @ p['rnn_w_out']
    if not need_ctx:
        return out_l, None
    h_ctx = hs_ctx[0] + hs_ctx[1]
    out_c = (jax.nn.gelu(yc) * h_ctx.astype(yc.dtype)) @ p['rnn_w_out']
    return out_l, out_c


def conformer_conv(g, p):
    u = g[..., :D_CONV] * jax.nn.sigmoid(g[..., D_CONV:])
    u = depthwise_conv(u, p['cv_dw_w'], p['cv_dw_b'])
    u = layer_norm(u, p['cv_ln_g'], p['cv_ln_b'])
    return jax.nn.silu(u) @ p['cv_w_out']


def diff_attention(ql, kl, vl, qc, kc, vc, p, lam_init, cos, sin, need_ctx):
    nb, seq, _ = ql.shape

    def heads_qk(t):
        return t.reshape(t.shape[0], t.shape[1], N_HEADS, 2, D_QK)

    def heads_v(t):
        return t.reshape(t.shape[0], t.shape[1], N_HEADS, D_V)

    ql = axial_rope(heads_qk(ql), cos, sin)
    kl = axial_rope(heads_qk(kl), cos, sin)
    kc = heads_qk(kc)
    vl, vc = heads_v(vl), heads_v(vc)
    lq = p['da_lam'].astype(F32)
    lam = jnp.exp(jnp.sum(lq[0] * lq[1])) - jnp.exp(jnp.sum(lq[2] * lq[3])) + lam_init
    scale = D_QK ** -0.5

    def attend(q, k, v):
        s = jnp.einsum('bqhcd,bkhcd->bhcqk', q.astype(F32), k.astype(F32)) * scale
        pr = jax.nn.softmax(s, axis=-1)
        w = pr[:, :, 0] - lam * pr[:, :, 1]
        return jnp.einsum('bhqk,bkhd->bqhd', w.astype(v.dtype), v)

    k_all = jnp.concatenate([kl, kc], axis=1)
    v_all = jnp.concatenate([vl, vc], axis=1)
    q_blocks = jnp.swapaxes(ql.reshape(nb, seq // Q_BLOCK, Q_BLOCK, N_HEADS, 2, D_QK), 0, 1)
    o_l = lax.map(lambda qb: attend(qb, k_all, v_all), q_blocks)
    o_l = jnp.swapaxes(o_l, 0, 1).reshape(nb, seq, N_HEADS, D_V)

    def head_out(o):
        o = rms_norm(o, p['da_subln_g']) * (1.0 - lam_init)
        return o.reshape(o.shape[0], o.shape[1], D_ATT) @ p['da_w_o']

    out_l = head_out(o_l)
    if not need_ctx:
        return out_l, None
    out_c = head_out(attend(heads_qk(qc), kc, vc))
    return out_l, out_c


def gated_merge(z, b0, b1, b2):
    g = jax.nn.sigmoid(z.astype(F32)).astype(z.dtype)
    g = g.reshape(z.shape[:-1] + (N_BRANCH, D_MODEL))
    return g[..., 0, :] * b0 + g[..., 1, :] * b1 + g[..., 2, :] * b2


def token_mix(hl, hc, p, lam_init, cos, sin, need_ctx):
    xl, yl, gl, ql, kl, vl, zl = split_columns(hl @ p['w_in'])
    xc, yc, gc, qc, kc, vc, zc = split_columns(hc @ p['w_in'])
    rnn_l, rnn_c = rglru_branch(xl, yl, xc, yc, p, need_ctx)
    att_l, att_c = diff_attention(ql, kl, vl, qc, kc, vc, p, lam_init, cos, sin, need_ctx)
    conv_l = conformer_conv(gl, p)
    out_l = gated_merge(zl, rnn_l, conv_l, att_l) @ p['w_out']
    if not need_ctx:
        return out_l, None
    conv_c = conformer_conv(gc, p)
    out_c = gated_merge(zc, rnn_c, conv_c, att_c) @ p['w_out']
    return out_l, out_c


def setup_inputs(seed: int = 0) -> dict:
    key = jax.random.key(seed)
    ks = jax.random.split(key, 29)
    D = D_MODEL

    def nrm(k, shape, s):
        return jax.random.normal(k, shape, F32) * s

    u = jax.random.uniform(ks[17], (DEPTH, 2, D_RNN), F32, 0.9, 0.999)
    base = u ** (1.0 / RG_C)
    rg_lam = jnp.log(base) - jnp.log1p(-base)
    return {
        'x': nrm(ks[0], (BATCH, SEQ, D), 1.0),
        'c': nrm(ks[1], (BATCH, D), 1.0),
        'ctx': nrm(ks[2], (BATCH, CTX_LEN, D), 1.0),
        'c_ctx': nrm(ks[3], (D,), 1.0),
        'ada_w': nrm(ks[4], (DEPTH, D, N_MOD * D), 0.5 * D ** -0.5),
        'ada_b': nrm(ks[5], (DEPTH, N_MOD * D), 0.01),
        'norm_g': 1.0 + nrm(ks[6], (DEPTH, N_SUB, D), 0.05),
        'ffn_w_gate': nrm(ks[7], (DEPTH, 2, D, D_FF), D ** -0.5),
        'ffn_w_up': nrm(ks[8], (DEPTH, 2, D, D_FF), D ** -0.5),
        'ffn_w_down': nrm(ks[9], (DEPTH, 2, D_FF, D), D_FF ** -0.5),
        'w_in': nrm(ks[10], (DEPTH, D, D_IN), D ** -0.5),
        'rnn_conv_w': nrm(ks[11], (DEPTH, RNN_CONV, D_RNN), RNN_CONV ** -0.5),
        'rnn_conv_b': nrm(ks[12], (DEPTH, D_RNN), 0.01),
        'rg_w_r': nrm(ks[13], (DEPTH, 2, RNN_BLOCKS, RNN_BW, RNN_BW), RNN_BW ** -0.5),
        'rg_b_r': nrm(ks[14], (DEPTH, 2, D_RNN), 0.01),
        'rg_w_i': nrm(ks[15], (DEPTH, 2, RNN_BLOCKS, RNN_BW, RNN_BW), RNN_BW ** -0.5),
        'rg_b_i': nrm(ks[16], (DEPTH, 2, D_RNN), 0.01),
        'rg_lam': rg_lam,
        'rnn_w_out': nrm(ks[18], (DEPTH, D_RNN, D), D_RNN ** -0.5),
        'cv_dw_w': nrm(ks[19], (DEPTH, CONV_K, D_CONV), CONV_K ** -0.5),
        'cv_dw_b': nrm(ks[20], (DEPTH, D_CONV), 0.01),
        'cv_ln_g': 1.0 + nrm(ks[21], (DEPTH, D_CONV), 0.05),
        'cv_ln_b': nrm(ks[22], (DEPTH, D_CONV), 0.01),
        'cv_w_out': nrm(ks[23], (DEPTH, D_CONV, D), D_CONV ** -0.5),
        'da_lam': nrm(ks[24], (DEPTH, 4, D_QK), 0.1),
        'da_subln_g': 1.0 + nrm(ks[25], (DEPTH, D_V), 0.05),
        'da_w_o': nrm(ks[26], (DEPTH, D_ATT, D), D_ATT ** -0.5),
        'w_out': nrm(ks[27], (DEPTH, D, D), D ** -0.5),
        'final_g': 1.0 + nrm(ks[28], (D,), 0.05),
    }


def reference(x, c, ctx, c_ctx, ada_w, ada_b, norm_g, ffn_w_gate, ffn_w_up, ffn_w_down,
              w_in, rnn_conv_w, rnn_conv_b, rg_w_r, rg_b_r, rg_w_i, rg_b_i, rg_lam,
              rnn_w_out, cv_dw_w, cv_dw_b, cv_ln_g, cv_ln_b, cv_w_out, da_lam,
              da_subln_g, da_w_o, w_out, final_g):
    cos, sin = axial_rope_tables(x.shape[1])
    for l in range(DEPTH):
        need_ctx = l < DEPTH - 1
        lam_init = 0.8 - 0.6 * math.exp(-0.3 * l)
        mods_l = (jax.nn.silu(c) @ ada_w[l] + ada_b[l]).reshape(c.shape[0], 1, N_MOD, D_MODEL)
        mods_c = (jax.nn.silu(c_ctx) @ ada_w[l] + ada_b[l]).reshape(1, 1, N_MOD, D_MODEL)
        p = {
            'w_in': w_in[l], 'rnn_conv_w': rnn_conv_w[l], 'rnn_conv_b': rnn_conv_b[l],
            'rg_w_r': rg_w_r[l], 'rg_b_r': rg_b_r[l], 'rg_w_i': rg_w_i[l], 'rg_b_i': rg_b_i[l],
            'rg_lam': rg_lam[l], 'rnn_w_out': rnn_w_out[l], 'cv_dw_w': cv_dw_w[l],
            'cv_dw_b': cv_dw_b[l], 'cv_ln_g': cv_ln_g[l], 'cv_ln_b': cv_ln_b[l],
            'cv_w_out': cv_w_out[l], 'da_lam': da_lam[l], 'da_subln_g': da_subln_g[l],
            'da_w_o': da_w_o[l], 'w_out': w_out[l],
        }
        x = ffn_half_step(x, mods_l, 0, norm_g[l, 0], ffn_w_gate[l, 0], ffn_w_up[l, 0], ffn_w_down[l, 0])
        ctx = ffn_half_step(ctx, mods_c, 0, norm_g[l, 0], ffn_w_gate[l, 0], ffn_w_up[l, 0], ffn_w_down[l, 0])
        hl = modulate(rms_norm(x, norm_g[l, 1]), mods_l[:, :, 3], mods_l[:, :, 4])
        hc = modulate(rms_norm(ctx, norm_g[l, 1]), mods_c[:, :, 3], mods_c[:, :, 4])
        mix_l, mix_c = token_mix(hl, hc, p, lam_init, cos, sin, need_ctx)
        x = x + mods_l[:, :, 5] * mix_l
        x = ffn_half_step(x, mods_l, 2, norm_g[l, 2], ffn_w_gate[l, 1], ffn_w_up[l, 1], ffn_w_down[l, 1])
        if need_ctx:
            ctx = ctx + mods_c[:, :, 5] * mix_c
            ctx = ffn_half_step(ctx, mods_c, 2, norm_g[l, 2], ffn_w_gate[l, 1], ffn_w_up[l, 1], ffn_w_down[l, 1])
    return rms_norm(x, final_g)
```

```python
from contextlib import ExitStack
import math
import numpy as np
import concourse.bass as bass
import concourse.mybir as mybir
from concourse.bass_utils import run_bass_kernel_spmd

F32 = mybir.dt.float32
BF16 = mybir.dt.bfloat16
AF = mybir.ActivationFunctionType
ALU = mybir.AluOpType
AX = mybir.AxisListType

D = 2048
NCH = 16
SEQ = 2048
CTX = 256
NT = SEQ + CTX
DFF = 5632
NFF = 44
DIN = 13312
EPS = 1e-6
DEPTH = 2
OFF_X, OFF_Y, OFF_G, OFF_Q, OFF_K, OFF_V, OFF_Z = 0, 1024, 2048, 4096, 5120, 6144, 7168
SEGS = [(0, 256, 1), (256, 512, 0), (768, 512, 0), (1280, 512, 0), (1792, 512, 0)]
PAD = 16
PL = PAD + CTX + PAD + SEQ + PAD


def ppos(t):
    return t + PAD if t < CTX else t + 2 * PAD


_po = {}
_n = 0
for _name, _w in (("adab", 144), ("ng", 48), ("rcw", 32), ("rcb", 8), ("rgbr", 16), ("rgbi", 16), ("rglam", 16),
                  ("cvw", 248), ("cvb", 8), ("cvg", 8), ("cvbb", 8), ("dalam", 256), ("subg", 1), ("fing", 16)):
    _po[_name] = _n
    _n += _w
NPAR = _n

WHOLE = "__whole__"


class _St:
    __slots__ = ("w", "r")

    def __init__(self):
        self.w = None
        self.r = []


class Buf:
    def __init__(self, t, name):
        self.t = t
        self.name = name
        self.st = {}

    def __getitem__(self, idx):
        return self.t[idx]


class Tok:
    __slots__ = ("eng", "seq", "sem", "val")

    def __init__(self, eng, seq=None, sem=None, val=None):
        self.eng = eng
        self.seq = seq
        self.sem = sem
        self.val = val


class Prog:
    ENGS = ("pe", "act", "dve", "pool", "sp")
    NDMA = 8

    def __init__(self, nc, es):
        self.nc = nc
        self.es = es
        self.e = {"pe": nc.tensor, "act": nc.scalar, "dve": nc.vector, "pool": nc.gpsimd, "sp": nc.sync}
        self.sem = {k: es.enter_context(nc.semaphore("s_" + k)) for k in self.ENGS}
        self.nseq = {k: 0 for k in self.ENGS}
        self.ninc = {k: 0 for k in self.ENGS}
        self.last_ins = {k: None for k in self.ENGS}
        self.incs = {k: [] for k in self.ENGS}
        self.recent = {k: [] for k in self.ENGS}
        self.seen = {k: {} for k in self.ENGS}
        self.dsem = {q: [es.enter_context(nc.semaphore("d_%s%d" % (q, i))) for i in range(self.NDMA)]
                     for q in ("sp", "pool")}
        self.dcnt = {q: 0 for q in ("sp", "pool")}
        self.dlast = {q: [0] * self.NDMA for q in ("sp", "pool")}
        self.uid = 0
        self.psb = None
        self.psi = 0

    def sbuf(self, name, shape, dt):
        return Buf(self.es.enter_context(self.nc.sbuf_tensor("sb_" + name, list(shape), dt)), name)

    def dram(self, name, shape, dt, kind="Internal"):
        return Buf(self.nc.dram_tensor(name, list(shape), dt, kind=kind), name)

    def init_psum(self):
        self.psb = [Buf(self.es.enter_context(self.nc.psum_tensor("psb%d" % i, [128, 512], F32)), "ps%d" % i)
                    for i in range(8)]

    def ps(self):
        b = self.psb[self.psi % 4]
        self.psi += 1
        return b

    def _resolve(self, tok):
        if tok.eng == "dma":
            return tok.sem, tok.val
        eng = tok.eng
        lst = self.incs[eng]
        lo, hi = 0, len(lst)
        while lo < hi:
            mid = (lo + hi) // 2
            if lst[mid][0] >= tok.seq:
                hi = mid
            else:
                lo = mid + 1
        if lo < len(lst):
            return self.sem[eng], lst[lo][1]
        rec = self.recent[eng]
        k = 0
        while rec[k][0] < tok.seq:
            k += 1
        sq, ins = rec[k]
        del rec[:k + 1]
        self.ninc[eng] += 1
        ins.then_inc(self.sem[eng], 1)
        lst.append((sq, self.ninc[eng]))
        return self.sem[eng], self.ninc[eng]

    def _wait(self, eng, tok):
        if tok is None:
            return
        if tok.eng == "pe" and eng == "pe":
            return
        sem, val = self._resolve(tok)
        key = id(sem)
        if self.seen[eng].get(key, 0) >= val:
            return
        self.seen[eng][key] = val
        self.e[eng].wait_ge(sem, val)

    @staticmethod
    def _norm(x):
        if isinstance(x, Buf):
            return x, WHOLE
        return x

    @staticmethod
    def _states(buf, key):
        if key == WHOLE:
            if WHOLE not in buf.st:
                buf.st[WHOLE] = _St()
            return list(buf.st.values())
        if key not in buf.st:
            buf.st[key] = _St()
        out = [buf.st[key]]
        if WHOLE in buf.st:
            out.append(buf.st[WHOLE])
        return out

    def deps(self, eng, reads, writes):
        for x in reads:
            buf, key = self._norm(x)
            for st in self._states(buf, key):
                self._wait(eng, st.w)
        for x in writes:
            buf, key = self._norm(x)
            for st in self._states(buf, key):
                self._wait(eng, st.w)
                for r in st.r:
                    self._wait(eng, r)

    def record(self, tok, reads, writes):
        for x in reads:
            buf, key = self._norm(x)
            sts = self._states(buf, key) if key == WHOLE else [self._states(buf, key)[0]]
            for st in sts:
                if tok.eng != "dma":
                    st.r = [r for r in st.r if r.eng != tok.eng]
                st.r.append(tok)
        for x in writes:
            buf, key = self._norm(x)
            if key == WHOLE:
                buf.st = {WHOLE: _St()}
                buf.st[WHOLE].w = tok
            else:
                st = self._states(buf, key)[0]
                st.w = tok
                st.r = []

    def op(self, eng, fn, reads=(), writes=()):
        self.deps(eng, reads, writes)
        ins = fn()
        self.nseq[eng] += 1
        self.last_ins[eng] = ins
        rec = self.recent[eng]
        rec.append((self.nseq[eng], ins))
        if len(rec) > 96:
            del rec[:32]
        self.record(Tok(eng, seq=self.nseq[eng]), reads, writes)
        return ins

    def group(self, eng, fns, reads=(), writes=()):
        self.deps(eng, reads, writes)
        ins = None
        for fn in fns:
            ins = fn()
            self.nseq[eng] += 1
        self.last_ins[eng] = ins
        rec = self.recent[eng]
        rec.append((self.nseq[eng], ins))
        if len(rec) > 96:
            del rec[:32]
        self.record(Tok(eng, seq=self.nseq[eng]), reads, writes)

    def dma(self, q, out, in_, reads=(), writes=(), **kw):
        self.deps(q, reads, writes)
        j = self.dcnt[q]
        self.dcnt[q] += 1
        i = j % self.NDMA
        s = self.dsem[q][i]
        rnd = j // self.NDMA
        if rnd > 0:
            key = id(s)
            if self.seen[q].get(key, 0) < 16 * rnd:
                self.seen[q][key] = 16 * rnd
                self.e[q].wait_ge(s, 16 * rnd)
        ins = self.e[q].dma_start(out=out, in_=in_, **kw)
        ins.then_inc(s, 16)
        self.dlast[q][i] = 16 * (rnd + 1)
        tok = Tok("dma", sem=s, val=16 * (rnd + 1))
        self.record(tok, reads, writes)
        return tok

    def barrier(self, final=False):
        toks = [Tok(e, seq=self.nseq[e]) for e in ("pe", "act", "dve") if self.nseq[e] > 0]
        dt = [Tok("dma", sem=self.dsem["sp"][i], val=self.dlast["sp"][i]) for i in range(self.NDMA)
              if self.dlast["sp"][i] > 0]
        if final:
            dt += [Tok("dma", sem=self.dsem["pool"][i], val=self.dlast["pool"][i]) for i in range(self.NDMA)
                   if self.dlast["pool"][i] > 0]
        for e in ("pe", "act", "dve", "sp"):
            for t in toks:
                if t.eng != e:
                    self._wait(e, t)
            for t in dt:
                self._wait(e, t)


class Phase:
    def __init__(self, P):
        self.P = P
        self.es = ExitStack()

    def __enter__(self):
        self.es.__enter__()
        return self

    def sbuf(self, name, shape, dt):
        self.P.uid += 1
        return Buf(self.es.enter_context(self.P.nc.sbuf_tensor("sp_%s_%d" % (name, self.P.uid), list(shape), dt)), name)

    def __exit__(self, *a):
        if a[0] is None:
            self.P.barrier()
        return self.es.__exit__(*a)


class Ring:
    def __init__(self, P, nslots, elems, tag=""):
        self.P = P
        self.slots = [P.sbuf("wring%s%d" % (tag, i), [128, elems], BF16) for i in range(nslots)]
        self.i = 0

    def load(self, src, a, b):
        s = self.slots[self.i % len(self.slots)]
        self.i += 1
        view = s.t[:, 0:a * b].rearrange("p (a b) -> p a b", a=a)
        self.P.dma("pool", view, src, writes=[s])
        return s, view


def build_program(debug=(), stop_after=None):
    nc = bass.Bass("TRN2", target_bir_lowering=False)
    dbg = {}
    with ExitStack() as es:
        P = Prog(nc, es)
        P.init_psum()
        IN = "ExternalInput"
        xT = P.dram("xT", [NCH, 128, NT], F32, kind=IN)
        ccd = P.dram("cc", [128, NCH, 2], F32, kind=IN)
        pard = P.dram("par", [DEPTH, 128, NPAR], F32, kind=IN)
        rgwd = P.dram("rgw", [DEPTH, 128, 8, 4, 128], F32, kind=IN)
        cosd = P.dram("cosT", [128, SEQ], F32, kind=IN)
        sind = P.dram("sinT", [128, SEQ], F32, kind=IN)
        cstd = P.dram("cst", [128, 3, 128], F32, kind=IN)
        cvdd = P.dram("cvd", [DEPTH, 128, 8, 31, 128], F32, kind=IN)
        ada_w = P.dram("ada_w", [DEPTH, D, 9 * D], F32, kind=IN)
        w_gate = P.dram("ffn_w_gate", [DEPTH, 2, D, DFF], F32, kind=IN)
        w_up = P.dram("ffn_w_up", [DEPTH, 2, D, DFF], F32, kind=IN)
        w_down = P.dram("ffn_w_down", [DEPTH, 2, DFF, D], F32, kind=IN)
        w_in = P.dram("w_in", [DEPTH, D, DIN], F32, kind=IN)
        rnn_wo = P.dram("rnn_w_out", [DEPTH, 1024, D], F32, kind=IN)
        cv_wo = P.dram("cv_w_out", [DEPTH, 1024, D], F32, kind=IN)
        da_wo = P.dram("da_w_o", [DEPTH, 1024, D], F32, kind=IN)
        w_out = P.dram("w_out", [DEPTH, D, D], F32, kind=IN)
        yT = P.dram("yT", [NCH, 128, SEQ], F32, kind="ExternalOutput")
        xs = P.dram("xs", [NCH, 128, NT], F32)
        mscr = P.dram("mscr", [3, 8, 128, NT], BF16, kind="ExternalOutput" if "m" in debug else "Internal")
        zsig = P.dram("zsig", [48, 128, NT], F32)
        cvo = P.dram("cvo", [8, 128, NT], F32)

        def dbgx(name):
            if name in debug:
                t = P.dram("dbg_" + name, [NCH, 128, NT], F32, kind="ExternalOutput")
                P.dma("sp", t.t.ap(), xs.t.ap(), reads=[xs], writes=[t])
                dbg[name] = t

        xsv = xs.t.ap().rearrange("c p t -> p c t")

        ring = Ring(P, 5, 6144)
        par = P.sbuf("par", [128, DEPTH, NPAR], F32)
        cst = P.sbuf("cst", [128, 3, 128], F32)
        onesb = P.sbuf("onesb", [128, 128], BF16)
        modr = P.sbuf("modr", [128, DEPTH, 144, 2], F32)
        modA = P.sbuf("modA", [128, DEPTH, 3, NCH, 2], F32)
        modG = P.sbuf("modG", [128, DEPTH, 3, NCH, 2], F32)
        cneg = P.sbuf("cneg", [128, DEPTH, 2, 16], F32)
        lamv = P.sbuf("lamv", [128, DEPTH, 4], F32)
        P.dma("sp", par[:], pard.t.ap().rearrange("l p n -> p l n"), writes=[par])
        P.dma("sp", cst[:], cstd.t.ap(), writes=[cst])
        P.op("dve", lambda: nc.vector.memset(onesb[:], 1.0), writes=[onesb])
        ones128b = P.sbuf("ones128b", [128, 128], BF16)
        P.op("dve", lambda: nc.vector.memset(ones128b[:], 1.0 / 128.0), writes=[ones128b])
        ones2048b = P.sbuf("ones2048b", [128, 128], BF16)
        P.op("dve", lambda: nc.vector.memset(ones2048b[:], 1.0 / 2048.0), writes=[ones2048b])
        epsc = P.sbuf("epsc", [128, 2], F32)
        P.op("dve", lambda: nc.vector.memset(epsc[:], EPS), writes=[epsc])
        ones2048 = cst[:, 0, :]
        ones128 = cst[:, 1, :]
        rperm = cst[:, 2, :]
        ones1024 = None

        def pr(l, name, i0=0, n=1):
            o = _po[name] + i0
            return par[:, l, o:o + n]

        for c in range(NCH):
            P.dma("sp", xs.t.ap()[c], xT.t.ap()[c], writes=[(xs, c)])
        P.barrier()

        scb = P.sbuf("scb", [128, NCH, 2], BF16)
        with Phase(P) as ph:
            ccs = ph.sbuf("ccs", [128, NCH, 2], F32)
            tmp = ph.sbuf("tmpm", [128, 64], F32)
            tmp2 = ph.sbuf("tmpm2", [128, 64], F32)
            P.dma("sp", ccs[:], ccd.t.ap(), writes=[ccs])
            P.op("act", lambda: nc.scalar.activation(scb[:], ccs[:], AF.Silu), reads=[ccs], writes=[scb])
            for l in range(DEPTH):
                lam_init = 0.8 - 0.6 * math.exp(-0.3 * l)
                P.op("act", lambda: nc.scalar.activation(tmp[:, 0:16], pr(l, "rglam", 0, 16), AF.Exp, scale=-1.0),
                     reads=[par], writes=[tmp])
                P.op("act", lambda: nc.scalar.activation(tmp2[:, 0:16], tmp[:, 0:16], AF.Ln, bias=1.0, scale=1.0),
                     reads=[tmp], writes=[tmp2])
                P.op("dve", lambda: nc.vector.tensor_scalar_mul(cneg[:, l, 0, :], tmp2[:, 0:16], -8.0),
                     reads=[tmp2], writes=[(cneg, (l, 0))])
                P.op("dve", lambda: nc.vector.tensor_scalar_mul(cneg[:, l, 1, :], tmp2[:, 0:16], -16.0),
                     reads=[tmp2], writes=[(cneg, (l, 1))])
                dl = _po["dalam"]
                P.op("dve", lambda: nc.vector.tensor_tensor(tmp[:, 0:64], par[:, l, dl:dl + 64], par[:, l, dl + 64:dl + 128],
                                                            ALU.mult), reads=[par, tmp2], writes=[tmp])
                P.op("dve", lambda: nc.vector.reduce_sum(tmp2[:, 0:1], tmp[:, 0:64], AX.X), reads=[tmp], writes=[tmp2])
                P.op("dve", lambda: nc.vector.tensor_tensor(tmp[:, 0:64], par[:, l, dl + 128:dl + 192],
                                                            par[:, l, dl + 192:dl + 256], ALU.mult),
                     reads=[par, tmp2], writes=[tmp])
                P.op("dve", lambda: nc.vector.reduce_sum(tmp2[:, 1:2], tmp[:, 0:64], AX.X), reads=[tmp], writes=[tmp2])
                P.op("act", lambda: nc.scalar.activation(tmp[:, 0:2], tmp2[:, 0:2], AF.Exp), reads=[tmp2], writes=[tmp])
                P.op("dve", lambda: nc.vector.scalar_tensor_tensor(lamv[:, l, 0:1], tmp[:, 1:2], -lam_init, tmp[:, 0:1],
                                                                   ALU.add, ALU.subtract), reads=[tmp], writes=[(lamv, (l, 0))])
                P.op("dve", lambda: nc.vector.tensor_scalar_mul(lamv[:, l, 1:2], pr(l, "subg"), 1.0 - lam_init),
                     reads=[par], writes=[(lamv, (l, 1))])

        mps = P.psb[7]

        def ada_part(l, k):
            awv = ada_w.t.ap()[l].rearrange("(kc p) n -> p kc n", p=128)
            def mods_mm(nb, ws, wv):
                for j in range(2):
                    n = nb * 2 + j
                    P.group("pe", [
                        (lambda kc=kc: nc.tensor.matmul(mps[:, 2 * n:2 * n + 2], wv[:, kc, j * 128:(j + 1) * 128],
                                                        scb[:, kc, :], start=(kc == 0), stop=(kc == NCH - 1)))
                        for kc in range(NCH)], reads=[ws, scb], writes=[(mps, k)])
            prev = None
            for nb in range(24 * k, 24 * (k + 1)):
                ws, wv = ring.load(awv[:, :, nb * 256:(nb + 1) * 256], NCH, 256)
                if prev is not None:
                    mods_mm(*prev)
                prev = (nb, ws, wv)
                yield
            mods_mm(*prev)
            mv = mps[:, 0:288].rearrange("p (n j) -> p n j", j=2)
            for j in range(2):
                P.op("dve", lambda: nc.vector.tensor_tensor(modr[:, l, 48 * k:48 * (k + 1), j], mv[:, 48 * k:48 * (k + 1), j],
                                                            pr(l, "adab", 48 * k, 48), ALU.add),
                     reads=[(mps, k), par], writes=[(modr, (l, k))])
            for j in range(2):
                P.op("dve", lambda: nc.vector.scalar_tensor_tensor(
                    modA[:, l, k, :, j], modr[:, l, (3 * k + 1) * 16:(3 * k + 2) * 16, j], 1.0,
                    pr(l, "ng", k * 16, 16), ALU.add, ALU.mult), reads=[(modr, (l, k)), par], writes=[(modA, (l, k))])
                P.op("dve", lambda: nc.vector.tensor_scalar_mul(
                    modG[:, l, k, :, j], modr[:, l, (3 * k + 2) * 16:(3 * k + 3) * 16, j], 1.0 if k == 1 else 0.5),
                    reads=[(modr, (l, k))], writes=[(modG, (l, k))])

        pending = []

        def ada_step():
            while pending:
                try:
                    next(pending[0])
                    return
                except StopIteration:
                    pending.pop(0)

        def ada_drain():
            while pending:
                ada_step()

        pending.append(ada_part(0, 0))
        ada_drain()

        def shiftp(l, k, c, j):
            return modr[:, l, (3 * k) * 16 + c, j:j + 1]

        def rsqrt_eps(dst, src, n):
            P.op("act", lambda: nc.scalar.activation(dst[:, 0:n], src[:, 0:n], AF.Ln, bias=epsc[:, 0:1], scale=1.0),
                 reads=[src, epsc], writes=[dst])
            P.op("act", lambda: nc.scalar.activation(dst[:, 0:n], dst[:, 0:n], AF.Exp, scale=-0.5), reads=[dst], writes=[dst])

        def norm_tile(ph, xt, sqb, rstd, start, n, consume, gains):
            P.dma("sp", xt[:, :, 0:n], xsv[:, :, start:start + n], reads=[xs], writes=[xt])
            sp_ = P.ps()
            for c in range(NCH):
                q = sqb[c % 2]
                P.op("act", lambda: nc.scalar.activation(q[:, 0:n], xt[:, c, 0:n], AF.Square), reads=[xt], writes=[q])
                P.op("pe", lambda: nc.tensor.matmul(sp_[:, 0:n], ones2048b[:], q[:, 0:n], start=(c == 0), stop=(c == NCH - 1)),
                     reads=[q, ones2048b], writes=[sp_])
            rsqrt_eps(rstd, sp_, n)
            for c in range(NCH):
                consume(c)

        def subtiles(with_ctx):
            if with_ctx:
                return [[(0, 256, 1), (256, 512, 0), (768, 384, 0)], [(1152, 512, 0), (1664, 512, 0), (2176, 128, 0)]]
            return [[(256, 512, 0), (768, 512, 0)], [(1280, 512, 0), (1792, 512, 0)]]

        def ffn(l, k, with_ctx):
            kn = 0 if k == 0 else 2
            KN = kn
            wgv = w_gate.t.ap()[l, k].rearrange("(kc p) n -> p kc n", p=128)
            wuv = w_up.t.ap()[l, k].rearrange("(kc p) n -> p kc n", p=128)
            wdv = w_down.t.ap()[l, k].rearrange("(f p) n -> p f n", p=128)
            for ST in subtiles(with_ctx):
                T = sum(s[1] for s in ST)
                offs = []
                o = 0
                for s in ST:
                    offs.append(o)
                    o += s[1]
                with Phase(P) as ph:
                    hT = ph.sbuf("hT", [128, NCH, T], BF16)
                    act = ph.sbuf("act", [128, 22, T], BF16)
                    xt = ph.sbuf("xt", [128, NCH, 256], F32)
                    sqb = [ph.sbuf("sq%d" % i, [128, 512], BF16) for i in range(2)]
                    tmpb = [ph.sbuf("tm%d" % i, [128, 512], F32) for i in range(2)]
                    rstd = ph.sbuf("rstd", [128, 512], F32)
                    xr = [ph.sbuf("xr%d" % i, [128, 512], F32) for i in range(2)]
                    xo = [ph.sbuf("xo%d" % i, [128, 512], F32) for i in range(2)]
                    cnt = [0]
                    halves = []
                    for si, (start, n, isc) in enumerate(ST):
                        for h0 in range(0, n, 256):
                            halves.append((si, start + h0, min(256, n - h0), isc, offs[si] + h0))
                    for (si, start, n, isc, off) in halves:

                        def consume(c, start=start, n=n, isc=isc, off=off, si=si):
                            tb = tmpb[c % 2]
                            P.op("dve", lambda: nc.vector.scalar_tensor_tensor(
                                tb[:, 0:n], xt[:, c, 0:n], modA[:, l, kn, c, isc:isc + 1], rstd[:, 0:n], ALU.mult, ALU.mult),
                                reads=[xt, rstd, (modA, (l, KN))], writes=[tb])
                            P.op("act", lambda: nc.scalar.activation(hT[:, c, off:off + n], tb[:, 0:n], AF.Identity,
                                                                     bias=shiftp(l, kn, c, isc), scale=1.0),
                                 reads=[tb, (modr, (l, KN))], writes=[(hT, si)])
                        norm_tile(ph, xt, sqb, rstd, start, n, consume, None)
                    for half in range(2):
                        for fp in range(11):
                            c0 = (half * 22 + fp * 2) * 128
                            gs, gv = ring.load(wgv[:, :, c0:c0 + 256], NCH, 256)
                            us, uv = ring.load(wuv[:, :, c0:c0 + 256], NCH, 256)
                            ada_step()
                            for j in range(2):
                                f = fp * 2 + j
                                for si, (start, n, isc) in enumerate(ST):
                                    off = offs[si]
                                    pg = P.ps()
                                    pu = P.ps()
                                    P.group("pe", [(lambda kc=kc: nc.tensor.matmul(
                                        pg[:, 0:n], gv[:, kc, j * 128:(j + 1) * 128], hT[:, kc, off:off + n],
                                        start=(kc == 0), stop=(kc == NCH - 1))) for kc in range(NCH)],
                                        reads=[gs, (hT, si)], writes=[pg])
                                    P.group("pe", [(lambda kc=kc: nc.tensor.matmul(
                                        pu[:, 0:n], uv[:, kc, j * 128:(j + 1) * 128], hT[:, kc, off:off + n],
                                        start=(kc == 0), stop=(kc == NCH - 1))) for kc in range(NCH)],
                                        reads=[us, (hT, si)], writes=[pu])
                                    tb = tmpb[cnt[0] % 2]
                                    cnt[0] += 1
                                    P.op("act", lambda: nc.scalar.activation(tb[:, 0:n], pg[:, 0:n], AF.Silu),
                                         reads=[pg], writes=[tb])
                                    P.op("dve", lambda: nc.vector.tensor_tensor(act[:, f, off:off + n], tb[:, 0:n], pu[:, 0:n],
                                                                                ALU.mult),
                                         reads=[tb, pu], writes=[(act, (f, si))])
                        for dp in range(8):
                            ds_, dv = ring.load(wdv[:, half * 22:(half + 1) * 22, dp * 256:(dp + 1) * 256], 22, 256)
                            ada_step()
                            for j in range(2):
                                d = dp * 2 + j
                                for si, (start, n, isc) in enumerate(ST):
                                    off = offs[si]
                                    pd = P.ps()
                                    P.group("pe", [(lambda f=f: nc.tensor.matmul(
                                        pd[:, 0:n], dv[:, f, j * 128:(j + 1) * 128], act[:, f, off:off + n],
                                        start=(f == 0), stop=(f == 21))) for f in range(22)],
                                        reads=[ds_] + [(act, (f, si)) for f in range(22)], writes=[pd])
                                    b = cnt[0] % 2
                                    cnt[0] += 1
                                    P.dma("sp", xr[b][:, 0:n], xs.t.ap()[d, :, start:start + n], reads=[(xs, (d, start))],
                                          writes=[xr[b]])
                                    P.op("dve", lambda: nc.vector.scalar_tensor_tensor(
                                        xo[b][:, 0:n], pd[:, 0:n], modG[:, l, kn, d, isc:isc + 1], xr[b][:, 0:n],
                                        ALU.mult, ALU.add), reads=[pd, xr[b], (modG, (l, kn))], writes=[xo[b]])
                                    P.dma("sp", xs.t.ap()[d, :, start:start + n], xo[b][:, 0:n], reads=[xo[b]],
                                          writes=[(xs, (d, start))])

        def proj(ws, wv, j0, hT, seg, si):
            start, n, isc = seg
            p_ = P.ps()
            P.group("pe", [(lambda kc=kc: nc.tensor.matmul(p_[:, 0:n], wv[:, kc, j0:j0 + 128], hT[:, kc, start:start + n],
                                                           start=(kc == 0), stop=(kc == NCH - 1))) for kc in range(NCH)],
                    reads=[ws, (hT, si)], writes=[p_])
            return p_

        def mixer(l, need_ctx):
            KN = 1
            lam_init = 0.8 - 0.6 * math.exp(-0.3 * l)
            winv = w_in.t.ap()[l].rearrange("(kc p) n -> p kc n", p=128)
            mv = mscr.t.ap()
            with Phase(P) as phh:
                hT = phh.sbuf("hmix", [128, NCH, NT], BF16)
                with Phase(P) as ph:
                    xt = ph.sbuf("xt", [128, NCH, 512], F32)
                    sqb = [ph.sbuf("sq%d" % i, [128, 512], BF16) for i in range(2)]
                    tmpb = [ph.sbuf("tm%d" % i, [128, 512], F32) for i in range(2)]
                    rstd = ph.sbuf("rstd", [128, 512], F32)
                    for si, (start, n, isc) in enumerate(SEGS):
                        def consume(c, start=start, n=n, isc=isc, si=si):
                            tb = tmpb[c % 2]
                            P.op("dve", lambda: nc.vector.scalar_tensor_tensor(
                                tb[:, 0:n], xt[:, c, 0:n], modA[:, l, 1, c, isc:isc + 1], rstd[:, 0:n], ALU.mult, ALU.mult),
                                reads=[xt, rstd, (modA, (l, KN))], writes=[tb])
                            P.op("act", lambda: nc.scalar.activation(hT[:, c, start:start + n], tb[:, 0:n], AF.Identity,
                                                                     bias=shiftp(l, 1, c, isc), scale=1.0),
                                 reads=[tb, (modr, (l, KN))], writes=[(hT, si)])
                        norm_tile(ph, xt, sqb, rstd, start, n, consume, None)
                if stop_after == "h":
                    return
                with Phase(P) as ph:
                    zb = [ph.sbuf("zb%d" % i, [128, 512], F32) for i in range(3)]
                    cnt = 0
                    for zp in range(24):
                        ws, wv = ring.load(winv[:, :, OFF_Z + zp * 256:OFF_Z + (zp + 1) * 256], NCH, 256)
                        for j in range(2):
                            zc = zp * 2 + j
                            for si, seg in enumerate(SEGS):
                                start, n, isc = seg
                                if isc and not need_ctx:
                                    continue
                                p_ = proj(ws, wv, j * 128, hT, seg, si)
                                b = zb[cnt % 3]
                                cnt += 1
                                P.op("act", lambda: nc.scalar.activation(b[:, 0:n], p_[:, 0:n], AF.Sigmoid), reads=[p_], writes=[b])
                                P.dma("sp", zsig.t.ap()[zc, :, start:start + n], b[:, 0:n], reads=[b], writes=[(zsig, (zc, si))])
                with Phase(P) as ph:
                    xp = ph.sbuf("xp", [128, PL], F32)
                    u = ph.sbuf("u", [128, PL], F32)
                    ub = ph.sbuf("ub", [128, PL], BF16)
                    hf = ph.sbuf("hf", [128, PL], F32)
                    hb = ph.sbuf("hb", [128, PL], F32)
                    rb = [ph.sbuf("rb%d" % i, [128, 512], F32) for i in range(2)]
                    ib = [ph.sbuf("ib%d" % i, [128, 512], F32) for i in range(2)]
                    sb = [ph.sbuf("sb%d" % i, [128, 512], F32) for i in range(2)]
                    gy = [ph.sbuf("gy%d" % i, [128, 512], F32) for i in range(2)]
                    mo = [ph.sbuf("mo%d" % i, [128, 512], BF16) for i in range(2)]
                    P.op("dve", lambda: nc.vector.memset(xp[:], 0.0), writes=[xp])
                    rgv = rgwd.t.ap()[l]
                    for c in range(8):
                        wxs, wxv = ring.load(winv[:, :, OFF_X + c * 128:OFF_X + (c + 1) * 128], NCH, 128)
                        wys, wyv = ring.load(winv[:, :, OFF_Y + c * 128:OFF_Y + (c + 1) * 128], NCH, 128)
                        rgs, rgt = ring.load(rgv[:, c], 4, 128)
                        for si, seg in enumerate(SEGS):
                            start, n, isc = seg
                            p_ = proj(wxs, wxv, 0, hT, seg, si)
                            pp = ppos(start)
                            P.op("act", lambda: nc.scalar.copy(xp[:, pp:pp + n], p_[:, 0:n]), reads=[p_], writes=[(xp, si)])
                        lo, hi = PAD, PL - PAD
                        P.op("dve", lambda: nc.vector.tensor_scalar(u[:, lo:hi], xp[:, lo - 1:hi - 1], pr(l, "rcw", c * 4, 1),
                                                                    pr(l, "rcb", c, 1), ALU.mult, ALU.add),
                             reads=[xp, par], writes=[u])
                        for j in range(1, 4):
                            P.op("dve", lambda: nc.vector.scalar_tensor_tensor(
                                u[:, lo:hi], xp[:, lo + j - 1:hi + j - 1], pr(l, "rcw", c * 4 + j, 1), u[:, lo:hi],
                                ALU.mult, ALU.add), reads=[xp, par, u], writes=[u])
                        P.op("act", lambda: nc.scalar.copy(ub[:, lo:hi], u[:, lo:hi]), reads=[u], writes=[ub])
                        cnt = 0
                        for d in range(2):
                            hbuf = hf if d == 0 else hb
                            order = list(range(5)) if d == 0 else [0, 4, 3, 2, 1]
                            prev = None
                            for si in order:
                                start, n, isc = SEGS[si]
                                pp = ppos(start)
                                b = cnt % 2
                                cnt += 1
                                pr_ = P.ps()
                                pi_ = P.ps()
                                P.op("pe", lambda: nc.tensor.matmul(pr_[:, 0:n], rgt[:, d * 2 + 0, :], ub[:, pp:pp + n],
                                                                    start=True, stop=True), reads=[rgs, ub], writes=[pr_])
                                P.op("pe", lambda: nc.tensor.matmul(pi_[:, 0:n], rgt[:, d * 2 + 1, :], ub[:, pp:pp + n],
                                                                    start=True, stop=True), reads=[rgs, ub], writes=[pi_])
                                P.op("act", lambda: nc.scalar.activation(rb[b][:, 0:n], pr_[:, 0:n], AF.Sigmoid,
                                                                         bias=pr(l, "rgbr", d * 8 + c, 1), scale=1.0),
                                     reads=[pr_, par], writes=[rb[b]])
                                P.op("act", lambda: nc.scalar.activation(ib[b][:, 0:n], pi_[:, 0:n], AF.Sigmoid,
                                                                         bias=pr(l, "rgbi", d * 8 + c, 1), scale=1.0),
                                     reads=[pi_, par], writes=[ib[b]])
                                P.op("act", lambda: nc.scalar.activation(sb[b][:, 0:n], rb[b][:, 0:n], AF.Exp,
                                                                         scale=cneg[:, l, 1, d * 8 + c:d * 8 + c + 1]),
                                     reads=[rb[b], cneg], writes=[sb[b]])
                                P.op("act", lambda: nc.scalar.activation(rb[b][:, 0:n], rb[b][:, 0:n], AF.Exp,
                                                                         scale=cneg[:, l, 0, d * 8 + c:d * 8 + c + 1]),
                                     reads=[rb[b], cneg], writes=[rb[b]])
                                P.op("act", lambda: nc.scalar.activation(sb[b][:, 0:n], sb[b][:, 0:n], AF.Sqrt, bias=1.0, scale=-1.0),
                                     reads=[sb[b]], writes=[sb[b]])
                                P.op("dve", lambda: nc.vector.tensor_tensor(ib[b][:, 0:n], ib[b][:, 0:n], sb[b][:, 0:n], ALU.mult),
                                     reads=[ib[b], sb[b]], writes=[ib[b]])
                                P.op("dve", lambda: nc.vector.tensor_tensor(ib[b][:, 0:n], ib[b][:, 0:n], u[:, pp:pp + n], ALU.mult),
                                     reads=[ib[b], u], writes=[ib[b]])
                                if prev is None:
                                    init = 0.0
                                elif d == 0:
                                    init = hbuf[:, prev + 0:prev + 1]
                                else:
                                    init = hbuf[:, prev:prev + 1]
                                if d == 0:
                                    P.op("dve", lambda: nc.vector.tensor_tensor_scan(hbuf[:, pp:pp + n], rb[b][:, 0:n], ib[b][:, 0:n],
                                                                                     init, ALU.mult, ALU.add),
                                         reads=[rb[b], ib[b], hbuf], writes=[hbuf])
                                    prev = pp + n - 1
                                else:
                                    P.op("dve", lambda: nc.vector.tensor_tensor_scan(
                                        hbuf[:, pp:pp + n][:, ::-1], rb[b][:, 0:n][:, ::-1], ib[b][:, 0:n][:, ::-1],
                                        init, ALU.mult, ALU.add), reads=[rb[b], ib[b], hbuf], writes=[hbuf])
                                    prev = pp
                        for si, seg in enumerate(SEGS):
                            start, n, isc = seg
                            if isc and not need_ctx:
                                continue
                            pp = ppos(start)
                            p_ = proj(wys, wyv, 0, hT, seg, si)
                            b = si % 2
                            P.op("act", lambda: nc.scalar.activation(gy[b][:, 0:n], p_[:, 0:n], AF.Gelu_apprx_tanh),
                                 reads=[p_], writes=[gy[b]])
                            P.op("dve", lambda: nc.vector.tensor_tensor(hf[:, pp:pp + n], hf[:, pp:pp + n], hb[:, pp:pp + n], ALU.add),
                                 reads=[hf, hb], writes=[hf])
                            P.op("dve", lambda: nc.vector.tensor_tensor(mo[b][:, 0:n], gy[b][:, 0:n], hf[:, pp:pp + n], ALU.mult),
                                 reads=[gy[b], hf], writes=[mo[b]])
                            P.dma("sp", mv[0, c, :, start:start + n], mo[b][:, 0:n], reads=[mo[b]], writes=[(mscr, (0, c, si))])
                if stop_after == "rnn":
                    return
                with Phase(P) as ph:
                    gpb = [ph.sbuf("gpb%d" % i, [128, PL], BF16) for i in range(2)]
                    sg = [ph.sbuf("sg%d" % i, [128, 512], F32) for i in range(2)]
                    co = [ph.sbuf("co%d" % i, [128, 512], F32) for i in range(2)]
                    for g_ in gpb:
                        P.op("dve", lambda: nc.vector.memset(g_[:], 0.0), writes=[g_])
                    cnt = 0
                    for c in range(8):
                        gp = gpb[c % 2]
                        was, wav = ring.load(winv[:, :, OFF_G + c * 128:OFF_G + (c + 1) * 128], NCH, 128)
                        wgs, wgv_ = ring.load(winv[:, :, OFF_G + 1024 + c * 128:OFF_G + 1024 + (c + 1) * 128], NCH, 128)
                        dgs, dgv = ring.load(cvdd.t.ap()[l][:, c], 31, 128)
                        for si, seg in enumerate(SEGS):
                            start, n, isc = seg
                            pp = ppos(start)
                            pa = proj(was, wav, 0, hT, seg, si)
                            pg = proj(wgs, wgv_, 0, hT, seg, si)
                            b = si % 2
                            P.op("act", lambda: nc.scalar.activation(sg[b][:, 0:n], pg[:, 0:n], AF.Sigmoid), reads=[pg], writes=[sg[b]])
                            P.op("dve", lambda: nc.vector.tensor_tensor(gp[:, pp:pp + n], sg[b][:, 0:n], pa[:, 0:n], ALU.mult),
                                 reads=[sg[b], pa], writes=[(gp, si)])
                        for si, seg in enumerate(SEGS):
                            start, n, isc = seg
                            if isc and not need_ctx:
                                continue
                            pp = ppos(start)
                            pc = P.ps()
                            P.group("pe", [(lambda j=j: nc.tensor.matmul(pc[:, 0:n], dgv[:, j, :], gp[:, pp + j - 15:pp + j - 15 + n],
                                                                         start=(j == 0), stop=(j == 30))) for j in range(31)],
                                    reads=[dgs, gp], writes=[pc])
                            b = cnt % 2
                            cnt += 1
                            P.op("act", lambda: nc.scalar.activation(co[b][:, 0:n], pc[:, 0:n], AF.Identity,
                                                                     bias=pr(l, "cvb", c, 1), scale=1.0),
                                 reads=[pc, par], writes=[co[b]])
                            P.dma("sp", cvo.t.ap()[c, :, start:start + n], co[b][:, 0:n], reads=[co[b]], writes=[(cvo, (c, si))])
                with Phase(P) as ph:
                    ct = ph.sbuf("ct", [128, 8, 512], F32)
                    sqb = [ph.sbuf("sq%d" % i, [128, 512], F32) for i in range(2)]
                    mean = ph.sbuf("mean", [128, 512], F32)
                    rstd = ph.sbuf("rstd", [128, 512], F32)
                    tb = [ph.sbuf("tb%d" % i, [128, 512], F32) for i in range(2)]
                    mo = [ph.sbuf("mo%d" % i, [128, 512], BF16) for i in range(2)]
                    cvv = cvo.t.ap().rearrange("c p t -> p c t")
                    for si, seg in enumerate(SEGS):
                        start, n, isc = seg
                        if isc and not need_ctx:
                            continue
                        P.dma("sp", ct[:, :, 0:n], cvv[:, :, start:start + n], reads=[cvo], writes=[ct])
                        pm = P.ps()
                        pq = P.ps()
                        for c in range(8):
                            q = sqb[c % 2]
                            P.op("act", lambda: nc.scalar.activation(q[:, 0:n], ct[:, c, 0:n], AF.Square), reads=[ct], writes=[q])
                            P.op("pe", lambda: nc.tensor.matmul(pm[:, 0:n], ones128, ct[:, c, 0:n], start=(c == 0), stop=(c == 7)),
                                 reads=[ct, cst], writes=[pm])
                            P.op("pe", lambda: nc.tensor.matmul(pq[:, 0:n], ones128, q[:, 0:n], start=(c == 0), stop=(c == 7)),
                                 reads=[q, cst], writes=[pq])
                        P.op("act", lambda: nc.scalar.mul(mean[:, 0:n], pm[:, 0:n], 0.125), reads=[pm], writes=[mean])
                        P.op("dve", lambda: nc.vector.tensor_tensor(rstd[:, 0:n], mean[:, 0:n], mean[:, 0:n], ALU.mult),
                             reads=[mean], writes=[rstd])
                        P.op("dve", lambda: nc.vector.scalar_tensor_tensor(rstd[:, 0:n], pq[:, 0:n], 0.125, rstd[:, 0:n],
                                                                           ALU.mult, ALU.subtract), reads=[pq, rstd], writes=[rstd])
                        rsqrt_eps(rstd, rstd, n)
                        for c in range(8):
                            b = c % 2
                            P.op("dve", lambda: nc.vector.tensor_tensor(tb[b][:, 0:n], ct[:, c, 0:n], mean[:, 0:n], ALU.subtract),
                                 reads=[ct, mean], writes=[tb[b]])
                            P.op("dve", lambda: nc.vector.tensor_tensor(tb[b][:, 0:n], tb[b][:, 0:n], rstd[:, 0:n], ALU.mult),
                                 reads=[tb[b], rstd], writes=[tb[b]])
                            P.op("act", lambda: nc.scalar.activation(mo[b][:, 0:n], tb[b][:, 0:n], AF.Silu,
                                                                     bias=pr(l, "cvbb", c, 1), scale=pr(l, "cvg", c, 1)),
                                 reads=[tb[b], par], writes=[mo[b]])
                            P.dma("sp", mv[1, c, :, start:start + n], mo[b][:, 0:n], reads=[mo[b]], writes=[(mscr, (1, c, si))])
                if stop_after == "conv":
                    return
                with Phase(P) as ph:
                    QTs = [ph.sbuf("QT%d" % i, [128, NT], BF16) for i in range(2)]
                    KTs = [ph.sbuf("KT%d" % i, [128, NT], BF16) for i in range(2)]
                    Vp = ph.sbuf("Vp", [128, 18, 256], BF16)
                    cs = [ph.sbuf("cs0", [128, 2, 512], F32)] * 2
                    qf = [ph.sbuf("qf%d" % i, [128, 512], F32) for i in range(2)]
                    t1 = [ph.sbuf("t1%d" % i, [128, 512], F32) for i in range(2)]
                    t2 = [ph.sbuf("t2%d" % i, [128, 512], F32) for i in range(2)]
                    Pt = [ph.sbuf("Pt%d" % i, [128, 512], BF16) for i in range(4)]
                    rz = [ph.sbuf("rz%d" % i, [128, 512], F32) for i in range(2)]
                    ob = [ph.sbuf("ob%d" % i, [128, 512], F32) for i in range(2)]
                    osq = ph.sbuf("osq", [128, 512], BF16)
                    orr = ph.sbuf("orr", [128, 512], F32)
                    mo = [ph.sbuf("mo%d" % i, [128, 512], BF16) for i in range(2)]
                    neglam = lamv[:, l, 0:1]
                    gsub = lamv[:, l, 1:2]
                    cnt = 0
                    def head_stage(h, stage):
                        nonlocal cnt
                        QT = QTs[h % 2]
                        KT = KTs[h % 2]
                        if stage == "v":
                            if h % 2 == 0:
                                wvs, wvv = ring.load(winv[:, :, OFF_V + h * 128:OFF_V + (h + 2) * 128], NCH, 256)
                                for tc in range(18):
                                    si = 0 if tc < 2 else 1 + (tc - 2) // 4
                                    p_ = P.ps()
                                    P.group("pe", [(lambda kc=kc: nc.tensor.matmul(p_[:, 0:256], hT[:, kc, tc * 128:(tc + 1) * 128],
                                                                                   wvv[:, kc, :], start=(kc == 0), stop=(kc == NCH - 1)))
                                                   for kc in range(NCH)], reads=[wvs, (hT, si)], writes=[p_])
                                    if tc % 2 == 0:
                                        P.op("act", lambda: nc.scalar.copy(Vp[:, tc, :], p_[:, 0:256]), reads=[p_], writes=[(Vp, tc)])
                                    else:
                                        P.op("dve", lambda: nc.vector.tensor_copy(Vp[:, tc, :], p_[:, 0:256]), reads=[p_], writes=[(Vp, tc)])
                        if stage == "qk":
                            wqs, wqv = ring.load(winv[:, :, OFF_Q + h * 128:OFF_Q + (h + 1) * 128], NCH, 128)
                            wks, wkv = ring.load(winv[:, :, OFF_K + h * 128:OFF_K + (h + 1) * 128], NCH, 128)
                            for si, seg in enumerate(SEGS):
                                start, n, isc = seg
                                if isc:
                                    if need_ctx:
                                        p_ = proj(wqs, wqv, 0, hT, seg, si)
                                        P.op("act", lambda: nc.scalar.copy(QT[:, start:start + n], p_[:, 0:n]), reads=[p_], writes=[(QT, si)])
                                    p_ = proj(wks, wkv, 0, hT, seg, si)
                                    P.op("act", lambda: nc.scalar.copy(KT[:, start:start + n], p_[:, 0:n]), reads=[p_], writes=[(KT, si)])
                                    continue
                                cb = cs[si % 2]
                                P.dma("sp", cb[:, 0, :], cosd.t.ap()[:, start - CTX:start - CTX + n], writes=[cb])
                                P.dma("sp", cb[:, 1, :], sind.t.ap()[:, start - CTX:start - CTX + n], writes=[cb])
                                for (ws_, wv_, dst) in ((wqs, wqv, QT), (wks, wkv, KT)):
                                    p_ = proj(ws_, wv_, 0, hT, seg, si)
                                    b = cnt % 2
                                    cnt += 1
                                    P.op("act", lambda: nc.scalar.copy(qf[b][:, 0:n], p_[:, 0:n]), reads=[p_], writes=[qf[b]])
                                    p2 = P.ps()
                                    P.op("pe", lambda: nc.tensor.matmul(p2[:, 0:n], rperm, qf[b][:, 0:n], start=True, stop=True),
                                         reads=[qf[b], cst], writes=[p2])
                                    P.op("dve", lambda: nc.vector.tensor_tensor(t1[b][:, 0:n], qf[b][:, 0:n], cb[:, 0, 0:n], ALU.mult),
                                         reads=[qf[b], cb], writes=[t1[b]])
                                    P.op("dve", lambda: nc.vector.tensor_tensor(t2[b][:, 0:n], p2[:, 0:n], cb[:, 1, 0:n], ALU.mult),
                                         reads=[p2, cb], writes=[t2[b]])
                                    P.op("dve", lambda: nc.vector.tensor_tensor(dst[:, start:start + n], t1[b][:, 0:n], t2[b][:, 0:n], ALU.add),
                                         reads=[t1[b], t2[b]], writes=[(dst, si)])
                        if stage == "core":
                            hoff = (h % 2) * 128
                            for si, seg in enumerate(SEGS):
                                qs, qn, isc = seg
                                if isc and not need_ctx:
                                    continue
                                keys = [0, 1] if isc else list(range(18))
                                nk = len(keys)
                                O = [P.psb[4], P.psb[5]]
                                Z = [P.psb[6], P.psb[7]]

                                def scores(kc):
                                    ksi = 0 if kc < 2 else 1 + (kc - 2) // 4
                                    out = []
                                    for comp in range(2):
                                        sc = P.ps()
                                        P.op("pe", lambda: nc.tensor.matmul(
                                            sc[:, 0:qn], KT[comp * 64:(comp + 1) * 64, kc * 128:(kc + 1) * 128],
                                            QT[comp * 64:(comp + 1) * 64, qs:qs + qn], start=True, stop=True),
                                            reads=[(KT, ksi), (QT, si)], writes=[sc])
                                        out.append(sc)
                                    return out
                                s_cur = scores(keys[0])
                                for i, kc in enumerate(keys):
                                    s_next = scores(keys[i + 1]) if i + 1 < nk else None
                                    for comp in range(2):
                                        pt = Pt[(i % 2) * 2 + comp]
                                        P.op("act", lambda: nc.scalar.activation(pt[:, 0:qn], s_cur[comp][:, 0:qn], AF.Exp, scale=0.125),
                                             reads=[s_cur[comp]], writes=[pt])
                                    for comp in range(2):
                                        pt = Pt[(i % 2) * 2 + comp]
                                        P.op("pe", lambda: nc.tensor.matmul(O[comp][:, 0:qn], Vp[:, kc, hoff:hoff + 128], pt[:, 0:qn],
                                                                            start=(i == 0), stop=(i == nk - 1)),
                                             reads=[(Vp, kc), pt], writes=[O[comp]])
                                        P.op("pe", lambda: nc.tensor.matmul(Z[comp][:, 0:qn], onesb[:], pt[:, 0:qn],
                                                                            start=(i == 0), stop=(i == nk - 1)),
                                             reads=[onesb, pt], writes=[Z[comp]])
                                    s_cur = s_next
                                for comp in range(2):
                                    P.op("act", lambda: nc.scalar.activation(rz[comp][:, 0:qn], Z[comp][:, 0:qn], AF.Ln), reads=[Z[comp]], writes=[rz[comp]])
                                    P.op("act", lambda: nc.scalar.activation(rz[comp][:, 0:qn], rz[comp][:, 0:qn], AF.Exp, scale=-1.0), reads=[rz[comp]], writes=[rz[comp]])
                                    P.op("dve", lambda: nc.vector.tensor_tensor(ob[comp][:, 0:qn], O[comp][:, 0:qn], rz[comp][:, 0:qn], ALU.mult),
                                         reads=[O[comp], rz[comp]], writes=[ob[comp]])
                                P.op("dve", lambda: nc.vector.scalar_tensor_tensor(ob[0][:, 0:qn], ob[1][:, 0:qn], neglam, ob[0][:, 0:qn],
                                                                                   ALU.mult, ALU.add), reads=[ob[0], ob[1], lamv], writes=[ob[0]])
                                P.op("act", lambda: nc.scalar.activation(osq[:, 0:qn], ob[0][:, 0:qn], AF.Square), reads=[ob[0]], writes=[osq])
                                pm = P.ps()
                                P.op("pe", lambda: nc.tensor.matmul(pm[:, 0:qn], ones128b[:], osq[:, 0:qn], start=True, stop=True),
                                     reads=[osq, ones128b], writes=[pm])
                                rsqrt_eps(orr, pm, qn)
                                P.op("dve", lambda: nc.vector.tensor_tensor(orr[:, 0:qn], orr[:, 0:qn], ob[0][:, 0:qn], ALU.mult),
                                     reads=[orr, ob[0]], writes=[orr])
                                b = si % 2
                                P.op("act", lambda: nc.scalar.activation(mo[b][:, 0:qn], orr[:, 0:qn], AF.Identity, bias=0.0, scale=gsub),
                                     reads=[orr, lamv], writes=[mo[b]])
                                P.dma("sp", mv[2, h, :, qs:qs + qn], mo[b][:, 0:qn], reads=[mo[b]], writes=[(mscr, (2, h, si))])
                    head_stage(0, "qk")
                    for h in range(8):
                        head_stage(h, "v")
                        if h + 1 < 8:
                            head_stage(h + 1, "qk")
                        head_stage(h, "core")
            if stop_after == "att":
                return
            bwv = [t.t.ap()[l].rearrange("(kc p) n -> p kc n", p=128) for t in (rnn_wo, cv_wo, da_wo)]
            wov = w_out.t.ap()[l].rearrange("(kc p) n -> p kc n", p=128)
            mvv = mscr.t.ap().rearrange("b c p t -> p b c t")
            with Phase(P) as ph:
                mt = ph.sbuf("mt", [128, 3, 8, 1024], BF16)
                mg = ph.sbuf("mg", [128, NCH, 1024], BF16)
                zt = [ph.sbuf("zt%d" % i, [128, 3, 512], F32) for i in range(2)]
                accb = [ph.sbuf("ac%d" % i, [128, 512], F32) for i in range(2)]
                tmb = [ph.sbuf("tmg%d" % i, [128, 512], F32) for i in range(2)]
                xr = [ph.sbuf("xr%d" % i, [128, 512], F32) for i in range(2)]
                xo = [ph.sbuf("xo%d" % i, [128, 512], F32) for i in range(2)]
                cnt = 0
                groups = [[0, 1], [2, 3], [4]] if need_ctx else [[1, 2], [3, 4]]
                for gi, grp in enumerate(groups):
                    goff = {}
                    o = 0
                    for si in grp:
                        goff[si] = o
                        o += SEGS[si][1]
                    for si in grp:
                        start, n, isc = SEGS[si]
                        off = goff[si]
                        for br in range(3):
                            P.dma("sp", mt[:, br, :, off:off + n], mvv[:, br, :, start:start + n], reads=[mscr],
                                  writes=[(mt, (br, si))])
                    for dp in range(8):
                        wts = []
                        for br in range(3):
                            wts.append(ring.load(bwv[br][:, :, dp * 256:(dp + 1) * 256], 8, 256))
                        for j in range(2):
                            d = dp * 2 + j
                            for si in grp:
                                start, n, isc = SEGS[si]
                                off = goff[si]
                                z_ = zt[cnt % 2]
                                a_ = accb[cnt % 2]
                                t_ = tmb[cnt % 2]
                                cnt += 1
                                for br in range(3):
                                    P.dma("sp", z_[:, br, 0:n], zsig.t.ap()[br * 16 + d, :, start:start + n], reads=[zsig],
                                          writes=[(z_, br)])
                                for br in range(3):
                                    ws, wv = wts[br]
                                    p_ = P.ps()
                                    P.group("pe", [(lambda kc=kc: nc.tensor.matmul(p_[:, 0:n], wv[:, kc, j * 128:(j + 1) * 128],
                                                                                   mt[:, br, kc, off:off + n], start=(kc == 0), stop=(kc == 7)))
                                                   for kc in range(8)], reads=[ws, (mt, (br, si))], writes=[p_])
                                    if br == 0:
                                        P.op("dve", lambda: nc.vector.tensor_tensor(a_[:, 0:n], p_[:, 0:n], z_[:, br, 0:n], ALU.mult),
                                             reads=[p_, (z_, br)], writes=[a_])
                                    else:
                                        P.op("dve", lambda: nc.vector.tensor_tensor(t_[:, 0:n], p_[:, 0:n], z_[:, br, 0:n], ALU.mult),
                                             reads=[p_, (z_, br)], writes=[t_])
                                        if br == 1:
                                            P.op("dve", lambda: nc.vector.tensor_tensor(a_[:, 0:n], a_[:, 0:n], t_[:, 0:n], ALU.add),
                                                 reads=[a_, t_], writes=[a_])
                                        else:
                                            P.op("dve", lambda: nc.vector.tensor_tensor(mg[:, d, off:off + n], a_[:, 0:n], t_[:, 0:n], ALU.add),
                                                 reads=[a_, t_], writes=[(mg, (d, si))])
                    for dp in range(8):
                        ws, wv = ring.load(wov[:, :, dp * 256:(dp + 1) * 256], NCH, 256)
                        for j in range(2):
                            d = dp * 2 + j
                            for si in grp:
                                start, n, isc = SEGS[si]
                                off = goff[si]
                                p_ = P.ps()
                                P.group("pe", [(lambda kc=kc: nc.tensor.matmul(p_[:, 0:n], wv[:, kc, j * 128:(j + 1) * 128],
                                                                               mg[:, kc, off:off + n], start=(kc == 0), stop=(kc == NCH - 1)))
                                               for kc in range(NCH)], reads=[ws] + [(mg, (kc, si)) for kc in range(NCH)], writes=[p_])
                                b = cnt % 2
                                cnt += 1
                                P.dma("sp", xr[b][:, 0:n], xs.t.ap()[d, :, start:start + n], reads=[(xs, (d, start))], writes=[xr[b]])
                                P.op("dve", lambda: nc.vector.scalar_tensor_tensor(
                                    xo[b][:, 0:n], p_[:, 0:n], modG[:, l, 1, d, isc:isc + 1], xr[b][:, 0:n], ALU.mult, ALU.add),
                                    reads=[p_, xr[b], (modG, (l, 1))], writes=[xo[b]])
                                P.dma("sp", xs.t.ap()[d, :, start:start + n], xo[b][:, 0:n], reads=[xo[b]], writes=[(xs, (d, start))])

        def final_norm():
            with Phase(P) as ph:
                xt = ph.sbuf("xt", [128, NCH, 512], F32)
                sqb = [ph.sbuf("sq%d" % i, [128, 512], BF16) for i in range(2)]
                rstd = ph.sbuf("rstd", [128, 512], F32)
                yo = [ph.sbuf("yo%d" % i, [128, 512], F32) for i in range(2)]
                for si, (start, n, isc) in enumerate(SEGS):
                    if isc:
                        continue

                    def consume(c, start=start, n=n):
                        y_ = yo[c % 2]
                        P.op("dve", lambda: nc.vector.scalar_tensor_tensor(
                            y_[:, 0:n], xt[:, c, 0:n], par[:, 0, _po["fing"] + c:_po["fing"] + c + 1], rstd[:, 0:n],
                            ALU.mult, ALU.mult), reads=[xt, rstd, par], writes=[y_])
                        P.dma("sp", yT.t.ap()[c, :, start - CTX:start - CTX + n], y_[:, 0:n], reads=[y_], writes=[(yT, (c, si))])
                    norm_tile(ph, xt, sqb, rstd, start, n, consume, None)

        def forward():
            for l in range(DEPTH):
                need_ctx = l < DEPTH - 1
                if l == 0:
                    pending.extend([ada_part(0, 1), ada_part(0, 2)])
                ffn(l, 0, True)
                ada_drain()
                dbgx("x_ffn1_%d" % l)
                if stop_after == "ffn1_%d" % l:
                    return
                mixer(l, need_ctx)
                dbgx("x_mix_%d" % l)
                if stop_after is not None and stop_after in ("h", "rnn", "conv", "att", "mix_%d" % l):
                    return
                if l == 0:
                    pending.extend([ada_part(1, 0), ada_part(1, 1), ada_part(1, 2)])
                ffn(l, 1, need_ctx)
                ada_drain()
                dbgx("x_ffn2_%d" % l)
                if stop_after == "ffn2_%d" % l:
                    return
            final_norm()

        forward()
        P.barrier(final=True)
    return nc, dbg


def _fm(v):
    v = np.asarray(v)
    return np.ascontiguousarray(v.reshape(-1, 128).T)


def _host_consts():
    inv = (np.float32(10000.0) ** (-np.arange(16, dtype=np.float32) * np.float32(2.0) / np.float32(32))).astype(np.float32)
    t = np.arange(SEQ)
    row = (t // 64).astype(np.float32)
    col = (t % 64).astype(np.float32)
    cosT = np.zeros((128, SEQ), np.float32)
    sinT = np.zeros((128, SEQ), np.float32)
    rperm = np.zeros((128, 128), np.float32)
    for p in range(128):
        d = p % 64
        a = d // 32
        half = (d % 32) // 16
        n = d % 16
        ang = ((row if a == 0 else col) * inv[n]).astype(np.float32)
        cosT[p] = np.cos(ang).astype(np.float32)
        sinT[p] = np.sin(ang).astype(np.float32)
        if half == 0:
            rperm[p + 16, p] = -1.0
        else:
            rperm[p - 16, p] = 1.0
    cst = np.zeros((128, 3, 128), np.float32)
    cst[:, 0, :] = 1.0 / 2048.0
    cst[:, 1, :] = 1.0 / 128.0
    cst[:, 2, :] = rperm
    return cosT, sinT, cst


def _prep_inputs(inp):
    f32 = np.float32
    cosT, sinT, cst = _host_consts()
    par = np.zeros((DEPTH, 128, NPAR), f32)
    rgw = np.zeros((DEPTH, 128, 8, 4, 128), f32)
    for l in range(DEPTH):
        def put(name, arr):
            arr = np.asarray(arr, f32)
            par[l, :, _po[name]:_po[name] + arr.shape[1]] = arr
        put("adab", _fm(inp["ada_b"][l]))
        put("ng", _fm(inp["norm_g"][l].reshape(-1)))
        put("rcw", inp["rnn_conv_w"][l].T.reshape(8, 128, 4).transpose(1, 0, 2).reshape(128, 32))
        put("rcb", _fm(inp["rnn_conv_b"][l]))
        put("rgbr", _fm(inp["rg_b_r"][l].reshape(-1)))
        put("rgbi", _fm(inp["rg_b_i"][l].reshape(-1)))
        put("rglam", _fm(inp["rg_lam"][l].reshape(-1)))
        put("cvw", inp["cv_dw_w"][l].T.reshape(8, 128, 31).transpose(1, 0, 2).reshape(128, 248))
        put("cvb", _fm(inp["cv_dw_b"][l]))
        put("cvg", _fm(inp["cv_ln_g"][l]))
        put("cvbb", _fm(inp["cv_ln_b"][l]))
        put("dalam", np.broadcast_to(inp["da_lam"][l].reshape(1, 256), (128, 256)))
        put("subg", inp["da_subln_g"][l].reshape(128, 1))
        put("fing", _fm(inp["final_g"]))
        for d in range(2):
            for g, nm in enumerate(("rg_w_r", "rg_w_i")):
                w = inp[nm][l, d]
                for c in range(8):
                    rgw[l, 0:64, c, d * 2 + g, 0:64] = w[2 * c]
                    rgw[l, 64:128, c, d * 2 + g, 64:128] = w[2 * c + 1]
    cvd = np.zeros((DEPTH, 128, 8, 31, 128), f32)
    ar = np.arange(128)
    for l in range(DEPTH):
        w = np.asarray(inp["cv_dw_w"][l], f32).T.reshape(8, 128, 31)
        for c in range(8):
            cvd[l, ar, c, :, ar] = w[c]
    shared = {
        "par": par, "rgw": rgw, "cvd": cvd, "cosT": cosT, "sinT": sinT, "cst": cst,
        "ada_w": np.ascontiguousarray(inp["ada_w"], dtype=f32),
        "ffn_w_gate": np.ascontiguousarray(inp["ffn_w_gate"], dtype=f32),
        "ffn_w_up": np.ascontiguousarray(inp["ffn_w_up"], dtype=f32),
        "ffn_w_down": np.ascontiguousarray(inp["ffn_w_down"], dtype=f32),
        "w_in": np.ascontiguousarray(inp["w_in"], dtype=f32),
        "rnn_w_out": np.ascontiguousarray(inp["rnn_w_out"], dtype=f32),
        "cv_w_out": np.ascontiguousarray(inp["cv_w_out"], dtype=f32),
        "da_w_o": np.ascontiguousarray(inp["da_w_o"], dtype=f32),
        "w_out": np.ascontiguousarray(inp["w_out"], dtype=f32),
    }
    maps = []
    B = inp["x"].shape[0]
    for b in range(B):
        xt = np.concatenate([inp["ctx"][b], inp["x"][b]], axis=0).astype(f32)
        xT = np.ascontiguousarray(xt.T).reshape(NCH, 128, NT)
        cc = np.stack([_fm(inp["c"][b]), _fm(inp["c_ctx"])], axis=-1).astype(f32)
        m = dict(shared)
        m["xT"] = xT
        m["cc"] = np.ascontiguousarray(cc)
        maps.append(m)
    return maps


_CACHE = {}


def kernel(**inputs):
    inp = {k: np.asarray(v) for k, v in inputs.items()}
    maps = _prep_inputs(inp)
    if "nc" not in _CACHE:
        _CACHE["nc"] = build_program()[0]
    nc = _CACHE["nc"]
    res = run_bass_kernel_spmd(nc, maps, core_ids=list(range(len(maps))))
    outs = []
    for r in res.results:
        yT = np.asarray(r["yT"]).reshape(D, SEQ)
        outs.append(np.ascontiguousarray(yT.T))
    return np.stack(outs, axis=0).astype(np.float32)
```

```python
from contextlib import ExitStack
import math
import numpy as np
import concourse.bass as bass
import concourse.mybir as mybir
from concourse.bass_utils import run_bass_kernel_spmd

F32 = mybir.dt.float32
BF16 = mybir.dt.bfloat16
AF = mybir.ActivationFunctionType
ALU = mybir.AluOpType
AX = mybir.AxisListType

D = 2048
NCH = 16
SEQ = 2048
CTX = 256
NT = SEQ + CTX
DFF = 5632
NFF = 44
DIN = 13312
EPS = 1e-6
DEPTH = 2
OFF_X, OFF_Y, OFF_G, OFF_Q, OFF_K, OFF_V, OFF_Z = 0, 1024, 2048, 4096, 5120, 6144, 7168
SEGS = [(0, 256, 1), (256, 512, 0), (768, 512, 0), (1280, 512, 0), (1792, 512, 0)]
PAD = 16
PL = PAD + CTX + PAD + SEQ + PAD


def ppos(t):
    return t + PAD if t < CTX else t + 2 * PAD


_po = {}
_n = 0
for _name, _w in (("adab", 144), ("ng", 48), ("rcw", 32), ("rcb", 8), ("rgbr", 16), ("rgbi", 16), ("rglam", 16),
                  ("cvw", 248), ("cvb", 8), ("cvg", 8), ("cvbb", 8), ("dalam", 256), ("subg", 1), ("fing", 16)):
    _po[_name] = _n
    _n += _w
NPAR = _n

WHOLE = "__whole__"


class _St:
    __slots__ = ("w", "r")

    def __init__(self):
        self.w = None
        self.r = []


class Buf:
    def __init__(self, t, name):
        self.t = t
        self.name = name
        self.st = {}

    def __getitem__(self, idx):
        return self.t[idx]


class Tok:
    __slots__ = ("eng", "seq", "sem", "val")

    def __init__(self, eng, seq=None, sem=None, val=None):
        self.eng = eng
        self.seq = seq
        self.sem = sem
        self.val = val


class Prog:
    ENGS = ("pe", "act", "dve", "pool", "sp")
    NDMA = 8

    def __init__(self, nc, es):
        self.nc = nc
        self.es = es
        self.e = {"pe": nc.tensor, "act": nc.scalar, "dve": nc.vector, "pool": nc.gpsimd, "sp": nc.sync}
        self.sem = {k: es.enter_context(nc.semaphore("s_" + k)) for k in self.ENGS}
        self.nseq = {k: 0 for k in self.ENGS}
        self.ninc = {k: 0 for k in self.ENGS}
        self.last_ins = {k: None for k in self.ENGS}
        self.incs = {k: [] for k in self.ENGS}
        self.recent = {k: [] for k in self.ENGS}
        self.seen = {k: {} for k in self.ENGS}
        self.dsem = {q: [es.enter_context(nc.semaphore("d_%s%d" % (q, i))) for i in range(self.NDMA)]
                     for q in ("sp", "pool")}
        self.dcnt = {q: 0 for q in ("sp", "pool")}
        self.dlast = {q: [0] * self.NDMA for q in ("sp", "pool")}
        self.uid = 0
        self.psb = None
        self.psi = 0

    def sbuf(self, name, shape, dt):
        return Buf(self.es.enter_context(self.nc.sbuf_tensor("sb_" + name, list(shape), dt)), name)

    def dram(self, name, shape, dt, kind="Internal"):
        return Buf(self.nc.dram_tensor(name, list(shape), dt, kind=kind), name)

    def init_psum(self):
        self.psb = [Buf(self.es.enter_context(self.nc.psum_tensor("psb%d" % i, [128, 512], F32)), "ps%d" % i)
                    for i in range(8)]

    def ps(self):
        b = self.psb[self.psi % 4]
        self.psi += 1
        return b

    def _resolve(self, tok):
        if tok.eng == "dma":
            return tok.sem, tok.val
        eng = tok.eng
        lst = self.incs[eng]
        lo, hi = 0, len(lst)
        while lo < hi:
            mid = (lo + hi) // 2
            if lst[mid][0] >= tok.seq:
                hi = mid
            else:
                lo = mid + 1
        if lo < len(lst):
            return self.sem[eng], lst[lo][1]
        rec = self.recent[eng]
        k = 0
        while rec[k][0] < tok.seq:
            k += 1
        sq, ins = rec[k]
        del rec[:k + 1]
        self.ninc[eng] += 1
        ins.then_inc(self.sem[eng], 1)
        lst.append((sq, self.ninc[eng]))
        return self.sem[eng], self.ninc[eng]

    def _wait(self, eng, tok):
        if tok is None:
            return
        if tok.eng == "pe" and eng == "pe":
            return
        sem, val = self._resolve(tok)
        key = id(sem)
        if self.seen[eng].get(key, 0) >= val:
            return
        self.seen[eng][key] = val
        self.e[eng].wait_ge(sem, val)

    @staticmethod
    def _norm(x):
        if isinstance(x, Buf):
            return x, WHOLE
        return x

    @staticmethod
    def _states(buf, key):
        if key == WHOLE:
            if WHOLE not in buf.st:
                buf.st[WHOLE] = _St()
            return list(buf.st.values())
        if key not in buf.st:
            buf.st[key] = _St()
        out = [buf.st[key]]
        if WHOLE in buf.st:
            out.append(buf.st[WHOLE])
        return out

    def deps(self, eng, reads, writes):
        for x in reads:
            buf, key = self._norm(x)
            for st in self._states(buf, key):
                self._wait(eng, st.w)
        for x in writes:
            buf, key = self._norm(x)
            for st in self._states(buf, key):
                self._wait(eng, st.w)
                for r in st.r:
                    self._wait(eng, r)

    def record(self, tok, reads, writes):
        for x in reads:
            buf, key = self._norm(x)
            sts = self._states(buf, key) if key == WHOLE else [self._states(buf, key)[0]]
            for st in sts:
                if tok.eng != "dma":
                    st.r = [r for r in st.r if r.eng != tok.eng]
                st.r.append(tok)
        for x in writes:
            buf, key = self._norm(x)
            if key == WHOLE:
                buf.st = {WHOLE: _St()}
                buf.st[WHOLE].w = tok
            else:
                st = self._states(buf, key)[0]
                st.w = tok
                st.r = []

    def op(self, eng, fn, reads=(), writes=()):
        self.deps(eng, reads, writes)
        ins = fn()
        self.nseq[eng] += 1
        self.last_ins[eng] = ins
        rec = self.recent[eng]
        rec.append((self.nseq[eng], ins))
        if len(rec) > 96:
            del rec[:32]
        self.record(Tok(eng, seq=self.nseq[eng]), reads, writes)
        return ins

    def group(self, eng, fns, reads=(), writes=()):
        self.deps(eng, reads, writes)
        ins = None
        for fn in fns:
            ins = fn()
            self.nseq[eng] += 1
        self.last_ins[eng] = ins
        rec = self.recent[eng]
        rec.append((self.nseq[eng], ins))
        if len(rec) > 96:
            del rec[:32]
        self.record(Tok(eng, seq=self.nseq[eng]), reads, writes)

    def dma(self, q, out, in_, reads=(), writes=(), **kw):
        self.deps(q, reads, writes)
        j = self.dcnt[q]
        self.dcnt[q] += 1
        i = j % self.NDMA
        s = self.dsem[q][i]
        rnd = j // self.NDMA
        if rnd > 0:
            key = id(s)
            if self.seen[q].get(key, 0) < 16 * rnd:
                self.seen[q][key] = 16 * rnd
                self.e[q].wait_ge(s, 16 * rnd)
        ins = self.e[q].dma_start(out=out, in_=in_, **kw)
        ins.then_inc(s, 16)
        self.dlast[q][i] = 16 * (rnd + 1)
        tok = Tok("dma", sem=s, val=16 * (rnd + 1))
        self.record(tok, reads, writes)
        return tok

    def barrier(self, final=False):
        toks = [Tok(e, seq=self.nseq[e]) for e in ("pe", "act", "dve") if self.nseq[e] > 0]
        dt = [Tok("dma", sem=self.dsem["sp"][i], val=self.dlast["sp"][i]) for i in range(self.NDMA)
              if self.dlast["sp"][i] > 0]
        if final:
            dt += [Tok("dma", sem=self.dsem["pool"][i], val=self.dlast["pool"][i]) for i in range(self.NDMA)
                   if self.dlast["pool"][i] > 0]
        for e in ("pe", "act", "dve", "sp"):
            for t in toks:
                if t.eng != e:
                    self._wait(e, t)
            for t in dt:
                self._wait(e, t)


class Phase:
    def __init__(self, P):
        self.P = P
        self.es = ExitStack()

    def __enter__(self):
        self.es.__enter__()
        return self

    def sbuf(self, name, shape, dt):
        self.P.uid += 1
        return Buf(self.es.enter_context(self.P.nc.sbuf_tensor("sp_%s_%d" % (name, self.P.uid), list(shape), dt)), name)

    def __exit__(self, *a):
        if a[0] is None:
            self.P.barrier()
        return self.es.__exit__(*a)


class Ring:
    def __init__(self, P, nslots, elems, tag=""):
        self.P = P
        self.slots = [P.sbuf("wring%s%d" % (tag, i), [128, elems], BF16) for i in range(nslots)]
        self.i = 0

    def load(self, src, a, b):
        s = self.slots[self.i % len(self.slots)]
        self.i += 1
        view = s.t[:, 0:a * b].rearrange("p (a b) -> p a b", a=a)
        self.P.dma("pool", view, src, writes=[s])
        return s, view


def build_program(debug=(), stop_after=None):
    nc = bass.Bass("TRN2", target_bir_lowering=False)
    dbg = {}
    with ExitStack() as es:
        P = Prog(nc, es)
        P.init_psum()
        IN = "ExternalInput"
        xT = P.dram("xT", [NCH, 128, NT], F32, kind=IN)
        ccd = P.dram("cc", [128, NCH, 2], F32, kind=IN)
        pard = P.dram("par", [DEPTH, 128, NPAR], F32, kind=IN)
        rgwd = P.dram("rgw", [DEPTH, 128, 8, 4, 128], F32, kind=IN)
        cosd = P.dram("cosT", [128, SEQ], F32, kind=IN)
        sind = P.dram("sinT", [128, SEQ], F32, kind=IN)
        cstd = P.dram("cst", [128, 3, 128], F32, kind=IN)
        cvdd = P.dram("cvd", [DEPTH, 128, 8, 31, 128], F32, kind=IN)
        ada_w = P.dram("ada_w", [DEPTH, D, 9 * D], F32, kind=IN)
        w_gate = P.dram("ffn_w_gate", [DEPTH, 2, D, DFF], F32, kind=IN)
        w_up = P.dram("ffn_w_up", [DEPTH, 2, D, DFF], F32, kind=IN)
        w_down = P.dram("ffn_w_down", [DEPTH, 2, DFF, D], F32, kind=IN)
        w_in = P.dram("w_in", [DEPTH, D, DIN], F32, kind=IN)
        rnn_wo = P.dram("rnn_w_out", [DEPTH, 1024, D], F32, kind=IN)
        cv_wo = P.dram("cv_w_out", [DEPTH, 1024, D], F32, kind=IN)
        da_wo = P.dram("da_w_o", [DEPTH, 1024, D], F32, kind=IN)
        w_out = P.dram("w_out", [DEPTH, D, D], F32, kind=IN)
        yT = P.dram("yT", [NCH, 128, SEQ], F32, kind="ExternalOutput")
        xs = P.dram("xs", [NCH, 128, NT], F32)
        mscr = P.dram("mscr", [3, 8, 128, NT], BF16, kind="ExternalOutput" if "m" in debug else "Internal")
        zsig = P.dram("zsig", [48, 128, NT], F32)
        cvo = P.dram("cvo", [8, 128, NT], F32)

        def dbgx(name):
            if name in debug:
                t = P.dram("dbg_" + name, [NCH, 128, NT], F32, kind="ExternalOutput")
                P.dma("sp", t.t.ap(), xs.t.ap(), reads=[xs], writes=[t])
                dbg[name] = t

        xsv = xs.t.ap().rearrange("c p t -> p c t")

        ring = Ring(P, 5, 6144)
        par = P.sbuf("par", [128, DEPTH, NPAR], F32)
        cst = P.sbuf("cst", [128, 3, 128], F32)
        onesb = P.sbuf("onesb", [128, 128], BF16)
        modr = P.sbuf("modr", [128, DEPTH, 144, 2], F32)
        modA = P.sbuf("modA", [128, DEPTH, 3, NCH, 2], F32)
        modG = P.sbuf("modG", [128, DEPTH, 3, NCH, 2], F32)
        cneg = P.sbuf("cneg", [128, DEPTH, 2, 16], F32)
        lamv = P.sbuf("lamv", [128, DEPTH, 4], F32)
        P.dma("sp", par[:], pard.t.ap().rearrange("l p n -> p l n"), writes=[par])
        P.dma("sp", cst[:], cstd.t.ap(), writes=[cst])
        P.op("dve", lambda: nc.vector.memset(onesb[:], 1.0), writes=[onesb])
        ones128b = P.sbuf("ones128b", [128, 128], BF16)
        P.op("dve", lambda: nc.vector.memset(ones128b[:], 1.0 / 128.0), writes=[ones128b])
        ones2048b = P.sbuf("ones2048b", [128, 128], BF16)
        P.op("dve", lambda: nc.vector.memset(ones2048b[:], 1.0 / 2048.0), writes=[ones2048b])
        epsc = P.sbuf("epsc", [128, 2], F32)
        P.op("dve", lambda: nc.vector.memset(epsc[:], EPS), writes=[epsc])
        ones2048 = cst[:, 0, :]
        ones128 = cst[:, 1, :]
        rperm = cst[:, 2, :]
        ones1024 = None

        def pr(l, name, i0=0, n=1):
            o = _po[name] + i0
            return par[:, l, o:o + n]

        for c in range(NCH):
            P.dma("sp", xs.t.ap()[c], xT.t.ap()[c], writes=[(xs, c)])
        P.barrier()

        scb = P.sbuf("scb", [128, NCH, 2], BF16)
        with Phase(P) as ph:
            ccs = ph.sbuf("ccs", [128, NCH, 2], F32)
            tmp = ph.sbuf("tmpm", [128, 64], F32)
            tmp2 = ph.sbuf("tmpm2", [128, 64], F32)
            P.dma("sp", ccs[:], ccd.t.ap(), writes=[ccs])
            P.op("act", lambda: nc.scalar.activation(scb[:], ccs[:], AF.Silu), reads=[ccs], writes=[scb])
            for l in range(DEPTH):
                lam_init = 0.8 - 0.6 * math.exp(-0.3 * l)
                P.op("act", lambda: nc.scalar.activation(tmp[:, 0:16], pr(l, "rglam", 0, 16), AF.Exp, scale=-1.0),
                     reads=[par], writes=[tmp])
                P.op("act", lambda: nc.scalar.activation(tmp2[:, 0:16], tmp[:, 0:16], AF.Ln, bias=1.0, scale=1.0),
                     reads=[tmp], writes=[tmp2])
                P.op("dve", lambda: nc.vector.tensor_scalar_mul(cneg[:, l, 0, :], tmp2[:, 0:16], -8.0),
                     reads=[tmp2], writes=[(cneg, (l, 0))])
                P.op("dve", lambda: nc.vector.tensor_scalar_mul(cneg[:, l, 1, :], tmp2[:, 0:16], -16.0),
                     reads=[tmp2], writes=[(cneg, (l, 1))])
                dl = _po["dalam"]
                P.op("dve", lambda: nc.vector.tensor_tensor(tmp[:, 0:64], par[:, l, dl:dl + 64], par[:, l, dl + 64:dl + 128],
                                                            ALU.mult), reads=[par, tmp2], writes=[tmp])
                P.op("dve", lambda: nc.vector.reduce_sum(tmp2[:, 0:1], tmp[:, 0:64], AX.X), reads=[tmp], writes=[tmp2])
                P.op("dve", lambda: nc.vector.tensor_tensor(tmp[:, 0:64], par[:, l, dl + 128:dl + 192],
                                                            par[:, l, dl + 192:dl + 256], ALU.mult),
                     reads=[par, tmp2], writes=[tmp])
                P.op("dve", lambda: nc.vector.reduce_sum(tmp2[:, 1:2], tmp[:, 0:64], AX.X), reads=[tmp], writes=[tmp2])
                P.op("act", lambda: nc.scalar.activation(tmp[:, 0:2], tmp2[:, 0:2], AF.Exp), reads=[tmp2], writes=[tmp])
                P.op("dve", lambda: nc.vector.scalar_tensor_tensor(lamv[:, l, 0:1], tmp[:, 1:2], -lam_init, tmp[:, 0:1],
                                                                   ALU.add, ALU.subtract), reads=[tmp], writes=[(lamv, (l, 0))])
                P.op("dve", lambda: nc.vector.tensor_scalar_mul(lamv[:, l, 1:2], pr(l, "subg"), 1.0 - lam_init),
                     reads=[par], writes=[(lamv, (l, 1))])

        mps = P.psb[7]

        def ada_part(l, k):
            awv = ada_w.t.ap()[l].rearrange("(kc p) n -> p kc n", p=128)
            def mods_mm(nb, ws, wv):
                for j in range(2):
                    n = nb * 2 + j
                    P.group("pe", [
                        (lambda kc=kc: nc.tensor.matmul(mps[:, 2 * n:2 * n + 2], wv[:, kc, j * 128:(j + 1) * 128],
                                                        scb[:, kc, :], start=(kc == 0), stop=(kc == NCH - 1)))
                        for kc in range(NCH)], reads=[ws, scb], writes=[(mps, k)])
            prev = None
            for nb in range(24 * k, 24 * (k + 1)):
                ws, wv = ring.load(awv[:, :, nb * 256:(nb + 1) * 256], NCH, 256)
                if prev is not None:
                    mods_mm(*prev)
                prev = (nb, ws, wv)
                yield
            mods_mm(*prev)
            mv = mps[:, 0:288].rearrange("p (n j) -> p n j", j=2)
            for j in range(2):
                P.op("dve", lambda: nc.vector.tensor_tensor(modr[:, l, 48 * k:48 * (k + 1), j], mv[:, 48 * k:48 * (k + 1), j],
                                                            pr(l, "adab", 48 * k, 48), ALU.add),
                     reads=[(mps, k), par], writes=[(modr, (l, k))])
            for j in range(2):
                P.op("dve", lambda: nc.vector.scalar_tensor_tensor(
                    modA[:, l, k, :, j], modr[:, l, (3 * k + 1) * 16:(3 * k + 2) * 16, j], 1.0,
                    pr(l, "ng", k * 16, 16), ALU.add, ALU.mult), reads=[(modr, (l, k)), par], writes=[(modA, (l, k))])
                P.op("dve", lambda: nc.vector.tensor_scalar_mul(
                    modG[:, l, k, :, j], modr[:, l, (3 * k + 2) * 16:(3 * k + 3) * 16, j], 1.0 if k == 1 else 0.5),
                    reads=[(modr, (l, k))], writes=[(modG, (l, k))])

        pending = []

        def ada_step():
            while pending:
                try:
                    next(pending[0])
                    return
                except StopIteration:
                    pending.pop(0)

        def ada_drain():
            while pending:
                ada_step()

        pending.append(ada_part(0, 0))
        ada_drain()

        def shiftp(l, k, c, j):
            return modr[:, l, (3 * k) * 16 + c, j:j + 1]

        def rsqrt_eps(dst, src, n):
            P.op("act", lambda: nc.scalar.activation(dst[:, 0:n], src[:, 0:n], AF.Ln, bias=epsc[:, 0:1], scale=1.0),
                 reads=[src, epsc], writes=[dst])
            P.op("act", lambda: nc.scalar.activation(dst[:, 0:n], dst[:, 0:n], AF.Exp, scale=-0.5), reads=[dst], writes=[dst])

        def norm_tile(ph, xt, sqb, rstd, start, n, consume, gains):
            P.dma("sp", xt[:, :, 0:n], xsv[:, :, start:start + n], reads=[xs], writes=[xt])
            sp_ = P.ps()
            for c in range(NCH):
                q = sqb[c % 2]
                P.op("act", lambda: nc.scalar.activation(q[:, 0:n], xt[:, c, 0:n], AF.Square), reads=[xt], writes=[q])
                P.op("pe", lambda: nc.tensor.matmul(sp_[:, 0:n], ones2048b[:], q[:, 0:n], start=(c == 0), stop=(c == NCH - 1)),
                     reads=[q, ones2048b], writes=[sp_])
            rsqrt_eps(rstd, sp_, n)
            for c in range(NCH):
                consume(c)

        def subtiles(with_ctx):
            if with_ctx:
                return [[(0, 256, 1), (256, 512, 0), (768, 384, 0)], [(1152, 512, 0), (1664, 512, 0), (2176, 128, 0)]]
            return [[(256, 512, 0), (768, 512, 0)], [(1280, 512, 0), (1792, 512, 0)]]

        def ffn(l, k, with_ctx):
            kn = 0 if k == 0 else 2
            KN = kn
            wgv = w_gate.t.ap()[l, k].rearrange("(kc p) n -> p kc n", p=128)
            wuv = w_up.t.ap()[l, k].rearrange("(kc p) n -> p kc n", p=128)
            wdv = w_down.t.ap()[l, k].rearrange("(f p) n -> p f n", p=128)
            for ST in subtiles(with_ctx):
                T = sum(s[1] for s in ST)
                offs = []
                o = 0
                for s in ST:
                    offs.append(o)
                    o += s[1]
                with Phase(P) as ph:
                    hT = ph.sbuf("hT", [128, NCH, T], BF16)
                    act = ph.sbuf("act", [128, 22, T], BF16)
                    xt = ph.sbuf("xt", [128, NCH, 256], F32)
                    sqb = [ph.sbuf("sq%d" % i, [128, 512], BF16) for i in range(2)]
                    tmpb = [ph.sbuf("tm%d" % i, [128, 512], F32) for i in range(2)]
                    rstd = ph.sbuf("rstd", [128, 512], F32)
                    xr = [ph.sbuf("xr%d" % i, [128, 512], F32) for i in range(2)]
                    xo = [ph.sbuf("xo%d" % i, [128, 512], F32) for i in range(2)]
                    cnt = [0]
                    halves = []
                    for si, (start, n, isc) in enumerate(ST):
                        for h0 in range(0, n, 256):
                            halves.append((si, start + h0, min(256, n - h0), isc, offs[si] + h0))
                    for (si, start, n, isc, off) in halves:

                        def consume(c, start=start, n=n, isc=isc, off=off, si=si):
                            tb = tmpb[c % 2]
                            P.op("dve", lambda: nc.vector.scalar_tensor_tensor(
                                tb[:, 0:n], xt[:, c, 0:n], modA[:, l, kn, c, isc:isc + 1], rstd[:, 0:n], ALU.mult, ALU.mult),
                                reads=[xt, rstd, (modA, (l, KN))], writes=[tb])
                            P.op("act", lambda: nc.scalar.activation(hT[:, c, off:off + n], tb[:, 0:n], AF.Identity,
                                                                     bias=shiftp(l, kn, c, isc), scale=1.0),
                                 reads=[tb, (modr, (l, KN))], writes=[(hT, si)])
                        norm_tile(ph, xt, sqb, rstd, start, n, consume, None)
                    for half in range(2):
                        for fp in range(11):
                            c0 = (half * 22 + fp * 2) * 128
                            gs, gv = ring.load(wgv[:, :, c0:c0 + 256], NCH, 256)
                            us, uv = ring.load(wuv[:, :, c0:c0 + 256], NCH, 256)
                            ada_step()
                            for j in range(2):
                                f = fp * 2 + j
                                for si, (start, n, isc) in enumerate(ST):
                                    off = offs[si]
                                    pg = P.ps()
                                    pu = P.ps()
                                    P.group("pe", [(lambda kc=kc: nc.tensor.matmul(
                                        pg[:, 0:n], gv[:, kc, j * 128:(j + 1) * 128], hT[:, kc, off:off + n],
                                        start=(kc == 0), stop=(kc == NCH - 1))) for kc in range(NCH)],
                                        reads=[gs, (hT, si)], writes=[pg])
                                    P.group("pe", [(lambda kc=kc: nc.tensor.matmul(
                                        pu[:, 0:n], uv[:, kc, j * 128:(j + 1) * 128], hT[:, kc, off:off + n],
                                        start=(kc == 0), stop=(kc == NCH - 1))) for kc in range(NCH)],
                                        reads=[us, (hT, si)], writes=[pu])
                                    tb = tmpb[cnt[0] % 2]
                                    cnt[0] += 1
                                    P.op("act", lambda: nc.scalar.activation(tb[:, 0:n], pg[:, 0:n], AF.Silu),
                                         reads=[pg], writes=[tb])
                                    P.op("dve", lambda: nc.vector.tensor_tensor(act[:, f, off:off + n], tb[:, 0:n], pu[:, 0:n],
                                                                                ALU.mult),
                                         reads=[tb, pu], writes=[(act, (f, si))])
                        for dp in range(8):
                            ds_, dv = ring.load(wdv[:, half * 22:(half + 1) * 22, dp * 256:(dp + 1) * 256], 22, 256)
                            ada_step()
                            for j in range(2):
                                d = dp * 2 + j
                                for si, (start, n, isc) in enumerate(ST):
                                    off = offs[si]
                                    pd = P.ps()
                                    P.group("pe", [(lambda f=f: nc.tensor.matmul(
                                        pd[:, 0:n], dv[:, f, j * 128:(j + 1) * 128], act[:, f, off:off + n],
                                        start=(f == 0), stop=(f == 21))) for f in range(22)],
                                        reads=[ds_] + [(act, (f, si)) for f in range(22)], writes=[pd])
                                    b = cnt[0] % 2
                                    cnt[0] += 1
                                    P.dma("sp", xr[b][:, 0:n], xs.t.ap()[d, :, start:start + n], reads=[(xs, (d, start))],
                                          writes=[xr[b]])
                                    P.op("dve", lambda: nc.vector.scalar_tensor_tensor(
                                        xo[b][:, 0:n], pd[:, 0:n], modG[:, l, kn, d, isc:isc + 1], xr[b][:, 0:n],
                                        ALU.mult, ALU.add), reads=[pd, xr[b], (modG, (l, kn))], writes=[xo[b]])
                                    P.dma("sp", xs.t.ap()[d, :, start:start + n], xo[b][:, 0:n], reads=[xo[b]],
                                          writes=[(xs, (d, start))])

        def proj(ws, wv, j0, hT, seg, si):
            start, n, isc = seg
            p_ = P.ps()
            P.group("pe", [(lambda kc=kc: nc.tensor.matmul(p_[:, 0:n], wv[:, kc, j0:j0 + 128], hT[:, kc, start:start + n],
                                                           start=(kc == 0), stop=(kc == NCH - 1))) for kc in range(NCH)],
                    reads=[ws, (hT, si)], writes=[p_])
            return p_

        def mixer(l, need_ctx):
            KN = 1
            lam_init = 0.8 - 0.6 * math.exp(-0.3 * l)
            winv = w_in.t.ap()[l].rearrange("(kc p) n -> p kc n", p=128)
            mv = mscr.t.ap()
            with Phase(P) as phh:
                hT = phh.sbuf("hmix", [128, NCH, NT], BF16)
                with Phase(P) as ph:
                    xt = ph.sbuf("xt", [128, NCH, 512], F32)
                    sqb = [ph.sbuf("sq%d" % i, [128, 512], BF16) for i in range(2)]
                    tmpb = [ph.sbuf("tm%d" % i, [128, 512], F32) for i in range(2)]
                    rstd = ph.sbuf("rstd", [128, 512], F32)
                    for si, (start, n, isc) in enumerate(SEGS):
                        def consume(c, start=start, n=n, isc=isc, si=si):
                            tb = tmpb[c % 2]
                            P.op("dve", lambda: nc.vector.scalar_tensor_tensor(
                                tb[:, 0:n], xt[:, c, 0:n], modA[:, l, 1, c, isc:isc + 1], rstd[:, 0:n], ALU.mult, ALU.mult),
                                reads=[xt, rstd, (modA, (l, KN))], writes=[tb])
                            P.op("act", lambda: nc.scalar.activation(hT[:, c, start:start + n], tb[:, 0:n], AF.Identity,
                                                                     bias=shiftp(l, 1, c, isc), scale=1.0),
                                 reads=[tb, (modr, (l, KN))], writes=[(hT, si)])
                        norm_tile(ph, xt, sqb, rstd, start, n, consume, None)
                if stop_after == "h":
                    return
                with Phase(P) as ph:
                    zb = [ph.sbuf("zb%d" % i, [128, 512], F32) for i in range(3)]
                    cnt = 0
                    for zp in range(24):
                        ws, wv = ring.load(winv[:, :, OFF_Z + zp * 256:OFF_Z + (zp + 1) * 256], NCH, 256)
                        for j in range(2):
                            zc = zp * 2 + j
                            for si, seg in enumerate(SEGS):
                                start, n, isc = seg
                                if isc and not need_ctx:
                                    continue
                                p_ = proj(ws, wv, j * 128, hT, seg, si)
                                b = zb[cnt % 3]
                                cnt += 1
                                P.op("act", lambda: nc.scalar.activation(b[:, 0:n], p_[:, 0:n], AF.Sigmoid), reads=[p_], writes=[b])
                                P.dma("sp", zsig.t.ap()[zc, :, start:start + n], b[:, 0:n], reads=[b], writes=[(zsig, (zc, si))])
                with Phase(P) as ph:
                    xp = ph.sbuf("xp", [128, PL], F32)
                    u = ph.sbuf("u", [128, PL], F32)
                    ub = ph.sbuf("ub", [128, PL], BF16)
                    hf = ph.sbuf("hf", [128, PL], F32)
                    hb = ph.sbuf("hb", [128, PL], F32)
                    RB = ph.sbuf("RB", [128, NT], F32)
                    IB = ph.sbuf("IB", [128, NT], F32)
                    gy = [ph.sbuf("gy0", [128, 512], F32)] * 2
                    mo = [ph.sbuf("mo0", [128, 512], BF16)] * 2
                    P.op("dve", lambda: nc.vector.memset(xp[:], 0.0), writes=[xp])
                    rgv = rgwd.t.ap()[l]
                    for c in range(8):
                        wxs, wxv = ring.load(winv[:, :, OFF_X + c * 128:OFF_X + (c + 1) * 128], NCH, 128)
                        wys, wyv = ring.load(winv[:, :, OFF_Y + c * 128:OFF_Y + (c + 1) * 128], NCH, 128)
                        rgs, rgt = ring.load(rgv[:, c], 4, 128)
                        for si, seg in enumerate(SEGS):
                            start, n, isc = seg
                            p_ = proj(wxs, wxv, 0, hT, seg, si)
                            pp = ppos(start)
                            P.op("act", lambda: nc.scalar.copy(xp[:, pp:pp + n], p_[:, 0:n]), reads=[p_], writes=[(xp, si)])
                        lo, hi = PAD, PL - PAD
                        P.op("dve", lambda: nc.vector.tensor_scalar(u[:, lo:hi], xp[:, lo - 1:hi - 1], pr(l, "rcw", c * 4, 1),
                                                                    pr(l, "rcb", c, 1), ALU.mult, ALU.add),
                             reads=[xp, par], writes=[u])
                        for j in range(1, 4):
                            P.op("dve", lambda: nc.vector.scalar_tensor_tensor(
                                u[:, lo:hi], xp[:, lo + j - 1:hi + j - 1], pr(l, "rcw", c * 4 + j, 1), u[:, lo:hi],
                                ALU.mult, ALU.add), reads=[xp, par, u], writes=[u])
                        P.op("act", lambda: nc.scalar.copy(ub[:, lo:hi], u[:, lo:hi]), reads=[u], writes=[ub])
                        for d in range(2):
                            hbuf = hf if d == 0 else hb
                            order = list(range(5)) if d == 0 else [0, 4, 3, 2, 1]
                            for si in range(5):
                                start, n, isc = SEGS[si]
                                pp = ppos(start)
                                pr_ = P.ps()
                                pi_ = P.ps()
                                P.op("pe", lambda: nc.tensor.matmul(pr_[:, 0:n], rgt[:, d * 2 + 0, :], ub[:, pp:pp + n],
                                                                    start=True, stop=True), reads=[rgs, ub], writes=[pr_])
                                P.op("pe", lambda: nc.tensor.matmul(pi_[:, 0:n], rgt[:, d * 2 + 1, :], ub[:, pp:pp + n],
                                                                    start=True, stop=True), reads=[rgs, ub], writes=[pi_])
                                P.op("act", lambda: nc.scalar.activation(RB[:, start:start + n], pr_[:, 0:n], AF.Sigmoid,
                                                                         bias=pr(l, "rgbr", d * 8 + c, 1), scale=1.0),
                                     reads=[pr_, par], writes=[(RB, si)])
                                P.op("act", lambda: nc.scalar.activation(IB[:, start:start + n], pi_[:, 0:n], AF.Sigmoid,
                                                                         bias=pr(l, "rgbi", d * 8 + c, 1), scale=1.0),
                                     reads=[pi_, par], writes=[(IB, si)])
                            for si in range(5):
                                start, n, isc = SEGS[si]
                                P.op("act", lambda: nc.scalar.activation(xp[:, ppos(start):ppos(start) + n], RB[:, start:start + n], AF.Exp,
                                                                         scale=cneg[:, l, 1, d * 8 + c:d * 8 + c + 1]),
                                     reads=[(RB, si), cneg], writes=[(xp, si)])
                                P.op("act", lambda: nc.scalar.activation(RB[:, start:start + n], RB[:, start:start + n], AF.Exp,
                                                                         scale=cneg[:, l, 0, d * 8 + c:d * 8 + c + 1]),
                                     reads=[(RB, si), cneg], writes=[(RB, si)])
                            for si in range(5):
                                start, n, isc = SEGS[si]
                                P.op("act", lambda: nc.scalar.activation(xp[:, ppos(start):ppos(start) + n], xp[:, ppos(start):ppos(start) + n], AF.Sqrt,
                                                                         bias=1.0, scale=-1.0),
                                     reads=[(xp, si)], writes=[(xp, si)])
                            prev = None
                            for si in order:
                                start, n, isc = SEGS[si]
                                pp = ppos(start)
                                P.op("dve", lambda: nc.vector.tensor_tensor(IB[:, start:start + n], IB[:, start:start + n],
                                                                            xp[:, ppos(start):ppos(start) + n], ALU.mult),
                                     reads=[(IB, si), (xp, si)], writes=[(IB, si)])
                                P.op("dve", lambda: nc.vector.tensor_tensor(IB[:, start:start + n], IB[:, start:start + n],
                                                                            u[:, pp:pp + n], ALU.mult),
                                     reads=[(IB, si), u], writes=[(IB, si)])
                                if prev is None:
                                    init = 0.0
                                else:
                                    init = hbuf[:, prev:prev + 1]
                                if d == 0:
                                    P.op("dve", lambda: nc.vector.tensor_tensor_scan(hbuf[:, pp:pp + n], RB[:, start:start + n],
                                                                                     IB[:, start:start + n], init, ALU.mult, ALU.add),
                                         reads=[(RB, si), (IB, si), hbuf], writes=[hbuf])
                                    prev = pp + n - 1
                                else:
                                    P.op("dve", lambda: nc.vector.tensor_tensor_scan(
                                        hbuf[:, pp:pp + n][:, ::-1], RB[:, start:start + n][:, ::-1], IB[:, start:start + n][:, ::-1],
                                        init, ALU.mult, ALU.add), reads=[(RB, si), (IB, si), hbuf], writes=[hbuf])
                                    prev = pp
                        for si, seg in enumerate(SEGS):
                            start, n, isc = seg
                            if isc and not need_ctx:
                                continue
                            pp = ppos(start)
                            p_ = proj(wys, wyv, 0, hT, seg, si)
                            b = si % 2
                            P.op("act", lambda: nc.scalar.activation(gy[b][:, 0:n], p_[:, 0:n], AF.Gelu_apprx_tanh),
                                 reads=[p_], writes=[gy[b]])
                            P.op("dve", lambda: nc.vector.tensor_tensor(hf[:, pp:pp + n], hf[:, pp:pp + n], hb[:, pp:pp + n], ALU.add),
                                 reads=[hf, hb], writes=[hf])
                            P.op("dve", lambda: nc.vector.tensor_tensor(mo[b][:, 0:n], gy[b][:, 0:n], hf[:, pp:pp + n], ALU.mult),
                                 reads=[gy[b], hf], writes=[mo[b]])
                            P.dma("sp", mv[0, c, :, start:start + n], mo[b][:, 0:n], reads=[mo[b]], writes=[(mscr, (0, c, si))])
                if stop_after == "rnn":
                    return
                with Phase(P) as ph:
                    gpb = [ph.sbuf("gpb%d" % i, [128, PL], BF16) for i in range(2)]
                    sg = [ph.sbuf("sg%d" % i, [128, 512], F32) for i in range(2)]
                    co = [ph.sbuf("co%d" % i, [128, 512], F32) for i in range(2)]
                    for g_ in gpb:
                        P.op("dve", lambda: nc.vector.memset(g_[:], 0.0), writes=[g_])
                    cnt = 0
                    for c in range(8):
                        gp = gpb[c % 2]
                        was, wav = ring.load(winv[:, :, OFF_G + c * 128:OFF_G + (c + 1) * 128], NCH, 128)
                        wgs, wgv_ = ring.load(winv[:, :, OFF_G + 1024 + c * 128:OFF_G + 1024 + (c + 1) * 128], NCH, 128)
                        dgs, dgv = ring.load(cvdd.t.ap()[l][:, c], 31, 128)
                        for si, seg in enumerate(SEGS):
                            start, n, isc = seg
                            pp = ppos(start)
                            pa = proj(was, wav, 0, hT, seg, si)
                            pg = proj(wgs, wgv_, 0, hT, seg, si)
                            b = si % 2
                            P.op("act", lambda: nc.scalar.activation(sg[b][:, 0:n], pg[:, 0:n], AF.Sigmoid), reads=[pg], writes=[sg[b]])
                            P.op("dve", lambda: nc.vector.tensor_tensor(gp[:, pp:pp + n], sg[b][:, 0:n], pa[:, 0:n], ALU.mult),
                                 reads=[sg[b], pa], writes=[(gp, si)])
                        for si, seg in enumerate(SEGS):
                            start, n, isc = seg
                            if isc and not need_ctx:
                                continue
                            pp = ppos(start)
                            pc = P.ps()
                            P.group("pe", [(lambda j=j: nc.tensor.matmul(pc[:, 0:n], dgv[:, j, :], gp[:, pp + j - 15:pp + j - 15 + n],
                                                                         start=(j == 0), stop=(j == 30))) for j in range(31)],
                                    reads=[dgs, gp], writes=[pc])
                            b = cnt % 2
                            cnt += 1
                            P.op("act", lambda: nc.scalar.activation(co[b][:, 0:n], pc[:, 0:n], AF.Identity,
                                                                     bias=pr(l, "cvb", c, 1), scale=1.0),
                                 reads=[pc, par], writes=[co[b]])
                            P.dma("sp", cvo.t.ap()[c, :, start:start + n], co[b][:, 0:n], reads=[co[b]], writes=[(cvo, (c, si))])
                with Phase(P) as ph:
                    ct = ph.sbuf("ct", [128, 8, 512], F32)
                    sqb = [ph.sbuf("sq%d" % i, [128, 512], F32) for i in range(2)]
                    mean = ph.sbuf("mean", [128, 512], F32)
                    rstd = ph.sbuf("rstd", [128, 512], F32)
                    tb = [ph.sbuf("tb%d" % i, [128, 512], F32) for i in range(2)]
                    mo = [ph.sbuf("mo%d" % i, [128, 512], BF16) for i in range(2)]
                    cvv = cvo.t.ap().rearrange("c p t -> p c t")
                    for si, seg in enumerate(SEGS):
                        start, n, isc = seg
                        if isc and not need_ctx:
                            continue
                        P.dma("sp", ct[:, :, 0:n], cvv[:, :, start:start + n], reads=[cvo], writes=[ct])
                        pm = P.ps()
                        pq = P.ps()
                        for c in range(8):
                            q = sqb[c % 2]
                            P.op("act", lambda: nc.scalar.activation(q[:, 0:n], ct[:, c, 0:n], AF.Square), reads=[ct], writes=[q])
                            P.op("pe", lambda: nc.tensor.matmul(pm[:, 0:n], ones128, ct[:, c, 0:n], start=(c == 0), stop=(c == 7)),
                                 reads=[ct, cst], writes=[pm])
                            P.op("pe", lambda: nc.tensor.matmul(pq[:, 0:n], ones128, q[:, 0:n], start=(c == 0), stop=(c == 7)),
                                 reads=[q, cst], writes=[pq])
                        P.op("act", lambda: nc.scalar.mul(mean[:, 0:n], pm[:, 0:n], 0.125), reads=[pm], writes=[mean])
                        P.op("dve", lambda: nc.vector.tensor_tensor(rstd[:, 0:n], mean[:, 0:n], mean[:, 0:n], ALU.mult),
                             reads=[mean], writes=[rstd])
                        P.op("dve", lambda: nc.vector.scalar_tensor_tensor(rstd[:, 0:n], pq[:, 0:n], 0.125, rstd[:, 0:n],
                                                                           ALU.mult, ALU.subtract), reads=[pq, rstd], writes=[rstd])
                        rsqrt_eps(rstd, rstd, n)
                        for c in range(8):
                            b = c % 2
                            P.op("dve", lambda: nc.vector.tensor_tensor(tb[b][:, 0:n], ct[:, c, 0:n], mean[:, 0:n], ALU.subtract),
                                 reads=[ct, mean], writes=[tb[b]])
                            P.op("dve", lambda: nc.vector.tensor_tensor(tb[b][:, 0:n], tb[b][:, 0:n], rstd[:, 0:n], ALU.mult),
                                 reads=[tb[b], rstd], writes=[tb[b]])
                            P.op("act", lambda: nc.scalar.activation(mo[b][:, 0:n], tb[b][:, 0:n], AF.Silu,
                                                                     bias=pr(l, "cvbb", c, 1), scale=pr(l, "cvg", c, 1)),
                                 reads=[tb[b], par], writes=[mo[b]])
                            P.dma("sp", mv[1, c, :, start:start + n], mo[b][:, 0:n], reads=[mo[b]], writes=[(mscr, (1, c, si))])
                if stop_after == "conv":
                    return
                with Phase(P) as ph:
                    QT = ph.sbuf("QT", [128, NT], BF16)
                    KT = ph.sbuf("KT", [128, NT], BF16)
                    Vp = ph.sbuf("Vp", [128, 18, 256], BF16)
                    cs = [ph.sbuf("cs%d" % i, [128, 2, 512], F32) for i in range(2)]
                    qf = [ph.sbuf("qf%d" % i, [128, 512], F32) for i in range(2)]
                    t1 = [ph.sbuf("t1%d" % i, [128, 512], F32) for i in range(2)]
                    t2 = [ph.sbuf("t2%d" % i, [128, 512], F32) for i in range(2)]
                    Pt = [ph.sbuf("Pt%d" % i, [128, 512], BF16) for i in range(4)]
                    rz = [ph.sbuf("rz%d" % i, [128, 512], F32) for i in range(2)]
                    ob = [ph.sbuf("ob%d" % i, [128, 512], F32) for i in range(2)]
                    osq = ph.sbuf("osq", [128, 512], BF16)
                    orr = ph.sbuf("orr", [128, 512], F32)
                    mo = [ph.sbuf("mo%d" % i, [128, 512], BF16) for i in range(2)]
                    neglam = lamv[:, l, 0:1]
                    gsub = lamv[:, l, 1:2]
                    cnt = 0
                    for h in range(8):
                        if h % 2 == 0:
                            wvs, wvv = ring.load(winv[:, :, OFF_V + h * 128:OFF_V + (h + 2) * 128], NCH, 256)
                            for tc in range(18):
                                si = 0 if tc < 2 else 1 + (tc - 2) // 4
                                p_ = P.ps()
                                P.group("pe", [(lambda kc=kc: nc.tensor.matmul(p_[:, 0:256], hT[:, kc, tc * 128:(tc + 1) * 128],
                                                                               wvv[:, kc, :], start=(kc == 0), stop=(kc == NCH - 1)))
                                               for kc in range(NCH)], reads=[wvs, (hT, si)], writes=[p_])
                                if tc % 2 == 0:
                                    P.op("act", lambda: nc.scalar.copy(Vp[:, tc, :], p_[:, 0:256]), reads=[p_], writes=[(Vp, tc)])
                                else:
                                    P.op("dve", lambda: nc.vector.tensor_copy(Vp[:, tc, :], p_[:, 0:256]), reads=[p_], writes=[(Vp, tc)])
                        wqs, wqv = ring.load(winv[:, :, OFF_Q + h * 128:OFF_Q + (h + 1) * 128], NCH, 128)
                        wks, wkv = ring.load(winv[:, :, OFF_K + h * 128:OFF_K + (h + 1) * 128], NCH, 128)
                        for si, seg in enumerate(SEGS):
                            start, n, isc = seg
                            if isc:
                                if need_ctx:
                                    p_ = proj(wqs, wqv, 0, hT, seg, si)
                                    P.op("act", lambda: nc.scalar.copy(QT[:, start:start + n], p_[:, 0:n]), reads=[p_], writes=[(QT, si)])
                                p_ = proj(wks, wkv, 0, hT, seg, si)
                                P.op("act", lambda: nc.scalar.copy(KT[:, start:start + n], p_[:, 0:n]), reads=[p_], writes=[(KT, si)])
                                continue
                            cb = cs[si % 2]
                            P.dma("sp", cb[:, 0, :], cosd.t.ap()[:, start - CTX:start - CTX + n], writes=[cb])
                            P.dma("sp", cb[:, 1, :], sind.t.ap()[:, start - CTX:start - CTX + n], writes=[cb])
                            for (ws_, wv_, dst) in ((wqs, wqv, QT), (wks, wkv, KT)):
                                p_ = proj(ws_, wv_, 0, hT, seg, si)
                                b = cnt % 2
                                cnt += 1
                                P.op("act", lambda: nc.scalar.copy(qf[b][:, 0:n], p_[:, 0:n]), reads=[p_], writes=[qf[b]])
                                p2 = P.ps()
                                P.op("pe", lambda: nc.tensor.matmul(p2[:, 0:n], rperm, qf[b][:, 0:n], start=True, stop=True),
                                     reads=[qf[b], cst], writes=[p2])
                                P.op("dve", lambda: nc.vector.tensor_tensor(t1[b][:, 0:n], qf[b][:, 0:n], cb[:, 0, 0:n], ALU.mult),
                                     reads=[qf[b], cb], writes=[t1[b]])
                                P.op("dve", lambda: nc.vector.tensor_tensor(t2[b][:, 0:n], p2[:, 0:n], cb[:, 1, 0:n], ALU.mult),
                                     reads=[p2, cb], writes=[t2[b]])
                                P.op("dve", lambda: nc.vector.tensor_tensor(dst[:, start:start + n], t1[b][:, 0:n], t2[b][:, 0:n], ALU.add),
                                     reads=[t1[b], t2[b]], writes=[(dst, si)])
                        hoff = (h % 2) * 128
                        for si, seg in enumerate(SEGS):
                            qs, qn, isc = seg
                            if isc and not need_ctx:
                                continue
                            keys = [0, 1] if isc else list(range(18))
                            nk = len(keys)
                            O = [P.psb[4], P.psb[5]]
                            Z = [P.psb[6], P.psb[7]]

                            def scores(kc):
                                ksi = 0 if kc < 2 else 1 + (kc - 2) // 4
                                out = []
                                for comp in range(2):
                                    sc = P.ps()
                                    P.op("pe", lambda: nc.tensor.matmul(
                                        sc[:, 0:qn], KT[comp * 64:(comp + 1) * 64, kc * 128:(kc + 1) * 128],
                                        QT[comp * 64:(comp + 1) * 64, qs:qs + qn], start=True, stop=True),
                                        reads=[(KT, ksi), (QT, si)], writes=[sc])
                                    out.append(sc)
                                return out
                            s_cur = scores(keys[0])
                            for i, kc in enumerate(keys):
                                s_next = scores(keys[i + 1]) if i + 1 < nk else None
                                for comp in range(2):
                                    pt = Pt[(i % 2) * 2 + comp]
                                    P.op("act", lambda: nc.scalar.activation(pt[:, 0:qn], s_cur[comp][:, 0:qn], AF.Exp, scale=0.125),
                                         reads=[s_cur[comp]], writes=[pt])
                                for comp in range(2):
                                    pt = Pt[(i % 2) * 2 + comp]
                                    P.op("pe", lambda: nc.tensor.matmul(O[comp][:, 0:qn], Vp[:, kc, hoff:hoff + 128], pt[:, 0:qn],
                                                                        start=(i == 0), stop=(i == nk - 1)),
                                         reads=[(Vp, kc), pt], writes=[O[comp]])
                                    P.op("pe", lambda: nc.tensor.matmul(Z[comp][:, 0:qn], onesb[:], pt[:, 0:qn],
                                                                        start=(i == 0), stop=(i == nk - 1)),
                                         reads=[onesb, pt], writes=[Z[comp]])
                                s_cur = s_next
                            for comp in range(2):
                                P.op("act", lambda: nc.scalar.activation(rz[comp][:, 0:qn], Z[comp][:, 0:qn], AF.Ln), reads=[Z[comp]], writes=[rz[comp]])
                                P.op("act", lambda: nc.scalar.activation(rz[comp][:, 0:qn], rz[comp][:, 0:qn], AF.Exp, scale=-1.0), reads=[rz[comp]], writes=[rz[comp]])
                                P.op("dve", lambda: nc.vector.tensor_tensor(ob[comp][:, 0:qn], O[comp][:, 0:qn], rz[comp][:, 0:qn], ALU.mult),
                                     reads=[O[comp], rz[comp]], writes=[ob[comp]])
                            P.op("dve", lambda: nc.vector.scalar_tensor_tensor(ob[0][:, 0:qn], ob[1][:, 0:qn], neglam, ob[0][:, 0:qn],
                                                                               ALU.mult, ALU.add), reads=[ob[0], ob[1], lamv], writes=[ob[0]])
                            P.op("act", lambda: nc.scalar.activation(osq[:, 0:qn], ob[0][:, 0:qn], AF.Square), reads=[ob[0]], writes=[osq])
                            pm = P.ps()
                            P.op("pe", lambda: nc.tensor.matmul(pm[:, 0:qn], ones128b[:], osq[:, 0:qn], start=True, stop=True),
                                 reads=[osq, ones128b], writes=[pm])
                            rsqrt_eps(orr, pm, qn)
                            P.op("dve", lambda: nc.vector.tensor_tensor(orr[:, 0:qn], orr[:, 0:qn], ob[0][:, 0:qn], ALU.mult),
                                 reads=[orr, ob[0]], writes=[orr])
                            b = si % 2
                            P.op("act", lambda: nc.scalar.activation(mo[b][:, 0:qn], orr[:, 0:qn], AF.Identity, bias=0.0, scale=gsub),
                                 reads=[orr, lamv], writes=[mo[b]])
                            P.dma("sp", mv[2, h, :, qs:qs + qn], mo[b][:, 0:qn], reads=[mo[b]], writes=[(mscr, (2, h, si))])
            if stop_after == "att":
                return
            bwv = [t.t.ap()[l].rearrange("(kc p) n -> p kc n", p=128) for t in (rnn_wo, cv_wo, da_wo)]
            wov = w_out.t.ap()[l].rearrange("(kc p) n -> p kc n", p=128)
            mvv = mscr.t.ap().rearrange("b c p t -> p b c t")
            with Phase(P) as ph:
                mt = ph.sbuf("mt", [128, 3, 8, 1024], BF16)
                mg = ph.sbuf("mg", [128, NCH, 1024], BF16)
                zt = [ph.sbuf("zt%d" % i, [128, 3, 512], F32) for i in range(2)]
                accb = [ph.sbuf("ac%d" % i, [128, 512], F32) for i in range(2)]
                tmb = [ph.sbuf("tmg%d" % i, [128, 512], F32) for i in range(2)]
                xr = [ph.sbuf("xr%d" % i, [128, 512], F32) for i in range(2)]
                xo = [ph.sbuf("xo%d" % i, [128, 512], F32) for i in range(2)]
                cnt = 0
                groups = [[0, 1], [2, 3], [4]] if need_ctx else [[1, 2], [3, 4]]
                for gi, grp in enumerate(groups):
                    goff = {}
                    o = 0
                    for si in grp:
                        goff[si] = o
                        o += SEGS[si][1]
                    for si in grp:
                        start, n, isc = SEGS[si]
                        off = goff[si]
                        for br in range(3):
                            P.dma("sp", mt[:, br, :, off:off + n], mvv[:, br, :, start:start + n], reads=[mscr],
                                  writes=[(mt, (br, si))])
                    for dp in range(8):
                        wts = []
                        for br in range(3):
                            wts.append(ring.load(bwv[br][:, :, dp * 256:(dp + 1) * 256], 8, 256))
                        for j in range(2):
                            d = dp * 2 + j
                            for si in grp:
                                start, n, isc = SEGS[si]
                                off = goff[si]
                                z_ = zt[cnt % 2]
                                a_ = accb[cnt % 2]
                                t_ = tmb[cnt % 2]
                                cnt += 1
                                for br in range(3):
                                    P.dma("sp", z_[:, br, 0:n], zsig.t.ap()[br * 16 + d, :, start:start + n], reads=[zsig],
                                          writes=[(z_, br)])
                                for br in range(3):
                                    ws, wv = wts[br]
                                    p_ = P.ps()
                                    P.group("pe", [(lambda kc=kc: nc.tensor.matmul(p_[:, 0:n], wv[:, kc, j * 128:(j + 1) * 128],
                                                                                   mt[:, br, kc, off:off + n], start=(kc == 0), stop=(kc == 7)))
                                                   for kc in range(8)], reads=[ws, (mt, (br, si))], writes=[p_])
                                    if br == 0:
                                        P.op("dve", lambda: nc.vector.tensor_tensor(a_[:, 0:n], p_[:, 0:n], z_[:, br, 0:n], ALU.mult),
                                             reads=[p_, (z_, br)], writes=[a_])
                                    else:
                                        P.op("dve", lambda: nc.vector.tensor_tensor(t_[:, 0:n], p_[:, 0:n], z_[:, br, 0:n], ALU.mult),
                                             reads=[p_, (z_, br)], writes=[t_])
                                        if br == 1:
                                            P.op("dve", lambda: nc.vector.tensor_tensor(a_[:, 0:n], a_[:, 0:n], t_[:, 0:n], ALU.add),
                                                 reads=[a_, t_], writes=[a_])
                                        else:
                                            P.op("dve", lambda: nc.vector.tensor_tensor(mg[:, d, off:off + n], a_[:, 0:n], t_[:, 0:n], ALU.add),
                                                 reads=[a_, t_], writes=[(mg, (d, si))])
                    for dp in range(8):
                        ws, wv = ring.load(wov[:, :, dp * 256:(dp + 1) * 256], NCH, 256)
                        for j in range(2):
                            d = dp * 2 + j
                            for si in grp:
                                start, n, isc = SEGS[si]
                                off = goff[si]
                                p_ = P.ps()
                                P.group("pe", [(lambda kc=kc: nc.tensor.matmul(p_[:, 0:n], wv[:, kc, j * 128:(j + 1) * 128],
                                                                               mg[:, kc, off:off + n], start=(kc == 0), stop=(kc == NCH - 1)))
                                               for kc in range(NCH)], reads=[ws] + [(mg, (kc, si)) for kc in range(NCH)], writes=[p_])
                                b = cnt % 2
                                cnt += 1
                                P.dma("sp", xr[b][:, 0:n], xs.t.ap()[d, :, start:start + n], reads=[(xs, (d, start))], writes=[xr[b]])
                                P.op("dve", lambda: nc.vector.scalar_tensor_tensor(
                                    xo[b][:, 0:n], p_[:, 0:n], modG[:, l, 1, d, isc:isc + 1], xr[b][:, 0:n], ALU.mult, ALU.add),
                                    reads=[p_, xr[b], (modG, (l, 1))], writes=[xo[b]])
                                P.dma("sp", xs.t.ap()[d, :, start:start + n], xo[b][:, 0:n], reads=[xo[b]], writes=[(xs, (d, start))])

        def final_norm():
            with Phase(P) as ph:
                xt = ph.sbuf("xt", [128, NCH, 512], F32)
                sqb = [ph.sbuf("sq%d" % i, [128, 512], BF16) for i in range(2)]
                rstd = ph.sbuf("rstd", [128, 512], F32)
                yo = [ph.sbuf("yo%d" % i, [128, 512], F32) for i in range(2)]
                for si, (start, n, isc) in enumerate(SEGS):
                    if isc:
                        continue

                    def consume(c, start=start, n=n):
                        y_ = yo[c % 2]
                        P.op("dve", lambda: nc.vector.scalar_tensor_tensor(
                            y_[:, 0:n], xt[:, c, 0:n], par[:, 0, _po["fing"] + c:_po["fing"] + c + 1], rstd[:, 0:n],
                            ALU.mult, ALU.mult), reads=[xt, rstd, par], writes=[y_])
                        P.dma("sp", yT.t.ap()[c, :, start - CTX:start - CTX + n], y_[:, 0:n], reads=[y_], writes=[(yT, (c, si))])
                    norm_tile(ph, xt, sqb, rstd, start, n, consume, None)

        def forward():
            for l in range(DEPTH):
                need_ctx = l < DEPTH - 1
                if l == 0:
                    pending.extend([ada_part(0, 1), ada_part(0, 2)])
                ffn(l, 0, True)
                ada_drain()
                dbgx("x_ffn1_%d" % l)
                if stop_after == "ffn1_%d" % l:
                    return
                mixer(l, need_ctx)
                dbgx("x_mix_%d" % l)
                if stop_after is not None and stop_after in ("h", "rnn", "conv", "att", "mix_%d" % l):
                    return
                if l == 0:
                    pending.extend([ada_part(1, 0), ada_part(1, 1), ada_part(1, 2)])
                ffn(l, 1, need_ctx)
                ada_drain()
                dbgx("x_ffn2_%d" % l)
                if stop_after == "ffn2_%d" % l:
                    return
            final_norm()

        forward()
        P.barrier(final=True)
    return nc, dbg


def _fm(v):
    v = np.asarray(v)
    return np.ascontiguousarray(v.reshape(-1, 128).T)


def _host_consts():
    inv = (np.float32(10000.0) ** (-np.arange(16, dtype=np.float32) * np.float32(2.0) / np.float32(32))).astype(np.float32)
    t = np.arange(SEQ)
    row = (t // 64).astype(np.float32)
    col = (t % 64).astype(np.float32)
    cosT = np.zeros((128, SEQ), np.float32)
    sinT = np.zeros((128, SEQ), np.float32)
    rperm = np.zeros((128, 128), np.float32)
    for p in range(128):
        d = p % 64
        a = d // 32
        half = (d % 32) // 16
        n = d % 16
        ang = ((row if a == 0 else col) * inv[n]).astype(np.float32)
        cosT[p] = np.cos(ang).astype(np.float32)
        sinT[p] = np.sin(ang).astype(np.float32)
        if half == 0:
            rperm[p + 16, p] = -1.0
        else:
            rperm[p - 16, p] = 1.0
    cst = np.zeros((128, 3, 128), np.float32)
    cst[:, 0, :] = 1.0 / 2048.0
    cst[:, 1, :] = 1.0 / 128.0
    cst[:, 2, :] = rperm
    return cosT, sinT, cst


def _prep_inputs(inp):
    f32 = np.float32
    cosT, sinT, cst = _host_consts()
    par = np.zeros((DEPTH, 128, NPAR), f32)
    rgw = np.zeros((DEPTH, 128, 8, 4, 128), f32)
    for l in range(DEPTH):
        def put(name, arr):
            arr = np.asarray(arr, f32)
            par[l, :, _po[name]:_po[name] + arr.shape[1]] = arr
        put("adab", _fm(inp["ada_b"][l]))
        put("ng", _fm(inp["norm_g"][l].reshape(-1)))
        put("rcw", inp["rnn_conv_w"][l].T.reshape(8, 128, 4).transpose(1, 0, 2).reshape(128, 32))
        put("rcb", _fm(inp["rnn_conv_b"][l]))
        put("rgbr", _fm(inp["rg_b_r"][l].reshape(-1)))
        put("rgbi", _fm(inp["rg_b_i"][l].reshape(-1)))
        put("rglam", _fm(inp["rg_lam"][l].reshape(-1)))
        put("cvw", inp["cv_dw_w"][l].T.reshape(8, 128, 31).transpose(1, 0, 2).reshape(128, 248))
        put("cvb", _fm(inp["cv_dw_b"][l]))
        put("cvg", _fm(inp["cv_ln_g"][l]))
        put("cvbb", _fm(inp["cv_ln_b"][l]))
        put("dalam", np.broadcast_to(inp["da_lam"][l].reshape(1, 256), (128, 256)))
        put("subg", inp["da_subln_g"][l].reshape(128, 1))
        put("fing", _fm(inp["final_g"]))
        for d in range(2):
            for g, nm in enumerate(("rg_w_r", "rg_w_i")):
                w = inp[nm][l, d]
                for c in range(8):
                    rgw[l, 0:64, c, d * 2 + g, 0:64] = w[2 * c]
                    rgw[l, 64:128, c, d * 2 + g, 64:128] = w[2 * c + 1]
    cvd = np.zeros((DEPTH, 128, 8, 31, 128), f32)
    ar = np.arange(128)
    for l in range(DEPTH):
        w = np.asarray(inp["cv_dw_w"][l], f32).T.reshape(8, 128, 31)
        for c in range(8):
            cvd[l, ar, c, :, ar] = w[c]
    shared = {
        "par": par, "rgw": rgw, "cvd": cvd, "cosT": cosT, "sinT": sinT, "cst": cst,
        "ada_w": np.ascontiguousarray(inp["ada_w"], dtype=f32),
        "ffn_w_gate": np.ascontiguousarray(inp["ffn_w_gate"], dtype=f32),
        "ffn_w_up": np.ascontiguousarray(inp["ffn_w_up"], dtype=f32),
        "ffn_w_down": np.ascontiguousarray(inp["ffn_w_down"], dtype=f32),
        "w_in": np.ascontiguousarray(inp["w_in"], dtype=f32),
        "rnn_w_out": np.ascontiguousarray(inp["rnn_w_out"], dtype=f32),
        "cv_w_out": np.ascontiguousarray(inp["cv_w_out"], dtype=f32),
        "da_w_o": np.ascontiguousarray(inp["da_w_o"], dtype=f32),
        "w_out": np.ascontiguousarray(inp["w_out"], dtype=f32),
    }
    maps = []
    B = inp["x"].shape[0]
    for b in range(B):
        xt = np.concatenate([inp["ctx"][b], inp["x"][b]], axis=0).astype(f32)
        xT = np.ascontiguousarray(xt.T).reshape(NCH, 128, NT)
        cc = np.stack([_fm(inp["c"][b]), _fm(inp["c_ctx"])], axis=-1).astype(f32)
        m = dict(shared)
        m["xT"] = xT
        m["cc"] = np.ascontiguousarray(cc)
        maps.append(m)
    return maps


_CACHE = {}


def kernel(**inputs):
    inp = {k: np.asarray(v) for k, v in inputs.items()}
    maps = _prep_inputs(inp)
    if "nc" not in _CACHE:
        _CACHE["nc"] = build_program()[0]
    nc = _CACHE["nc"]
    res = run_bass_kernel_spmd(nc, maps, core_ids=list(range(len(maps))))
    outs = []
    for r in res.results:
        yT = np.asarray(r["yT"]).reshape(D, SEQ)
        outs.append(np.ascontiguousarray(yT.T))
    return np.stack(outs, axis=0).astype(np.float32)
```

```python
from contextlib import ExitStack
import math
import numpy as np
import concourse.bass as bass
import concourse.mybir as mybir
from concourse.bass_utils import run_bass_kernel_spmd

F32 = mybir.dt.float32
BF16 = mybir.dt.bfloat16
AF = mybir.ActivationFunctionType
ALU = mybir.AluOpType
AX = mybir.AxisListType

D = 2048
NCH = 16
SEQ = 2048
CTX = 256
NT = SEQ + CTX
DFF = 5632
NFF = 44
DIN = 13312
EPS = 1e-6
DEPTH = 2
OFF_X, OFF_Y, OFF_G, OFF_Q, OFF_K, OFF_V, OFF_Z = 0, 1024, 2048, 4096, 5120, 6144, 7168
SEGS = [(0, 256, 1), (256, 512, 0), (768, 512, 0), (1280, 512, 0), (1792, 512, 0)]
PAD = 16
PL = PAD + CTX + PAD + SEQ + PAD


def ppos(t):
    return t + PAD if t < CTX else t + 2 * PAD


_po = {}
_n = 0
for _name, _w in (("adab", 144), ("ng", 48), ("rcw", 32), ("rcb", 8), ("rgbr", 16), ("rgbi", 16), ("rglam", 16),
                  ("cvw", 248), ("cvb", 8), ("cvg", 8), ("cvbb", 8), ("dalam", 256), ("subg", 1), ("fing", 16)):
    _po[_name] = _n
    _n += _w
NPAR = _n

WHOLE = "__whole__"


class _St:
    __slots__ = ("w", "r")

    def __init__(self):
        self.w = None
        self.r = []


class Buf:
    def __init__(self, t, name):
        self.t = t
        self.name = name
        self.st = {}

    def __getitem__(self, idx):
        return self.t[idx]


class Tok:
    __slots__ = ("eng", "seq", "sem", "val")

    def __init__(self, eng, seq=None, sem=None, val=None):
        self.eng = eng
        self.seq = seq
        self.sem = sem
        self.val = val


class Prog:
    ENGS = ("pe", "act", "dve", "pool", "sp")
    NDMA = 8

    def __init__(self, nc, es):
        self.nc = nc
        self.es = es
        self.e = {"pe": nc.tensor, "act": nc.scalar, "dve": nc.vector, "pool": nc.gpsimd, "sp": nc.sync}
        self.sem = {k: es.enter_context(nc.semaphore("s_" + k)) for k in self.ENGS}
        self.nseq = {k: 0 for k in self.ENGS}
        self.ninc = {k: 0 for k in self.ENGS}
        self.last_ins = {k: None for k in self.ENGS}
        self.incs = {k: [] for k in self.ENGS}
        self.recent = {k: [] for k in self.ENGS}
        self.seen = {k: {} for k in self.ENGS}
        self.dsem = {q: [es.enter_context(nc.semaphore("d_%s%d" % (q, i))) for i in range(self.NDMA)]
                     for q in ("sp", "pool")}
        self.dcnt = {q: 0 for q in ("sp", "pool")}
        self.dlast = {q: [0] * self.NDMA for q in ("sp", "pool")}
        self.uid = 0
        self.psb = None
        self.psi = 0

    def sbuf(self, name, shape, dt):
        return Buf(self.es.enter_context(self.nc.sbuf_tensor("sb_" + name, list(shape), dt)), name)

    def dram(self, name, shape, dt, kind="Internal"):
        return Buf(self.nc.dram_tensor(name, list(shape), dt, kind=kind), name)

    def init_psum(self):
        self.psb = [Buf(self.es.enter_context(self.nc.psum_tensor("psb%d" % i, [128, 512], F32)), "ps%d" % i)
                    for i in range(8)]

    def ps(self):
        b = self.psb[self.psi % 4]
        self.psi += 1
        return b

    def _resolve(self, tok):
        if tok.eng == "dma":
            return tok.sem, tok.val
        eng = tok.eng
        lst = self.incs[eng]
        lo, hi = 0, len(lst)
        while lo < hi:
            mid = (lo + hi) // 2
            if lst[mid][0] >= tok.seq:
                hi = mid
            else:
                lo = mid + 1
        if lo < len(lst):
            return self.sem[eng], lst[lo][1]
        rec = self.recent[eng]
        k = 0
        while rec[k][0] < tok.seq:
            k += 1
        sq, ins = rec[k]
        del rec[:k + 1]
        self.ninc[eng] += 1
        ins.then_inc(self.sem[eng], 1)
        lst.append((sq, self.ninc[eng]))
        return self.sem[eng], self.ninc[eng]

    def _wait(self, eng, tok):
        if tok is None:
            return
        if tok.eng == "pe" and eng == "pe":
            return
        sem, val = self._resolve(tok)
        key = id(sem)
        if self.seen[eng].get(key, 0) >= val:
            return
        self.seen[eng][key] = val
        self.e[eng].wait_ge(sem, val)

    @staticmethod
    def _norm(x):
        if isinstance(x, Buf):
            return x, WHOLE
        return x

    @staticmethod
    def _states(buf, key):
        if key == WHOLE:
            if WHOLE not in buf.st:
                buf.st[WHOLE] = _St()
            return list(buf.st.values())
        if key not in buf.st:
            buf.st[key] = _St()
        out = [buf.st[key]]
        if WHOLE in buf.st:
            out.append(buf.st[WHOLE])
        return out

    def deps(self, eng, reads, writes):
        for x in reads:
            buf, key = self._norm(x)
            for st in self._states(buf, key):
                self._wait(eng, st.w)
        for x in writes:
            buf, key = self._norm(x)
            for st in self._states(buf, key):
                self._wait(eng, st.w)
                for r in st.r:
                    self._wait(eng, r)

    def record(self, tok, reads, writes):
        for x in reads:
            buf, key = self._norm(x)
            sts = self._states(buf, key) if key == WHOLE else [self._states(buf, key)[0]]
            for st in sts:
                if tok.eng != "dma":
                    st.r = [r for r in st.r if r.eng != tok.eng]
                st.r.append(tok)
        for x in writes:
            buf, key = self._norm(x)
            if key == WHOLE:
                buf.st = {WHOLE: _St()}
                buf.st[WHOLE].w = tok
            else:
                st = self._states(buf, key)[0]
                st.w = tok
                st.r = []

    def op(self, eng, fn, reads=(), writes=()):
        self.deps(eng, reads, writes)
        ins = fn()
        self.nseq[eng] += 1
        self.last_ins[eng] = ins
        rec = self.recent[eng]
        rec.append((self.nseq[eng], ins))
        if len(rec) > 96:
            del rec[:32]
        self.record(Tok(eng, seq=self.nseq[eng]), reads, writes)
        return ins

    def group(self, eng, fns, reads=(), writes=()):
        self.deps(eng, reads, writes)
        ins = None
        for fn in fns:
            ins = fn()
            self.nseq[eng] += 1
        self.last_ins[eng] = ins
        rec = self.recent[eng]
        rec.append((self.nseq[eng], ins))
        if len(rec) > 96:
            del rec[:32]
        self.record(Tok(eng, seq=self.nseq[eng]), reads, writes)

    def dma(self, q, out, in_, reads=(), writes=(), **kw):
        self.deps(q, reads, writes)
        j = self.dcnt[q]
        self.dcnt[q] += 1
        i = j % self.NDMA
        s = self.dsem[q][i]
        rnd = j // self.NDMA
        if rnd > 0:
            key = id(s)
            if self.seen[q].get(key, 0) < 16 * rnd:
                self.seen[q][key] = 16 * rnd
                self.e[q].wait_ge(s, 16 * rnd)
        ins = self.e[q].dma_start(out=out, in_=in_, **kw)
        ins.then_inc(s, 16)
        self.dlast[q][i] = 16 * (rnd + 1)
        tok = Tok("dma", sem=s, val=16 * (rnd + 1))
        self.record(tok, reads, writes)
        return tok

    def barrier(self, final=False):
        toks = [Tok(e, seq=self.nseq[e]) for e in ("pe", "act", "dve") if self.nseq[e] > 0]
        dt = [Tok("dma", sem=self.dsem["sp"][i], val=self.dlast["sp"][i]) for i in range(self.NDMA)
              if self.dlast["sp"][i] > 0]
        if final:
            dt += [Tok("dma", sem=self.dsem["pool"][i], val=self.dlast["pool"][i]) for i in range(self.NDMA)
                   if self.dlast["pool"][i] > 0]
        for e in ("pe", "act", "dve", "sp"):
            for t in toks:
                if t.eng != e:
                    self._wait(e, t)
            for t in dt:
                self._wait(e, t)


class Phase:
    def __init__(self, P):
        self.P = P
        self.es = ExitStack()

    def __enter__(self):
        self.es.__enter__()
        return self

    def sbuf(self, name, shape, dt):
        self.P.uid += 1
        return Buf(self.es.enter_context(self.P.nc.sbuf_tensor("sp_%s_%d" % (name, self.P.uid), list(shape), dt)), name)

    def __exit__(self, *a):
        if a[0] is None:
            self.P.barrier()
        return self.es.__exit__(*a)


class Ring:
    def __init__(self, P, nslots, elems, tag=""):
        self.P = P
        self.slots = [P.sbuf("wring%s%d" % (tag, i), [128, elems], BF16) for i in range(nslots)]
        self.i = 0

    def load(self, src, a, b):
        s = self.slots[self.i % len(self.slots)]
        self.i += 1
        view = s.t[:, 0:a * b].rearrange("p (a b) -> p a b", a=a)
        self.P.dma("pool", view, src, writes=[s])
        return s, view


def build_program(debug=(), stop_after=None):
    nc = bass.Bass("TRN2", target_bir_lowering=False)
    dbg = {}
    with ExitStack() as es:
        P = Prog(nc, es)
        P.init_psum()
        IN = "ExternalInput"
        xT = P.dram("xT", [NCH, 128, NT], F32, kind=IN)
        ccd = P.dram("cc", [128, NCH, 2], F32, kind=IN)
        pard = P.dram("par", [DEPTH, 128, NPAR], F32, kind=IN)
        rgwd = P.dram("rgw", [DEPTH, 128, 8, 4, 128], F32, kind=IN)
        cosd = P.dram("cosT", [128, SEQ], F32, kind=IN)
        sind = P.dram("sinT", [128, SEQ], F32, kind=IN)
        cstd = P.dram("cst", [128, 3, 128], F32, kind=IN)
        cvdd = P.dram("cvd", [DEPTH, 128, 8, 31, 128], F32, kind=IN)
        ada_w = P.dram("ada_w", [DEPTH, D, 9 * D], F32, kind=IN)
        w_gate = P.dram("ffn_w_gate", [DEPTH, 2, D, DFF], F32, kind=IN)
        w_up = P.dram("ffn_w_up", [DEPTH, 2, D, DFF], F32, kind=IN)
        w_down = P.dram("ffn_w_down", [DEPTH, 2, DFF, D], F32, kind=IN)
        w_in = P.dram("w_in", [DEPTH, D, DIN], F32, kind=IN)
        rnn_wo = P.dram("rnn_w_out", [DEPTH, 1024, D], F32, kind=IN)
        cv_wo = P.dram("cv_w_out", [DEPTH, 1024, D], F32, kind=IN)
        da_wo = P.dram("da_w_o", [DEPTH, 1024, D], F32, kind=IN)
        w_out = P.dram("w_out", [DEPTH, D, D], F32, kind=IN)
        yT = P.dram("yT", [NCH, 128, SEQ], F32, kind="ExternalOutput")
        xs = P.dram("xs", [NCH, 128, NT], F32)
        mscr = P.dram("mscr", [3, 8, 128, NT], BF16, kind="ExternalOutput" if "m" in debug else "Internal")
        zsig = P.dram("zsig", [48, 128, NT], F32)
        cvo = P.dram("cvo", [8, 128, NT], F32)

        def dbgx(name):
            if name in debug:
                t = P.dram("dbg_" + name, [NCH, 128, NT], F32, kind="ExternalOutput")
                P.dma("sp", t.t.ap(), xs.t.ap(), reads=[xs], writes=[t])
                dbg[name] = t

        xsv = xs.t.ap().rearrange("c p t -> p c t")

        ring = Ring(P, 5, 6144)
        par = P.sbuf("par", [128, DEPTH, NPAR], F32)
        cst = P.sbuf("cst", [128, 3, 128], F32)
        onesb = P.sbuf("onesb", [128, 128], BF16)
        modr = P.sbuf("modr", [128, DEPTH, 144, 2], F32)
        modA = P.sbuf("modA", [128, DEPTH, 3, NCH, 2], F32)
        modG = P.sbuf("modG", [128, DEPTH, 3, NCH, 2], F32)
        cneg = P.sbuf("cneg", [128, DEPTH, 2, 16], F32)
        lamv = P.sbuf("lamv", [128, DEPTH, 4], F32)
        P.dma("sp", par[:], pard.t.ap().rearrange("l p n -> p l n"), writes=[par])
        P.dma("sp", cst[:], cstd.t.ap(), writes=[cst])
        P.op("dve", lambda: nc.vector.memset(onesb[:], 1.0), writes=[onesb])
        ones128b = P.sbuf("ones128b", [128, 128], BF16)
        P.op("dve", lambda: nc.vector.memset(ones128b[:], 1.0 / 128.0), writes=[ones128b])
        ones2048b = P.sbuf("ones2048b", [128, 128], BF16)
        P.op("dve", lambda: nc.vector.memset(ones2048b[:], 1.0 / 2048.0), writes=[ones2048b])
        epsc = P.sbuf("epsc", [128, 2], F32)
        P.op("dve", lambda: nc.vector.memset(epsc[:], EPS), writes=[epsc])
        ones2048 = cst[:, 0, :]
        ones128 = cst[:, 1, :]
        rperm = cst[:, 2, :]
        ones1024 = None

        def pr(l, name, i0=0, n=1):
            o = _po[name] + i0
            return par[:, l, o:o + n]

        for c in range(NCH):
            P.dma("sp", xs.t.ap()[c], xT.t.ap()[c], writes=[(xs, c)])
        P.barrier()

        scb = P.sbuf("scb", [128, NCH, 2], BF16)
        with Phase(P) as ph:
            ccs = ph.sbuf("ccs", [128, NCH, 2], F32)
            tmp = ph.sbuf("tmpm", [128, 64], F32)
            tmp2 = ph.sbuf("tmpm2", [128, 64], F32)
            P.dma("sp", ccs[:], ccd.t.ap(), writes=[ccs])
            P.op("act", lambda: nc.scalar.activation(scb[:], ccs[:], AF.Silu), reads=[ccs], writes=[scb])
            for l in range(DEPTH):
                lam_init = 0.8 - 0.6 * math.exp(-0.3 * l)
                P.op("act", lambda: nc.scalar.activation(tmp[:, 0:16], pr(l, "rglam", 0, 16), AF.Exp, scale=-1.0),
                     reads=[par], writes=[tmp])
                P.op("act", lambda: nc.scalar.activation(tmp2[:, 0:16], tmp[:, 0:16], AF.Ln, bias=1.0, scale=1.0),
                     reads=[tmp], writes=[tmp2])
                P.op("dve", lambda: nc.vector.tensor_scalar_mul(cneg[:, l, 0, :], tmp2[:, 0:16], -8.0),
                     reads=[tmp2], writes=[(cneg, (l, 0))])
                P.op("dve", lambda: nc.vector.tensor_scalar_mul(cneg[:, l, 1, :], tmp2[:, 0:16], -16.0),
                     reads=[tmp2], writes=[(cneg, (l, 1))])
                dl = _po["dalam"]
                P.op("dve", lambda: nc.vector.tensor_tensor(tmp[:, 0:64], par[:, l, dl:dl + 64], par[:, l, dl + 64:dl + 128],
                                                            ALU.mult), reads=[par, tmp2], writes=[tmp])
                P.op("dve", lambda: nc.vector.reduce_sum(tmp2[:, 0:1], tmp[:, 0:64], AX.X), reads=[tmp], writes=[tmp2])
                P.op("dve", lambda: nc.vector.tensor_tensor(tmp[:, 0:64], par[:, l, dl + 128:dl + 192],
                                                            par[:, l, dl + 192:dl + 256], ALU.mult),
                     reads=[par, tmp2], writes=[tmp])
                P.op("dve", lambda: nc.vector.reduce_sum(tmp2[:, 1:2], tmp[:, 0:64], AX.X), reads=[tmp], writes=[tmp2])
                P.op("act", lambda: nc.scalar.activation(tmp[:, 0:2], tmp2[:, 0:2], AF.Exp), reads=[tmp2], writes=[tmp])
                P.op("dve", lambda: nc.vector.scalar_tensor_tensor(lamv[:, l, 0:1], tmp[:, 1:2], -lam_init, tmp[:, 0:1],
                                                                   ALU.add, ALU.subtract), reads=[tmp], writes=[(lamv, (l, 0))])
                P.op("dve", lambda: nc.vector.tensor_scalar_mul(lamv[:, l, 1:2], pr(l, "subg"), 1.0 - lam_init),
                     reads=[par], writes=[(lamv, (l, 1))])

        mps = P.psb[7]

        def ada_part(l, k):
            awv = ada_w.t.ap()[l].rearrange("(kc p) n -> p kc n", p=128)
            def mods_mm(nb, ws, wv):
                for j in range(2):
                    n = nb * 2 + j
                    P.group("pe", [
                        (lambda kc=kc: nc.tensor.matmul(mps[:, 2 * n:2 * n + 2], wv[:, kc, j * 128:(j + 1) * 128],
                                                        scb[:, kc, :], start=(kc == 0), stop=(kc == NCH - 1)))
                        for kc in range(NCH)], reads=[ws, scb], writes=[(mps, k)])
            prev = None
            for nb in range(24 * k, 24 * (k + 1)):
                ws, wv = ring.load(awv[:, :, nb * 256:(nb + 1) * 256], NCH, 256)
                if prev is not None:
                    mods_mm(*prev)
                prev = (nb, ws, wv)
                yield
            mods_mm(*prev)
            mv = mps[:, 0:288].rearrange("p (n j) -> p n j", j=2)
            for j in range(2):
                P.op("dve", lambda: nc.vector.tensor_tensor(modr[:, l, 48 * k:48 * (k + 1), j], mv[:, 48 * k:48 * (k + 1), j],
                                                            pr(l, "adab", 48 * k, 48), ALU.add),
                     reads=[(mps, k), par], writes=[(modr, (l, k))])
            for j in range(2):
                P.op("dve", lambda: nc.vector.scalar_tensor_tensor(
                    modA[:, l, k, :, j], modr[:, l, (3 * k + 1) * 16:(3 * k + 2) * 16, j], 1.0,
                    pr(l, "ng", k * 16, 16), ALU.add, ALU.mult), reads=[(modr, (l, k)), par], writes=[(modA, (l, k))])
                P.op("dve", lambda: nc.vector.tensor_scalar_mul(
                    modG[:, l, k, :, j], modr[:, l, (3 * k + 2) * 16:(3 * k + 3) * 16, j], 1.0 if k == 1 else 0.5),
                    reads=[(modr, (l, k))], writes=[(modG, (l, k))])

        pending = []

        def ada_step():
            while pending:
                try:
                    next(pending[0])
                    return
                except StopIteration:
                    pending.pop(0)

        def ada_drain():
            while pending:
                ada_step()

        pending.append(ada_part(0, 0))
        ada_drain()

        def shiftp(l, k, c, j):
            return modr[:, l, (3 * k) * 16 + c, j:j + 1]

        def rsqrt_eps(dst, src, n):
            P.op("act", lambda: nc.scalar.activation(dst[:, 0:n], src[:, 0:n], AF.Ln, bias=epsc[:, 0:1], scale=1.0),
                 reads=[src, epsc], writes=[dst])
            P.op("act", lambda: nc.scalar.activation(dst[:, 0:n], dst[:, 0:n], AF.Exp, scale=-0.5), reads=[dst], writes=[dst])

        def norm_tile(ph, xt, sqb, rstd, start, n, consume, gains):
            P.dma("sp", xt[:, :, 0:n], xsv[:, :, start:start + n], reads=[xs], writes=[xt])
            sp_ = P.ps()
            for c in range(NCH):
                q = sqb[c % 2]
                P.op("act", lambda: nc.scalar.activation(q[:, 0:n], xt[:, c, 0:n], AF.Square), reads=[xt], writes=[q])
                P.op("pe", lambda: nc.tensor.matmul(sp_[:, 0:n], ones2048b[:], q[:, 0:n], start=(c == 0), stop=(c == NCH - 1)),
                     reads=[q, ones2048b], writes=[sp_])
            rsqrt_eps(rstd, sp_, n)
            for c in range(NCH):
                consume(c)

        def subtiles(with_ctx):
            if with_ctx:
                return [[(0, 256, 1), (256, 512, 0), (768, 512, 0)], [(1280, 512, 0), (1792, 512, 0)]]
            return [[(256, 512, 0), (768, 512, 0)], [(1280, 512, 0), (1792, 512, 0)]]

        def ffn(l, k, with_ctx):
            kn = 0 if k == 0 else 2
            KN = kn
            wgv = w_gate.t.ap()[l, k].rearrange("(kc p) n -> p kc n", p=128)
            wuv = w_up.t.ap()[l, k].rearrange("(kc p) n -> p kc n", p=128)
            wdv = w_down.t.ap()[l, k].rearrange("(f p) n -> p f n", p=128)
            for ST in subtiles(with_ctx):
                T = sum(s[1] for s in ST)
                offs = []
                o = 0
                for s in ST:
                    offs.append(o)
                    o += s[1]
                with Phase(P) as ph:
                    hT = ph.sbuf("hT", [128, NCH, T], BF16)
                    act = ph.sbuf("act", [128, 22, T], BF16)
                    xt = ph.sbuf("xt", [128, NCH, 256], F32)
                    sqb = [ph.sbuf("sq%d" % i, [128, 512], BF16) for i in range(2)]
                    tmpb = [ph.sbuf("tm%d" % i, [128, 512], F32) for i in range(2)]
                    rstd = ph.sbuf("rstd", [128, 512], F32)
                    xr = [ph.sbuf("xr%d" % i, [128, 512], F32) for i in range(2)]
                    xo = [ph.sbuf("xo%d" % i, [128, 512], F32) for i in range(2)]
                    cnt = [0]
                    halves = []
                    for si, (start, n, isc) in enumerate(ST):
                        for h0 in range(0, n, 256):
                            halves.append((si, start + h0, min(256, n - h0), isc, offs[si] + h0))
                    for (si, start, n, isc, off) in halves:

                        def consume(c, start=start, n=n, isc=isc, off=off, si=si):
                            tb = tmpb[c % 2]
                            P.op("dve", lambda: nc.vector.scalar_tensor_tensor(
                                tb[:, 0:n], xt[:, c, 0:n], modA[:, l, kn, c, isc:isc + 1], rstd[:, 0:n], ALU.mult, ALU.mult),
                                reads=[xt, rstd, (modA, (l, KN))], writes=[tb])
                            P.op("act", lambda: nc.scalar.activation(hT[:, c, off:off + n], tb[:, 0:n], AF.Identity,
                                                                     bias=shiftp(l, kn, c, isc), scale=1.0),
                                 reads=[tb, (modr, (l, KN))], writes=[(hT, si)])
                        norm_tile(ph, xt, sqb, rstd, start, n, consume, None)
                    for half in range(2):
                        for fp in range(11):
                            c0 = (half * 22 + fp * 2) * 128
                            gs, gv = ring.load(wgv[:, :, c0:c0 + 256], NCH, 256)
                            us, uv = ring.load(wuv[:, :, c0:c0 + 256], NCH, 256)
                            ada_step()
                            for j in range(2):
                                f = fp * 2 + j
                                for si, (start, n, isc) in enumerate(ST):
                                    off = offs[si]
                                    pg = P.ps()
                                    pu = P.ps()
                                    P.group("pe", [(lambda kc=kc: nc.tensor.matmul(
                                        pg[:, 0:n], gv[:, kc, j * 128:(j + 1) * 128], hT[:, kc, off:off + n],
                                        start=(kc == 0), stop=(kc == NCH - 1))) for kc in range(NCH)],
                                        reads=[gs, (hT, si)], writes=[pg])
                                    P.group("pe", [(lambda kc=kc: nc.tensor.matmul(
                                        pu[:, 0:n], uv[:, kc, j * 128:(j + 1) * 128], hT[:, kc, off:off + n],
                                        start=(kc == 0), stop=(kc == NCH - 1))) for kc in range(NCH)],
                                        reads=[us, (hT, si)], writes=[pu])
                                    tb = tmpb[cnt[0] % 2]
                                    cnt[0] += 1
                                    P.op("act", lambda: nc.scalar.activation(tb[:, 0:n], pg[:, 0:n], AF.Silu),
                                         reads=[pg], writes=[tb])
                                    P.op("dve", lambda: nc.vector.tensor_tensor(act[:, f, off:off + n], tb[:, 0:n], pu[:, 0:n],
                                                                                ALU.mult),
                                         reads=[tb, pu], writes=[(act, (f, si))])
                        for dp in range(8):
                            ds_, dv = ring.load(wdv[:, half * 22:(half + 1) * 22, dp * 256:(dp + 1) * 256], 22, 256)
                            ada_step()
                            for j in range(2):
                                d = dp * 2 + j
                                for si, (start, n, isc) in enumerate(ST):
                                    off = offs[si]
                                    pd = P.ps()
                                    P.group("pe", [(lambda f=f: nc.tensor.matmul(
                                        pd[:, 0:n], dv[:, f, j * 128:(j + 1) * 128], act[:, f, off:off + n],
                                        start=(f == 0), stop=(f == 21))) for f in range(22)],
                                        reads=[ds_] + [(act, (f, si)) for f in range(22)], writes=[pd])
                                    b = cnt[0] % 2
                                    cnt[0] += 1
                                    P.dma("sp", xr[b][:, 0:n], xs.t.ap()[d, :, start:start + n], reads=[(xs, (d, start))],
                                          writes=[xr[b]])
                                    P.op("dve", lambda: nc.vector.scalar_tensor_tensor(
                                        xo[b][:, 0:n], pd[:, 0:n], modG[:, l, kn, d, isc:isc + 1], xr[b][:, 0:n],
                                        ALU.mult, ALU.add), reads=[pd, xr[b], (modG, (l, kn))], writes=[xo[b]])
                                    P.dma("sp", xs.t.ap()[d, :, start:start + n], xo[b][:, 0:n], reads=[xo[b]],
                                          writes=[(xs, (d, start))])

        def proj(ws, wv, j0, hT, seg, si):
            start, n, isc = seg
            p_ = P.ps()
            P.group("pe", [(lambda kc=kc: nc.tensor.matmul(p_[:, 0:n], wv[:, kc, j0:j0 + 128], hT[:, kc, start:start + n],
                                                           start=(kc == 0), stop=(kc == NCH - 1))) for kc in range(NCH)],
                    reads=[ws, (hT, si)], writes=[p_])
            return p_

        def mixer(l, need_ctx):
            KN = 1
            lam_init = 0.8 - 0.6 * math.exp(-0.3 * l)
            winv = w_in.t.ap()[l].rearrange("(kc p) n -> p kc n", p=128)
            mv = mscr.t.ap()
            with Phase(P) as phh:
                hT = phh.sbuf("hmix", [128, NCH, NT], BF16)
                with Phase(P) as ph:
                    xt = ph.sbuf("xt", [128, NCH, 512], F32)
                    sqb = [ph.sbuf("sq%d" % i, [128, 512], BF16) for i in range(2)]
                    tmpb = [ph.sbuf("tm%d" % i, [128, 512], F32) for i in range(2)]
                    rstd = ph.sbuf("rstd", [128, 512], F32)
                    for si, (start, n, isc) in enumerate(SEGS):
                        def consume(c, start=start, n=n, isc=isc, si=si):
                            tb = tmpb[c % 2]
                            P.op("dve", lambda: nc.vector.scalar_tensor_tensor(
                                tb[:, 0:n], xt[:, c, 0:n], modA[:, l, 1, c, isc:isc + 1], rstd[:, 0:n], ALU.mult, ALU.mult),
                                reads=[xt, rstd, (modA, (l, KN))], writes=[tb])
                            P.op("act", lambda: nc.scalar.activation(hT[:, c, start:start + n], tb[:, 0:n], AF.Identity,
                                                                     bias=shiftp(l, 1, c, isc), scale=1.0),
                                 reads=[tb, (modr, (l, KN))], writes=[(hT, si)])
                        norm_tile(ph, xt, sqb, rstd, start, n, consume, None)
                if stop_after == "h":
                    return
                with Phase(P) as ph:
                    zb = [ph.sbuf("zb%d" % i, [128, 512], F32) for i in range(3)]
                    cnt = 0
                    for zp in range(24):
                        ws, wv = ring.load(winv[:, :, OFF_Z + zp * 256:OFF_Z + (zp + 1) * 256], NCH, 256)
                        for j in range(2):
                            zc = zp * 2 + j
                            for si, seg in enumerate(SEGS):
                                start, n, isc = seg
                                if isc and not need_ctx:
                                    continue
                                p_ = proj(ws, wv, j * 128, hT, seg, si)
                                b = zb[cnt % 3]
                                cnt += 1
                                P.op("act", lambda: nc.scalar.activation(b[:, 0:n], p_[:, 0:n], AF.Sigmoid), reads=[p_], writes=[b])
                                P.dma("sp", zsig.t.ap()[zc, :, start:start + n], b[:, 0:n], reads=[b], writes=[(zsig, (zc, si))])
                with Phase(P) as ph:
                    xp = ph.sbuf("xp", [128, PL], F32)
                    u = ph.sbuf("u", [128, PL], F32)
                    ub = ph.sbuf("ub", [128, PL], BF16)
                    hf = ph.sbuf("hf", [128, PL], F32)
                    hb = ph.sbuf("hb", [128, PL], F32)
                    RB = ph.sbuf("RB", [128, NT], F32)
                    IB = ph.sbuf("IB", [128, NT], F32)
                    gy = [ph.sbuf("gy0", [128, 512], F32)] * 2
                    mo = [ph.sbuf("mo0", [128, 512], BF16)] * 2
                    P.op("dve", lambda: nc.vector.memset(xp[:], 0.0), writes=[xp])
                    rgv = rgwd.t.ap()[l]
                    for c in range(8):
                        wxs, wxv = ring.load(winv[:, :, OFF_X + c * 128:OFF_X + (c + 1) * 128], NCH, 128)
                        wys, wyv = ring.load(winv[:, :, OFF_Y + c * 128:OFF_Y + (c + 1) * 128], NCH, 128)
                        rgs, rgt = ring.load(rgv[:, c], 4, 128)
                        for si, seg in enumerate(SEGS):
                            start, n, isc = seg
                            p_ = proj(wxs, wxv, 0, hT, seg, si)
                            pp = ppos(start)
                            P.op("act", lambda: nc.scalar.copy(xp[:, pp:pp + n], p_[:, 0:n]), reads=[p_], writes=[(xp, si)])
                        lo, hi = PAD, PL - PAD
                        P.op("dve", lambda: nc.vector.tensor_scalar(u[:, lo:hi], xp[:, lo - 1:hi - 1], pr(l, "rcw", c * 4, 1),
                                                                    pr(l, "rcb", c, 1), ALU.mult, ALU.add),
                             reads=[xp, par], writes=[u])
                        for j in range(1, 4):
                            P.op("dve", lambda: nc.vector.scalar_tensor_tensor(
                                u[:, lo:hi], xp[:, lo + j - 1:hi + j - 1], pr(l, "rcw", c * 4 + j, 1), u[:, lo:hi],
                                ALU.mult, ALU.add), reads=[xp, par, u], writes=[u])
                        P.op("act", lambda: nc.scalar.copy(ub[:, lo:hi], u[:, lo:hi]), reads=[u], writes=[ub])
                        for d in range(2):
                            hbuf = hf if d == 0 else hb
                            order = list(range(5)) if d == 0 else [0, 4, 3, 2, 1]
                            for si in range(5):
                                start, n, isc = SEGS[si]
                                pp = ppos(start)
                                pr_ = P.ps()
                                pi_ = P.ps()
                                P.op("pe", lambda: nc.tensor.matmul(pr_[:, 0:n], rgt[:, d * 2 + 0, :], ub[:, pp:pp + n],
                                                                    start=True, stop=True), reads=[rgs, ub], writes=[pr_])
                                P.op("pe", lambda: nc.tensor.matmul(pi_[:, 0:n], rgt[:, d * 2 + 1, :], ub[:, pp:pp + n],
                                                                    start=True, stop=True), reads=[rgs, ub], writes=[pi_])
                                P.op("act", lambda: nc.scalar.activation(RB[:, start:start + n], pr_[:, 0:n], AF.Sigmoid,
                                                                         bias=pr(l, "rgbr", d * 8 + c, 1), scale=1.0),
                                     reads=[pr_, par], writes=[(RB, si)])
                                P.op("act", lambda: nc.scalar.activation(IB[:, start:start + n], pi_[:, 0:n], AF.Sigmoid,
                                                                         bias=pr(l, "rgbi", d * 8 + c, 1), scale=1.0),
                                     reads=[pi_, par], writes=[(IB, si)])
                            for si in range(5):
                                start, n, isc = SEGS[si]
                                P.op("act", lambda: nc.scalar.activation(xp[:, ppos(start):ppos(start) + n], RB[:, start:start + n], AF.Exp,
                                                                         scale=cneg[:, l, 1, d * 8 + c:d * 8 + c + 1]),
                                     reads=[(RB, si), cneg], writes=[(xp, si)])
                                P.op("act", lambda: nc.scalar.activation(RB[:, start:start + n], RB[:, start:start + n], AF.Exp,
                                                                         scale=cneg[:, l, 0, d * 8 + c:d * 8 + c + 1]),
                                     reads=[(RB, si), cneg], writes=[(RB, si)])
                            for si in range(5):
                                start, n, isc = SEGS[si]
                                P.op("act", lambda: nc.scalar.activation(xp[:, ppos(start):ppos(start) + n], xp[:, ppos(start):ppos(start) + n], AF.Sqrt,
                                                                         bias=1.0, scale=-1.0),
                                     reads=[(xp, si)], writes=[(xp, si)])
                            prev = None
                            for si in order:
                                start, n, isc = SEGS[si]
                                pp = ppos(start)
                                P.op("dve", lambda: nc.vector.tensor_tensor(IB[:, start:start + n], IB[:, start:start + n],
                                                                            xp[:, ppos(start):ppos(start) + n], ALU.mult),
                                     reads=[(IB, si), (xp, si)], writes=[(IB, si)])
                                P.op("dve", lambda: nc.vector.tensor_tensor(IB[:, start:start + n], IB[:, start:start + n],
                                                                            u[:, pp:pp + n], ALU.mult),
                                     reads=[(IB, si), u], writes=[(IB, si)])
                                if prev is None:
                                    init = 0.0
                                else:
                                    init = hbuf[:, prev:prev + 1]
                                if d == 0:
                                    P.op("dve", lambda: nc.vector.tensor_tensor_scan(hbuf[:, pp:pp + n], RB[:, start:start + n],
                                                                                     IB[:, start:start + n], init, ALU.mult, ALU.add),
                                         reads=[(RB, si), (IB, si), hbuf], writes=[hbuf])
                                    prev = pp + n - 1
                                else:
                                    P.op("dve", lambda: nc.vector.tensor_tensor_scan(
                                        hbuf[:, pp:pp + n][:, ::-1], RB[:, start:start + n][:, ::-1], IB[:, start:start + n][:, ::-1],
                                        init, ALU.mult, ALU.add), reads=[(RB, si), (IB, si), hbuf], writes=[hbuf])
                                    prev = pp
                        for si, seg in enumerate(SEGS):
                            start, n, isc = seg
                            if isc and not need_ctx:
                                continue
                            pp = ppos(start)
                            p_ = proj(wys, wyv, 0, hT, seg, si)
                            b = si % 2
                            P.op("act", lambda: nc.scalar.activation(gy[b][:, 0:n], p_[:, 0:n], AF.Gelu_apprx_tanh),
                                 reads=[p_], writes=[gy[b]])
                            P.op("dve", lambda: nc.vector.tensor_tensor(hf[:, pp:pp + n], hf[:, pp:pp + n], hb[:, pp:pp + n], ALU.add),
                                 reads=[hf, hb], writes=[hf])
                            P.op("dve", lambda: nc.vector.tensor_tensor(mo[b][:, 0:n], gy[b][:, 0:n], hf[:, pp:pp + n], ALU.mult),
                                 reads=[gy[b], hf], writes=[mo[b]])
                            P.dma("sp", mv[0, c, :, start:start + n], mo[b][:, 0:n], reads=[mo[b]], writes=[(mscr, (0, c, si))])
                if stop_after == "rnn":
                    return
                with Phase(P) as ph:
                    gpb = [ph.sbuf("gpb%d" % i, [128, PL], BF16) for i in range(2)]
                    sg = [ph.sbuf("sg%d" % i, [128, 512], F32) for i in range(2)]
                    co = [ph.sbuf("co%d" % i, [128, 512], F32) for i in range(2)]
                    for g_ in gpb:
                        P.op("dve", lambda: nc.vector.memset(g_[:], 0.0), writes=[g_])
                    cnt = 0
                    for c in range(8):
                        gp = gpb[c % 2]
                        was, wav = ring.load(winv[:, :, OFF_G + c * 128:OFF_G + (c + 1) * 128], NCH, 128)
                        wgs, wgv_ = ring.load(winv[:, :, OFF_G + 1024 + c * 128:OFF_G + 1024 + (c + 1) * 128], NCH, 128)
                        dgs, dgv = ring.load(cvdd.t.ap()[l][:, c], 31, 128)
                        for si, seg in enumerate(SEGS):
                            start, n, isc = seg
                            pp = ppos(start)
                            pa = proj(was, wav, 0, hT, seg, si)
                            pg = proj(wgs, wgv_, 0, hT, seg, si)
                            b = si % 2
                            P.op("act", lambda: nc.scalar.activation(sg[b][:, 0:n], pg[:, 0:n], AF.Sigmoid), reads=[pg], writes=[sg[b]])
                            P.op("dve", lambda: nc.vector.tensor_tensor(gp[:, pp:pp + n], sg[b][:, 0:n], pa[:, 0:n], ALU.mult),
                                 reads=[sg[b], pa], writes=[(gp, si)])
                        for si, seg in enumerate(SEGS):
                            start, n, isc = seg
                            if isc and not need_ctx:
                                continue
                            pp = ppos(start)
                            pc = P.ps()
                            P.group("pe", [(lambda j=j: nc.tensor.matmul(pc[:, 0:n], dgv[:, j, :], gp[:, pp + j - 15:pp + j - 15 + n],
                                                                         start=(j == 0), stop=(j == 30))) for j in range(31)],
                                    reads=[dgs, gp], writes=[pc])
                            b = cnt % 2
                            cnt += 1
                            P.op("act", lambda: nc.scalar.activation(co[b][:, 0:n], pc[:, 0:n], AF.Identity,
                                                                     bias=pr(l, "cvb", c, 1), scale=1.0),
                                 reads=[pc, par], writes=[co[b]])
                            P.dma("sp", cvo.t.ap()[c, :, start:start + n], co[b][:, 0:n], reads=[co[b]], writes=[(cvo, (c, si))])
                with Phase(P) as ph:
                    ct = ph.sbuf("ct", [128, 8, 512], F32)
                    sqb = [ph.sbuf("sq%d" % i, [128, 512], F32) for i in range(2)]
                    mean = ph.sbuf("mean", [128, 512], F32)
                    rstd = ph.sbuf("rstd", [128, 512], F32)
                    tb = [ph.sbuf("tb%d" % i, [128, 512], F32) for i in range(2)]
                    mo = [ph.sbuf("mo%d" % i, [128, 512], BF16) for i in range(2)]
                    cvv = cvo.t.ap().rearrange("c p t -> p c t")
                    for si, seg in enumerate(SEGS):
                        start, n, isc = seg
                        if isc and not need_ctx:
                            continue
                        P.dma("sp", ct[:, :, 0:n], cvv[:, :, start:start + n], reads=[cvo], writes=[ct])
                        pm = P.ps()
                        pq = P.ps()
                        for c in range(8):
                            q = sqb[c % 2]
                            P.op("act", lambda: nc.scalar.activation(q[:, 0:n], ct[:, c, 0:n], AF.Square), reads=[ct], writes=[q])
                            P.op("pe", lambda: nc.tensor.matmul(pm[:, 0:n], ones128, ct[:, c, 0:n], start=(c == 0), stop=(c == 7)),
                                 reads=[ct, cst], writes=[pm])
                            P.op("pe", lambda: nc.tensor.matmul(pq[:, 0:n], ones128, q[:, 0:n], start=(c == 0), stop=(c == 7)),
                                 reads=[q, cst], writes=[pq])
                        P.op("act", lambda: nc.scalar.mul(mean[:, 0:n], pm[:, 0:n], 0.125), reads=[pm], writes=[mean])
                        P.op("dve", lambda: nc.vector.tensor_tensor(rstd[:, 0:n], mean[:, 0:n], mean[:, 0:n], ALU.mult),
                             reads=[mean], writes=[rstd])
                        P.op("dve", lambda: nc.vector.scalar_tensor_tensor(rstd[:, 0:n], pq[:, 0:n], 0.125, rstd[:, 0:n],
                                                                           ALU.mult, ALU.subtract), reads=[pq, rstd], writes=[rstd])
                        rsqrt_eps(rstd, rstd, n)
                        for c in range(8):
                            b = c % 2
                            P.op("dve", lambda: nc.vector.tensor_tensor(tb[b][:, 0:n], ct[:, c, 0:n], mean[:, 0:n], ALU.subtract),
                                 reads=[ct, mean], writes=[tb[b]])
                            P.op("dve", lambda: nc.vector.tensor_tensor(tb[b][:, 0:n], tb[b][:, 0:n], rstd[:, 0:n], ALU.mult),
                                 reads=[tb[b], rstd], writes=[tb[b]])
                            P.op("act", lambda: nc.scalar.activation(mo[b][:, 0:n], tb[b][:, 0:n], AF.Silu,
                                                                     bias=pr(l, "cvbb", c, 1), scale=pr(l, "cvg", c, 1)),
                                 reads=[tb[b], par], writes=[mo[b]])
                            P.dma("sp", mv[1, c, :, start:start + n], mo[b][:, 0:n], reads=[mo[b]], writes=[(mscr, (1, c, si))])
                if stop_after == "conv":
                    return
                with Phase(P) as ph:
                    QT = ph.sbuf("QT", [128, NT], BF16)
                    KT = ph.sbuf("KT", [128, NT], BF16)
                    Vp = ph.sbuf("Vp", [128, 18, 256], BF16)
                    cs = [ph.sbuf("cs%d" % i, [128, 2, 512], F32) for i in range(2)]
                    qf = [ph.sbuf("qf%d" % i, [128, 512], F32) for i in range(2)]
                    t1 = [ph.sbuf("t1%d" % i, [128, 512], F32) for i in range(2)]
                    t2 = [ph.sbuf("t2%d" % i, [128, 512], F32) for i in range(2)]
                    Pt = [ph.sbuf("Pt%d" % i, [128, 512], BF16) for i in range(4)]
                    rz = [ph.sbuf("rz%d" % i, [128, 512], F32) for i in range(2)]
                    ob = [ph.sbuf("ob%d" % i, [128, 512], F32) for i in range(2)]
                    osq = ph.sbuf("osq", [128, 512], BF16)
                    orr = ph.sbuf("orr", [128, 512], F32)
                    mo = [ph.sbuf("mo%d" % i, [128, 512], BF16) for i in range(2)]
                    neglam = lamv[:, l, 0:1]
                    gsub = lamv[:, l, 1:2]
                    cnt = 0
                    for h in range(8):
                        if h % 2 == 0:
                            wvs, wvv = ring.load(winv[:, :, OFF_V + h * 128:OFF_V + (h + 2) * 128], NCH, 256)
                            for tc in range(18):
                                si = 0 if tc < 2 else 1 + (tc - 2) // 4
                                p_ = P.ps()
                                P.group("pe", [(lambda kc=kc: nc.tensor.matmul(p_[:, 0:256], hT[:, kc, tc * 128:(tc + 1) * 128],
                                                                               wvv[:, kc, :], start=(kc == 0), stop=(kc == NCH - 1)))
                                               for kc in range(NCH)], reads=[wvs, (hT, si)], writes=[p_])
                                if tc % 2 == 0:
                                    P.op("act", lambda: nc.scalar.copy(Vp[:, tc, :], p_[:, 0:256]), reads=[p_], writes=[(Vp, tc)])
                                else:
                                    P.op("dve", lambda: nc.vector.tensor_copy(Vp[:, tc, :], p_[:, 0:256]), reads=[p_], writes=[(Vp, tc)])
                        wqs, wqv = ring.load(winv[:, :, OFF_Q + h * 128:OFF_Q + (h + 1) * 128], NCH, 128)
                        wks, wkv = ring.load(winv[:, :, OFF_K + h * 128:OFF_K + (h + 1) * 128], NCH, 128)
                        for si, seg in enumerate(SEGS):
                            start, n, isc = seg
                            if isc:
                                if need_ctx:
                                    p_ = proj(wqs, wqv, 0, hT, seg, si)
                                    P.op("act", lambda: nc.scalar.copy(QT[:, start:start + n], p_[:, 0:n]), reads=[p_], writes=[(QT, si)])
                                p_ = proj(wks, wkv, 0, hT, seg, si)
                                P.op("act", lambda: nc.scalar.copy(KT[:, start:start + n], p_[:, 0:n]), reads=[p_], writes=[(KT, si)])
                                continue
                            cb = cs[si % 2]
                            P.dma("sp", cb[:, 0, :], cosd.t.ap()[:, start - CTX:start - CTX + n], writes=[cb])
                            P.dma("sp", cb[:, 1, :], sind.t.ap()[:, start - CTX:start - CTX + n], writes=[cb])
                            for (ws_, wv_, dst) in ((wqs, wqv, QT), (wks, wkv, KT)):
                                p_ = proj(ws_, wv_, 0, hT, seg, si)
                                b = cnt % 2
                                cnt += 1
                                P.op("act", lambda: nc.scalar.copy(qf[b][:, 0:n], p_[:, 0:n]), reads=[p_], writes=[qf[b]])
                                p2 = P.ps()
                                P.op("pe", lambda: nc.tensor.matmul(p2[:, 0:n], rperm, qf[b][:, 0:n], start=True, stop=True),
                                     reads=[qf[b], cst], writes=[p2])
                                P.op("dve", lambda: nc.vector.tensor_tensor(t1[b][:, 0:n], qf[b][:, 0:n], cb[:, 0, 0:n], ALU.mult),
                                     reads=[qf[b], cb], writes=[t1[b]])
                                P.op("dve", lambda: nc.vector.tensor_tensor(t2[b][:, 0:n], p2[:, 0:n], cb[:, 1, 0:n], ALU.mult),
                                     reads=[p2, cb], writes=[t2[b]])
                                P.op("dve", lambda: nc.vector.tensor_tensor(dst[:, start:start + n], t1[b][:, 0:n], t2[b][:, 0:n], ALU.add),
                                     reads=[t1[b], t2[b]], writes=[(dst, si)])
                        hoff = (h % 2) * 128
                        for si, seg in enumerate(SEGS):
                            qs, qn, isc = seg
                            if isc and not need_ctx:
                                continue
                            keys = [0, 1] if isc else list(range(18))
                            nk = len(keys)
                            O = [P.psb[4], P.psb[5]]
                            Z = [P.psb[6], P.psb[7]]

                            def scores(kc):
                                ksi = 0 if kc < 2 else 1 + (kc - 2) // 4
                                out = []
                                for comp in range(2):
                                    sc = P.ps()
                                    P.op("pe", lambda: nc.tensor.matmul(
                                        sc[:, 0:qn], KT[comp * 64:(comp + 1) * 64, kc * 128:(kc + 1) * 128],
                                        QT[comp * 64:(comp + 1) * 64, qs:qs + qn], start=True, stop=True),
                                        reads=[(KT, ksi), (QT, si)], writes=[sc])
                                    out.append(sc)
                                return out
                            s_cur = scores(keys[0])
                            for i, kc in enumerate(keys):
                                s_next = scores(keys[i + 1]) if i + 1 < nk else None
                                for comp in range(2):
                                    pt = Pt[(i % 2) * 2 + comp]
                                    P.op("act", lambda: nc.scalar.activation(pt[:, 0:qn], s_cur[comp][:, 0:qn], AF.Exp, scale=0.125),
                                         reads=[s_cur[comp]], writes=[pt])
                                for comp in range(2):
                                    pt = Pt[(i % 2) * 2 + comp]
                                    P.op("pe", lambda: nc.tensor.matmul(O[comp][:, 0:qn], Vp[:, kc, hoff:hoff + 128], pt[:, 0:qn],
                                                                        start=(i == 0), stop=(i == nk - 1)),
                                         reads=[(Vp, kc), pt], writes=[O[comp]])
                                    P.op("pe", lambda: nc.tensor.matmul(Z[comp][:, 0:qn], onesb[:], pt[:, 0:qn],
                                                                        start=(i == 0), stop=(i == nk - 1)),
                                         reads=[onesb, pt], writes=[Z[comp]])
                                s_cur = s_next
                            for comp in range(2):
                                P.op("act", lambda: nc.scalar.activation(rz[comp][:, 0:qn], Z[comp][:, 0:qn], AF.Ln), reads=[Z[comp]], writes=[rz[comp]])
                                P.op("act", lambda: nc.scalar.activation(rz[comp][:, 0:qn], rz[comp][:, 0:qn], AF.Exp, scale=-1.0), reads=[rz[comp]], writes=[rz[comp]])
                                P.op("dve", lambda: nc.vector.tensor_tensor(ob[comp][:, 0:qn], O[comp][:, 0:qn], rz[comp][:, 0:qn], ALU.mult),
                                     reads=[O[comp], rz[comp]], writes=[ob[comp]])
                            P.op("dve", lambda: nc.vector.scalar_tensor_tensor(ob[0][:, 0:qn], ob[1][:, 0:qn], neglam, ob[0][:, 0:qn],
                                                                               ALU.mult, ALU.add), reads=[ob[0], ob[1], lamv], writes=[ob[0]])
                            P.op("act", lambda: nc.scalar.activation(osq[:, 0:qn], ob[0][:, 0:qn], AF.Square), reads=[ob[0]], writes=[osq])
                            pm = P.ps()
                            P.op("pe", lambda: nc.tensor.matmul(pm[:, 0:qn], ones128b[:], osq[:, 0:qn], start=True, stop=True),
                                 reads=[osq, ones128b], writes=[pm])
                            rsqrt_eps(orr, pm, qn)
                            P.op("dve", lambda: nc.vector.tensor_tensor(orr[:, 0:qn], orr[:, 0:qn], ob[0][:, 0:qn], ALU.mult),
                                 reads=[orr, ob[0]], writes=[orr])
                            b = si % 2
                            P.op("act", lambda: nc.scalar.activation(mo[b][:, 0:qn], orr[:, 0:qn], AF.Identity, bias=0.0, scale=gsub),
                                 reads=[orr, lamv], writes=[mo[b]])
                            P.dma("sp", mv[2, h, :, qs:qs + qn], mo[b][:, 0:qn], reads=[mo[b]], writes=[(mscr, (2, h, si))])
            if stop_after == "att":
                return
            bwv = [t.t.ap()[l].rearrange("(kc p) n -> p kc n", p=128) for t in (rnn_wo, cv_wo, da_wo)]
            wov = w_out.t.ap()[l].rearrange("(kc p) n -> p kc n", p=128)
            mvv = mscr.t.ap().rearrange("b c p t -> p b c t")
            with Phase(P) as ph:
                mt = ph.sbuf("mt", [128, 3, 8, 1024], BF16)
                mg = ph.sbuf("mg", [128, NCH, 1024], BF16)
                zt = [ph.sbuf("zt%d" % i, [128, 3, 512], F32) for i in range(2)]
                accb = [ph.sbuf("ac%d" % i, [128, 512], F32) for i in range(2)]
                tmb = [ph.sbuf("tmg%d" % i, [128, 512], F32) for i in range(2)]
                xr = [ph.sbuf("xr%d" % i, [128, 512], F32) for i in range(2)]
                xo = [ph.sbuf("xo%d" % i, [128, 512], F32) for i in range(2)]
                cnt = 0
                groups = [[0, 1], [2, 3], [4]] if need_ctx else [[1, 2], [3, 4]]
                for gi, grp in enumerate(groups):
                    goff = {}
                    o = 0
                    for si in grp:
                        goff[si] = o
                        o += SEGS[si][1]
                    for si in grp:
                        start, n, isc = SEGS[si]
                        off = goff[si]
                        for br in range(3):
                            P.dma("sp", mt[:, br, :, off:off + n], mvv[:, br, :, start:start + n], reads=[mscr],
                                  writes=[(mt, (br, si))])
                    for dp in range(8):
                        wts = []
                        for br in range(3):
                            wts.append(ring.load(bwv[br][:, :, dp * 256:(dp + 1) * 256], 8, 256))
                        for j in range(2):
                            d = dp * 2 + j
                            for si in grp:
                                start, n, isc = SEGS[si]
                                off = goff[si]
                                z_ = zt[cnt % 2]
                                a_ = accb[cnt % 2]
                                t_ = tmb[cnt % 2]
                                cnt += 1
                                for br in range(3):
                                    P.dma("sp", z_[:, br, 0:n], zsig.t.ap()[br * 16 + d, :, start:start + n], reads=[zsig],
                                          writes=[(z_, br)])
                                for br in range(3):
                                    ws, wv = wts[br]
                                    p_ = P.ps()
                                    P.group("pe", [(lambda kc=kc: nc.tensor.matmul(p_[:, 0:n], wv[:, kc, j * 128:(j + 1) * 128],
                                                                                   mt[:, br, kc, off:off + n], start=(kc == 0), stop=(kc == 7)))
                                                   for kc in range(8)], reads=[ws, (mt, (br, si))], writes=[p_])
                                    if br == 0:
                                        P.op("dve", lambda: nc.vector.tensor_tensor(a_[:, 0:n], p_[:, 0:n], z_[:, br, 0:n], ALU.mult),
                                             reads=[p_, (z_, br)], writes=[a_])
                                    else:
                                        P.op("dve", lambda: nc.vector.tensor_tensor(t_[:, 0:n], p_[:, 0:n], z_[:, br, 0:n], ALU.mult),
                                             reads=[p_, (z_, br)], writes=[t_])
                                        if br == 1:
                                            P.op("dve", lambda: nc.vector.tensor_tensor(a_[:, 0:n], a_[:, 0:n], t_[:, 0:n], ALU.add),
                                                 reads=[a_, t_], writes=[a_])
                                        else:
                                            P.op("dve", lambda: nc.vector.tensor_tensor(mg[:, d, off:off + n], a_[:, 0:n], t_[:, 0:n], ALU.add),
                                                 reads=[a_, t_], writes=[(mg, (d, si))])
                    for dp in range(8):
                        ws, wv = ring.load(wov[:, :, dp * 256:(dp + 1) * 256], NCH, 256)
                        for j in range(2):
                            d = dp * 2 + j
                            for si in grp:
                                start, n, isc = SEGS[si]
                                off = goff[si]
                                p_ = P.ps()
                                P.group("pe", [(lambda kc=kc: nc.tensor.matmul(p_[:, 0:n], wv[:, kc, j * 128:(j + 1) * 128],
                                                                               mg[:, kc, off:off + n], start=(kc == 0), stop=(kc == NCH - 1)))
                                               for kc in range(NCH)], reads=[ws] + [(mg, (kc, si)) for kc in range(NCH)], writes=[p_])
                                b = cnt % 2
                                cnt += 1
                                P.dma("sp", xr[b][:, 0:n], xs.t.ap()[d, :, start:start + n], reads=[(xs, (d, start))], writes=[xr[b]])
                                P.op("dve", lambda: nc.vector.scalar_tensor_tensor(
                                    xo[b][:, 0:n], p_[:, 0:n], modG[:, l, 1, d, isc:isc + 1], xr[b][:, 0:n], ALU.mult, ALU.add),
                                    reads=[p_, xr[b], (modG, (l, 1))], writes=[xo[b]])
                                P.dma("sp", xs.t.ap()[d, :, start:start + n], xo[b][:, 0:n], reads=[xo[b]], writes=[(xs, (d, start))])

        def final_norm():
            with Phase(P) as ph:
                xt = ph.sbuf("xt", [128, NCH, 512], F32)
                sqb = [ph.sbuf("sq%d" % i, [128, 512], BF16) for i in range(2)]
                rstd = ph.sbuf("rstd", [128, 512], F32)
                yo = [ph.sbuf("yo%d" % i, [128, 512], F32) for i in range(2)]
                for si, (start, n, isc) in enumerate(SEGS):
                    if isc:
                        continue

                    def consume(c, start=start, n=n):
                        y_ = yo[c % 2]
                        P.op("dve", lambda: nc.vector.scalar_tensor_tensor(
                            y_[:, 0:n], xt[:, c, 0:n], par[:, 0, _po["fing"] + c:_po["fing"] + c + 1], rstd[:, 0:n],
                            ALU.mult, ALU.mult), reads=[xt, rstd, par], writes=[y_])
                        P.dma("sp", yT.t.ap()[c, :, start - CTX:start - CTX + n], y_[:, 0:n], reads=[y_], writes=[(yT, (c, si))])
                    norm_tile(ph, xt, sqb, rstd, start, n, consume, None)

        def forward():
            for l in range(DEPTH):
                need_ctx = l < DEPTH - 1
                if l == 0:
                    pending.extend([ada_part(0, 1), ada_part(0, 2)])
                ffn(l, 0, True)
                ada_drain()
                dbgx("x_ffn1_%d" % l)
                if stop_after == "ffn1_%d" % l:
                    return
                mixer(l, need_ctx)
                dbgx("x_mix_%d" % l)
                if stop_after is not None and stop_after in ("h", "rnn", "conv", "att", "mix_%d" % l):
                    return
                if l == 0:
                    pending.extend([ada_part(1, 0), ada_part(1, 1), ada_part(1, 2)])
                ffn(l, 1, need_ctx)
                ada_drain()
                dbgx("x_ffn2_%d" % l)
                if stop_after == "ffn2_%d" % l:
                    return
            final_norm()

        forward()
        P.barrier(final=True)
    return nc, dbg


def _fm(v):
    v = np.asarray(v)
    return np.ascontiguousarray(v.reshape(-1, 128).T)


def _host_consts():
    inv = (np.float32(10000.0) ** (-np.arange(16, dtype=np.float32) * np.float32(2.0) / np.float32(32))).astype(np.float32)
    t = np.arange(SEQ)
    row = (t // 64).astype(np.float32)
    col = (t % 64).astype(np.float32)
    cosT = np.zeros((128, SEQ), np.float32)
    sinT = np.zeros((128, SEQ), np.float32)
    rperm = np.zeros((128, 128), np.float32)
    for p in range(128):
        d = p % 64
        a = d // 32
        half = (d % 32) // 16
        n = d % 16
        ang = ((row if a == 0 else col) * inv[n]).astype(np.float32)
        cosT[p] = np.cos(ang).astype(np.float32)
        sinT[p] = np.sin(ang).astype(np.float32)
        if half == 0:
            rperm[p + 16, p] = -1.0
        else:
            rperm[p - 16, p] = 1.0
    cst = np.zeros((128, 3, 128), np.float32)
    cst[:, 0, :] = 1.0 / 2048.0
    cst[:, 1, :] = 1.0 / 128.0
    cst[:, 2, :] = rperm
    return cosT, sinT, cst


def _prep_inputs(inp):
    f32 = np.float32
    cosT, sinT, cst = _host_consts()
    par = np.zeros((DEPTH, 128, NPAR), f32)
    rgw = np.zeros((DEPTH, 128, 8, 4, 128), f32)
    for l in range(DEPTH):
        def put(name, arr):
            arr = np.asarray(arr, f32)
            par[l, :, _po[name]:_po[name] + arr.shape[1]] = arr
        put("adab", _fm(inp["ada_b"][l]))
        put("ng", _fm(inp["norm_g"][l].reshape(-1)))
        put("rcw", inp["rnn_conv_w"][l].T.reshape(8, 128, 4).transpose(1, 0, 2).reshape(128, 32))
        put("rcb", _fm(inp["rnn_conv_b"][l]))
        put("rgbr", _fm(inp["rg_b_r"][l].reshape(-1)))
        put("rgbi", _fm(inp["rg_b_i"][l].reshape(-1)))
        put("rglam", _fm(inp["rg_lam"][l].reshape(-1)))
        put("cvw", inp["cv_dw_w"][l].T.reshape(8, 128, 31).transpose(1, 0, 2).reshape(128, 248))
        put("cvb", _fm(inp["cv_dw_b"][l]))
        put("cvg", _fm(inp["cv_ln_g"][l]))
        put("cvbb", _fm(inp["cv_ln_b"][l]))
        put("dalam", np.broadcast_to(inp["da_lam"][l].reshape(1, 256), (128, 256)))
        put("subg", inp["da_subln_g"][l].reshape(128, 1))
        put("fing", _fm(inp["final_g"]))
        for d in range(2):
            for g, nm in enumerate(("rg_w_r", "rg_w_i")):
                w = inp[nm][l, d]
                for c in range(8):
                    rgw[l, 0:64, c, d * 2 + g, 0:64] = w[2 * c]
                    rgw[l, 64:128, c, d * 2 + g, 64:128] = w[2 * c + 1]
    cvd = np.zeros((DEPTH, 128, 8, 31, 128), f32)
    ar = np.arange(128)
    for l in range(DEPTH):
        w = np.asarray(inp["cv_dw_w"][l], f32).T.reshape(8, 128, 31)
        for c in range(8):
            cvd[l, ar, c, :, ar] = w[c]
    shared = {
        "par": par, "rgw": rgw, "cvd": cvd, "cosT": cosT, "sinT": sinT, "cst": cst,
        "ada_w": np.ascontiguousarray(inp["ada_w"], dtype=f32),
        "ffn_w_gate": np.ascontiguousarray(inp["ffn_w_gate"], dtype=f32),
        "ffn_w_up": np.ascontiguousarray(inp["ffn_w_up"], dtype=f32),
        "ffn_w_down": np.ascontiguousarray(inp["ffn_w_down"], dtype=f32),
        "w_in": np.ascontiguousarray(inp["w_in"], dtype=f32),
        "rnn_w_out": np.ascontiguousarray(inp["rnn_w_out"], dtype=f32),
        "cv_w_out": np.ascontiguousarray(inp["cv_w_out"], dtype=f32),
        "da_w_o": np.ascontiguousarray(inp["da_w_o"], dtype=f32),
        "w_out": np.ascontiguousarray(inp["w_out"], dtype=f32),
    }
    maps = []
    B = inp["x"].shape[0]
    for b in range(B):
        xt = np.concatenate([inp["ctx"][b], inp["x"][b]], axis=0).astype(f32)
        xT = np.ascontiguousarray(xt.T).reshape(NCH, 128, NT)
        cc = np.stack([_fm(inp["c"][b]), _fm(inp["c_ctx"])], axis=-1).astype(f32)
        m = dict(shared)
        m["xT"] = xT
        m["cc"] = np.ascontiguousarray(cc)
        maps.append(m)
    return maps


_CACHE = {}


def kernel(**inputs):
    inp = {k: np.asarray(v) for k, v in inputs.items()}
    maps = _prep_inputs(inp)
    if "nc" not in _CACHE:
        _CACHE["nc"] = build_program()[0]
    nc = _CACHE["nc"]
    res = run_bass_kernel_spmd(nc, maps, core_ids=list(range(len(maps))))
    outs = []
    for r in res.results:
        yT = np.asarray(r["yT"]).reshape(D, SEQ)
        outs.append(np.ascontiguousarray(yT.T))
    return np.stack(outs, axis=0).astype(np.float32)
```

```python
from contextlib import ExitStack
import math
import numpy as np
import concourse.bass as bass
import concourse.mybir as mybir
from concourse.bass_utils import run_bass_kernel_spmd

F32 = mybir.dt.float32
BF16 = mybir.dt.bfloat16
AF = mybir.ActivationFunctionType
ALU = mybir.AluOpType
AX = mybir.AxisListType

D = 2048
NCH = 16
SEQ = 2048
CTX = 256
NT = SEQ + CTX
DFF = 5632
NFF = 44
DIN = 13312
EPS = 1e-6
DEPTH = 2
OFF_X, OFF_Y, OFF_G, OFF_Q, OFF_K, OFF_V, OFF_Z = 0, 1024, 2048, 4096, 5120, 6144, 7168
SEGS = [(0, 256, 1), (256, 512, 0), (768, 512, 0), (1280, 512, 0), (1792, 512, 0)]
PAD = 16
PL = PAD + CTX + PAD + SEQ + PAD


def ppos(t):
    return t + PAD if t < CTX else t + 2 * PAD


_po = {}
_n = 0
for _name, _w in (("adab", 144), ("ng", 48), ("rcw", 32), ("rcb", 8), ("rgbr", 16), ("rgbi", 16), ("rglam", 16),
                  ("cvw", 248), ("cvb", 8), ("cvg", 8), ("cvbb", 8), ("dalam", 256), ("subg", 1), ("fing", 16)):
    _po[_name] = _n
    _n += _w
NPAR = _n

WHOLE = "__whole__"


class _St:
    __slots__ = ("w", "r")

    def __init__(self):
        self.w = None
        self.r = []


class Buf:
    def __init__(self, t, name):
        self.t = t
        self.name = name
        self.st = {}

    def __getitem__(self, idx):
        return self.t[idx]


class Tok:
    __slots__ = ("eng", "seq", "sem", "val")

    def __init__(self, eng, seq=None, sem=None, val=None):
        self.eng = eng
        self.seq = seq
        self.sem = sem
        self.val = val


class Prog:
    ENGS = ("pe", "act", "dve", "pool", "sp")
    NDMA = 8

    def __init__(self, nc, es):
        self.nc = nc
        self.es = es
        self.e = {"pe": nc.tensor, "act": nc.scalar, "dve": nc.vector, "pool": nc.gpsimd, "sp": nc.sync}
        self.sem = {k: es.enter_context(nc.semaphore("s_" + k)) for k in self.ENGS}
        self.nseq = {k: 0 for k in self.ENGS}
        self.ninc = {k: 0 for k in self.ENGS}
        self.last_ins = {k: None for k in self.ENGS}
        self.incs = {k: [] for k in self.ENGS}
        self.recent = {k: [] for k in self.ENGS}
        self.seen = {k: {} for k in self.ENGS}
        self.dsem = {q: [es.enter_context(nc.semaphore("d_%s%d" % (q, i))) for i in range(self.NDMA)]
                     for q in ("sp", "pool")}
        self.dcnt = {q: 0 for q in ("sp", "pool")}
        self.dlast = {q: [0] * self.NDMA for q in ("sp", "pool")}
        self.uid = 0
        self.psb = None
        self.psi = 0

    def sbuf(self, name, shape, dt):
        return Buf(self.es.enter_context(self.nc.sbuf_tensor("sb_" + name, list(shape), dt)), name)

    def dram(self, name, shape, dt, kind="Internal"):
        return Buf(self.nc.dram_tensor(name, list(shape), dt, kind=kind), name)

    def init_psum(self):
        self.psb = [Buf(self.es.enter_context(self.nc.psum_tensor("psb%d" % i, [128, 512], F32)), "ps%d" % i)
                    for i in range(8)]

    def ps(self):
        b = self.psb[self.psi % 4]
        self.psi += 1
        return b

    def _resolve(self, tok):
        if tok.eng == "dma":
            return tok.sem, tok.val
        eng = tok.eng
        lst = self.incs[eng]
        lo, hi = 0, len(lst)
        while lo < hi:
            mid = (lo + hi) // 2
            if lst[mid][0] >= tok.seq:
                hi = mid
            else:
                lo = mid + 1
        if lo < len(lst):
            return self.sem[eng], lst[lo][1]
        rec = self.recent[eng]
        k = 0
        while rec[k][0] < tok.seq:
            k += 1
        sq, ins = rec[k]
        del rec[:k + 1]
        self.ninc[eng] += 1
        ins.then_inc(self.sem[eng], 1)
        lst.append((sq, self.ninc[eng]))
        return self.sem[eng], self.ninc[eng]

    def _wait(self, eng, tok):
        if tok is None:
            return
        if tok.eng == "pe" and eng == "pe":
            return
        sem, val = self._resolve(tok)
        key = id(sem)
        if self.seen[eng].get(key, 0) >= val:
            return
        self.seen[eng][key] = val
        self.e[eng].wait_ge(sem, val)

    @staticmethod
    def _norm(x):
        if isinstance(x, Buf):
            return x, WHOLE
        return x

    @staticmethod
    def _states(buf, key):
        if key == WHOLE:
            if WHOLE not in buf.st:
                buf.st[WHOLE] = _St()
            return list(buf.st.values())
        if key not in buf.st:
            buf.st[key] = _St()
        out = [buf.st[key]]
        if WHOLE in buf.st:
            out.append(buf.st[WHOLE])
        return out

    def deps(self, eng, reads, writes):
        for x in reads:
            buf, key = self._norm(x)
            for st in self._states(buf, key):
                self._wait(eng, st.w)
        for x in writes:
            buf, key = self._norm(x)
            for st in self._states(buf, key):
                self._wait(eng, st.w)
                for r in st.r:
                    self._wait(eng, r)

    def record(self, tok, reads, writes):
        for x in reads:
            buf, key = self._norm(x)
            sts = self._states(buf, key) if key == WHOLE else [self._states(buf, key)[0]]
            for st in sts:
                if tok.eng != "dma":
                    st.r = [r for r in st.r if r.eng != tok.eng]
                st.r.append(tok)
        for x in writes:
            buf, key = self._norm(x)
            if key == WHOLE:
                buf.st = {WHOLE: _St()}
                buf.st[WHOLE].w = tok
            else:
                st = self._states(buf, key)[0]
                st.w = tok
                st.r = []

    def op(self, eng, fn, reads=(), writes=()):
        self.deps(eng, reads, writes)
        ins = fn()
        self.nseq[eng] += 1
        self.last_ins[eng] = ins
        rec = self.recent[eng]
        rec.append((self.nseq[eng], ins))
        if len(rec) > 96:
            del rec[:32]
        self.record(Tok(eng, seq=self.nseq[eng]), reads, writes)
        return ins

    def group(self, eng, fns, reads=(), writes=()):
        self.deps(eng, reads, writes)
        ins = None
        for fn in fns:
            ins = fn()
            self.nseq[eng] += 1
        self.last_ins[eng] = ins
        rec = self.recent[eng]
        rec.append((self.nseq[eng], ins))
        if len(rec) > 96:
            del rec[:32]
        self.record(Tok(eng, seq=self.nseq[eng]), reads, writes)

    def dma(self, q, out, in_, reads=(), writes=(), **kw):
        self.deps(q, reads, writes)
        j = self.dcnt[q]
        self.dcnt[q] += 1
        i = j % self.NDMA
        s = self.dsem[q][i]
        rnd = j // self.NDMA
        if rnd > 0:
            key = id(s)
            if self.seen[q].get(key, 0) < 16 * rnd:
                self.seen[q][key] = 16 * rnd
                self.e[q].wait_ge(s, 16 * rnd)
        ins = self.e[q].dma_start(out=out, in_=in_, **kw)
        ins.then_inc(s, 16)
        self.dlast[q][i] = 16 * (rnd + 1)
        tok = Tok("dma", sem=s, val=16 * (rnd + 1))
        self.record(tok, reads, writes)
        return tok

    def barrier(self, final=False):
        toks = [Tok(e, seq=self.nseq[e]) for e in ("pe", "act", "dve") if self.nseq[e] > 0]
        dt = [Tok("dma", sem=self.dsem["sp"][i], val=self.dlast["sp"][i]) for i in range(self.NDMA)
              if self.dlast["sp"][i] > 0]
        if final:
            dt += [Tok("dma", sem=self.dsem["pool"][i], val=self.dlast["pool"][i]) for i in range(self.NDMA)
                   if self.dlast["pool"][i] > 0]
        for e in ("pe", "act", "dve", "sp"):
            for t in toks:
                if t.eng != e:
                    self._wait(e, t)
            for t in dt:
                self._wait(e, t)


class Phase:
    def __init__(self, P):
        self.P = P
        self.es = ExitStack()

    def __enter__(self):
        self.es.__enter__()
        return self

    def sbuf(self, name, shape, dt):
        self.P.uid += 1
        return Buf(self.es.enter_context(self.P.nc.sbuf_tensor("sp_%s_%d" % (name, self.P.uid), list(shape), dt)), name)

    def __exit__(self, *a):
        if a[0] is None:
            self.P.barrier()
        return self.es.__exit__(*a)


class Ring:
    def __init__(self, P, nslots, elems, tag=""):
        self.P = P
        self.slots = [P.sbuf("wring%s%d" % (tag, i), [128, elems], BF16) for i in range(nslots)]
        self.i = 0

    def load(self, src, a, b):
        s = self.slots[self.i % len(self.slots)]
        self.i += 1
        view = s.t[:, 0:a * b].rearrange("p (a b) -> p a b", a=a)
        self.P.dma("pool", view, src, writes=[s])
        return s, view


def build_program(debug=(), stop_after=None):
    nc = bass.Bass("TRN2", target_bir_lowering=False)
    dbg = {}
    with ExitStack() as es:
        P = Prog(nc, es)
        P.init_psum()
        IN = "ExternalInput"
        xT = P.dram("xT", [NCH, 128, NT], F32, kind=IN)
        ccd = P.dram("cc", [128, NCH, 2], F32, kind=IN)
        pard = P.dram("par", [DEPTH, 128, NPAR], F32, kind=IN)
        rgwd = P.dram("rgw", [DEPTH, 128, 8, 4, 128], F32, kind=IN)
        cosd = P.dram("cosT", [128, SEQ], F32, kind=IN)
        sind = P.dram("sinT", [128, SEQ], F32, kind=IN)
        cstd = P.dram("cst", [128, 3, 128], F32, kind=IN)
        cvdd = P.dram("cvd", [DEPTH, 128, 8, 31, 128], F32, kind=IN)
        ada_w = P.dram("ada_w", [DEPTH, D, 9 * D], F32, kind=IN)
        w_gate = P.dram("ffn_w_gate", [DEPTH, 2, D, DFF], F32, kind=IN)
        w_up = P.dram("ffn_w_up", [DEPTH, 2, D, DFF], F32, kind=IN)
        w_down = P.dram("ffn_w_down", [DEPTH, 2, DFF, D], F32, kind=IN)
        w_in = P.dram("w_in", [DEPTH, D, DIN], F32, kind=IN)
        rnn_wo = P.dram("rnn_w_out", [DEPTH, 1024, D], F32, kind=IN)
        cv_wo = P.dram("cv_w_out", [DEPTH, 1024, D], F32, kind=IN)
        da_wo = P.dram("da_w_o", [DEPTH, 1024, D], F32, kind=IN)
        w_out = P.dram("w_out", [DEPTH, D, D], F32, kind=IN)
        yT = P.dram("yT", [NCH, 128, SEQ], F32, kind="ExternalOutput")
        xs = P.dram("xs", [NCH, 128, NT], F32)
        mscr = P.dram("mscr", [3, 8, 128, NT], BF16, kind="ExternalOutput" if "m" in debug else "Internal")
        zsig = P.dram("zsig", [48, 128, NT], F32)
        cvo = P.dram("cvo", [8, 128, NT], F32)

        def dbgx(name):
            if name in debug:
                t = P.dram("dbg_" + name, [NCH, 128, NT], F32, kind="ExternalOutput")
                P.dma("sp", t.t.ap(), xs.t.ap(), reads=[xs], writes=[t])
                dbg[name] = t

        xsv = xs.t.ap().rearrange("c p t -> p c t")

        ring = Ring(P, 5, 6144)
        par = P.sbuf("par", [128, DEPTH, NPAR], F32)
        cst = P.sbuf("cst", [128, 3, 128], F32)
        onesb = P.sbuf("onesb", [128, 128], BF16)
        modr = P.sbuf("modr", [128, DEPTH, 144, 2], F32)
        modA = P.sbuf("modA", [128, DEPTH, 3, NCH, 2], F32)
        modG = P.sbuf("modG", [128, DEPTH, 3, NCH, 2], F32)
        cneg = P.sbuf("cneg", [128, DEPTH, 2, 16], F32)
        lamv = P.sbuf("lamv", [128, DEPTH, 4], F32)
        P.dma("sp", par[:], pard.t.ap().rearrange("l p n -> p l n"), writes=[par])
        P.dma("sp", cst[:], cstd.t.ap(), writes=[cst])
        P.op("dve", lambda: nc.vector.memset(onesb[:], 1.0), writes=[onesb])
        ones128b = P.sbuf("ones128b", [128, 128], BF16)
        P.op("dve", lambda: nc.vector.memset(ones128b[:], 1.0 / 128.0), writes=[ones128b])
        ones2048b = P.sbuf("ones2048b", [128, 128], BF16)
        P.op("dve", lambda: nc.vector.memset(ones2048b[:], 1.0 / 2048.0), writes=[ones2048b])
        epsc = P.sbuf("epsc", [128, 2], F32)
        P.op("dve", lambda: nc.vector.memset(epsc[:], EPS), writes=[epsc])
        ones2048 = cst[:, 0, :]
        ones128 = cst[:, 1, :]
        rperm = cst[:, 2, :]
        ones1024 = None

        def pr(l, name, i0=0, n=1):
            o = _po[name] + i0
            return par[:, l, o:o + n]

        for c in range(NCH):
            P.dma("sp", xs.t.ap()[c], xT.t.ap()[c], writes=[(xs, c)])
        P.barrier()

        scb = P.sbuf("scb", [128, NCH, 2], BF16)
        with Phase(P) as ph:
            ccs = ph.sbuf("ccs", [128, NCH, 2], F32)
            tmp = ph.sbuf("tmpm", [128, 64], F32)
            tmp2 = ph.sbuf("tmpm2", [128, 64], F32)
            P.dma("sp", ccs[:], ccd.t.ap(), writes=[ccs])
            P.op("act", lambda: nc.scalar.activation(scb[:], ccs[:], AF.Silu), reads=[ccs], writes=[scb])
            for l in range(DEPTH):
                lam_init = 0.8 - 0.6 * math.exp(-0.3 * l)
                P.op("act", lambda: nc.scalar.activation(tmp[:, 0:16], pr(l, "rglam", 0, 16), AF.Exp, scale=-1.0),
                     reads=[par], writes=[tmp])
                P.op("act", lambda: nc.scalar.activation(tmp2[:, 0:16], tmp[:, 0:16], AF.Ln, bias=1.0, scale=1.0),
                     reads=[tmp], writes=[tmp2])
                P.op("dve", lambda: nc.vector.tensor_scalar_mul(cneg[:, l, 0, :], tmp2[:, 0:16], -8.0),
                     reads=[tmp2], writes=[(cneg, (l, 0))])
                P.op("dve", lambda: nc.vector.tensor_scalar_mul(cneg[:, l, 1, :], tmp2[:, 0:16], -16.0),
                     reads=[tmp2], writes=[(cneg, (l, 1))])
                dl = _po["dalam"]
                P.op("dve", lambda: nc.vector.tensor_tensor(tmp[:, 0:64], par[:, l, dl:dl + 64], par[:, l, dl + 64:dl + 128],
                                                            ALU.mult), reads=[par, tmp2], writes=[tmp])
                P.op("dve", lambda: nc.vector.reduce_sum(tmp2[:, 0:1], tmp[:, 0:64], AX.X), reads=[tmp], writes=[tmp2])
                P.op("dve", lambda: nc.vector.tensor_tensor(tmp[:, 0:64], par[:, l, dl + 128:dl + 192],
                                                            par[:, l, dl + 192:dl + 256], ALU.mult),
                     reads=[par, tmp2], writes=[tmp])
                P.op("dve", lambda: nc.vector.reduce_sum(tmp2[:, 1:2], tmp[:, 0:64], AX.X), reads=[tmp], writes=[tmp2])
                P.op("act", lambda: nc.scalar.activation(tmp[:, 0:2], tmp2[:, 0:2], AF.Exp), reads=[tmp2], writes=[tmp])
                P.op("dve", lambda: nc.vector.scalar_tensor_tensor(lamv[:, l, 0:1], tmp[:, 1:2], -lam_init, tmp[:, 0:1],
                                                                   ALU.add, ALU.subtract), reads=[tmp], writes=[(lamv, (l, 0))])
                P.op("dve", lambda: nc.vector.tensor_scalar_mul(lamv[:, l, 1:2], pr(l, "subg"), 1.0 - lam_init),
                     reads=[par], writes=[(lamv, (l, 1))])

        mps = P.psb[7]

        def ada_part(l, k):
            awv = ada_w.t.ap()[l].rearrange("(kc p) n -> p kc n", p=128)
            def mods_mm(nb, ws, wv):
                for j in range(2):
                    n = nb * 2 + j
                    P.group("pe", [
                        (lambda kc=kc: nc.tensor.matmul(mps[:, 2 * n:2 * n + 2], wv[:, kc, j * 128:(j + 1) * 128],
                                                        scb[:, kc, :], start=(kc == 0), stop=(kc == NCH - 1)))
                        for kc in range(NCH)], reads=[ws, scb], writes=[(mps, k)])
            prev = None
            for nb in range(24 * k, 24 * (k + 1)):
                ws, wv = ring.load(awv[:, :, nb * 256:(nb + 1) * 256], NCH, 256)
                if prev is not None:
                    mods_mm(*prev)
                prev = (nb, ws, wv)
                yield
            mods_mm(*prev)
            mv = mps[:, 0:288].rearrange("p (n j) -> p n j", j=2)
            for j in range(2):
                P.op("dve", lambda: nc.vector.tensor_tensor(modr[:, l, 48 * k:48 * (k + 1), j], mv[:, 48 * k:48 * (k + 1), j],
                                                            pr(l, "adab", 48 * k, 48), ALU.add),
                     reads=[(mps, k), par], writes=[(modr, (l, k))])
            for j in range(2):
                P.op("dve", lambda: nc.vector.scalar_tensor_tensor(
                    modA[:, l, k, :, j], modr[:, l, (3 * k + 1) * 16:(3 * k + 2) * 16, j], 1.0,
                    pr(l, "ng", k * 16, 16), ALU.add, ALU.mult), reads=[(modr, (l, k)), par], writes=[(modA, (l, k))])
                P.op("dve", lambda: nc.vector.tensor_scalar_mul(
                    modG[:, l, k, :, j], modr[:, l, (3 * k + 2) * 16:(3 * k + 3) * 16, j], 1.0 if k == 1 else 0.5),
                    reads=[(modr, (l, k))], writes=[(modG, (l, k))])

        pending = []

        def ada_step():
            while pending:
                try:
                    next(pending[0])
                    return
                except StopIteration:
                    pending.pop(0)

        def ada_drain():
            while pending:
                ada_step()

        pending.append(ada_part(0, 0))
        ada_drain()

        def shiftp(l, k, c, j):
            return modr[:, l, (3 * k) * 16 + c, j:j + 1]

        def rsqrt_eps(dst, src, n):
            P.op("act", lambda: nc.scalar.activation(dst[:, 0:n], src[:, 0:n], AF.Ln, bias=epsc[:, 0:1], scale=1.0),
                 reads=[src, epsc], writes=[dst])
            P.op("act", lambda: nc.scalar.activation(dst[:, 0:n], dst[:, 0:n], AF.Exp, scale=-0.5), reads=[dst], writes=[dst])

        def norm_tile(ph, xt, sqb, rstd, start, n, consume, gains):
            P.dma("sp", xt[:, :, 0:n], xsv[:, :, start:start + n], reads=[xs], writes=[xt])
            sp_ = P.ps()
            for c in range(NCH):
                q = sqb[c % 2]
                P.op("act", lambda: nc.scalar.activation(q[:, 0:n], xt[:, c, 0:n], AF.Square), reads=[xt], writes=[q])
                P.op("pe", lambda: nc.tensor.matmul(sp_[:, 0:n], ones2048b[:], q[:, 0:n], start=(c == 0), stop=(c == NCH - 1)),
                     reads=[q, ones2048b], writes=[sp_])
            rsqrt_eps(rstd, sp_, n)
            for c in range(NCH):
                consume(c)

        def subtiles(with_ctx):
            if with_ctx:
                return [[(0, 256, 1), (256, 512, 0), (768, 512, 0)], [(1280, 512, 0), (1792, 512, 0)]]
            return [[(256, 512, 0), (768, 512, 0)], [(1280, 512, 0), (1792, 512, 0)]]

        def ffn(l, k, with_ctx):
            kn = 0 if k == 0 else 2
            KN = kn
            wgv = w_gate.t.ap()[l, k].rearrange("(kc p) n -> p kc n", p=128)
            wuv = w_up.t.ap()[l, k].rearrange("(kc p) n -> p kc n", p=128)
            wdv = w_down.t.ap()[l, k].rearrange("(f p) n -> p f n", p=128)
            for ST in subtiles(with_ctx):
                T = sum(s[1] for s in ST)
                offs = []
                o = 0
                for s in ST:
                    offs.append(o)
                    o += s[1]
                with Phase(P) as ph:
                    hT = ph.sbuf("hT", [128, NCH, T], BF16)
                    act = ph.sbuf("act", [128, 22, T], BF16)
                    xt = ph.sbuf("xt", [128, NCH, 256], F32)
                    sqb = [ph.sbuf("sq%d" % i, [128, 512], BF16) for i in range(2)]
                    tmpb = [ph.sbuf("tm%d" % i, [128, 512], F32) for i in range(2)]
                    rstd = ph.sbuf("rstd", [128, 512], F32)
                    xr = [ph.sbuf("xr%d" % i, [128, 512], F32) for i in range(2)]
                    xo = [ph.sbuf("xo%d" % i, [128, 512], F32) for i in range(2)]
                    cnt = [0]
                    halves = []
                    for si, (start, n, isc) in enumerate(ST):
                        for h0 in range(0, n, 256):
                            halves.append((si, start + h0, min(256, n - h0), isc, offs[si] + h0))
                    for (si, start, n, isc, off) in halves:

                        def consume(c, start=start, n=n, isc=isc, off=off, si=si):
                            tb = tmpb[c % 2]
                            P.op("dve", lambda: nc.vector.scalar_tensor_tensor(
                                tb[:, 0:n], xt[:, c, 0:n], modA[:, l, kn, c, isc:isc + 1], rstd[:, 0:n], ALU.mult, ALU.mult),
                                reads=[xt, rstd, (modA, (l, KN))], writes=[tb])
                            P.op("act", lambda: nc.scalar.activation(hT[:, c, off:off + n], tb[:, 0:n], AF.Identity,
                                                                     bias=shiftp(l, kn, c, isc), scale=1.0),
                                 reads=[tb, (modr, (l, KN))], writes=[(hT, si)])
                        norm_tile(ph, xt, sqb, rstd, start, n, consume, None)
                    for half in range(2):
                        for fp in range(11):
                            c0 = (half * 22 + fp * 2) * 128
                            gs, gv = ring.load(wgv[:, :, c0:c0 + 256], NCH, 256)
                            us, uv = ring.load(wuv[:, :, c0:c0 + 256], NCH, 256)
                            ada_step()
                            for j in range(2):
                                f = fp * 2 + j
                                for si, (start, n, isc) in enumerate(ST):
                                    off = offs[si]
                                    pg = P.ps()
                                    pu = P.ps()
                                    P.group("pe", [(lambda kc=kc: nc.tensor.matmul(
                                        pg[:, 0:n], gv[:, kc, j * 128:(j + 1) * 128], hT[:, kc, off:off + n],
                                        start=(kc == 0), stop=(kc == NCH - 1))) for kc in range(NCH)],
                                        reads=[gs, (hT, si)], writes=[pg])
                                    P.group("pe", [(lambda kc=kc: nc.tensor.matmul(
                                        pu[:, 0:n], uv[:, kc, j * 128:(j + 1) * 128], hT[:, kc, off:off + n],
                                        start=(kc == 0), stop=(kc == NCH - 1))) for kc in range(NCH)],
                                        reads=[us, (hT, si)], writes=[pu])
                                    tb = tmpb[cnt[0] % 2]
                                    cnt[0] += 1
                                    P.op("act", lambda: nc.scalar.activation(tb[:, 0:n], pg[:, 0:n], AF.Silu),
                                         reads=[pg], writes=[tb])
                                    P.op("dve", lambda: nc.vector.tensor_tensor(act[:, f, off:off + n], tb[:, 0:n], pu[:, 0:n],
                                                                                ALU.mult),
                                         reads=[tb, pu], writes=[(act, (f, si))])
                        for dp in range(8):
                            ds_, dv = ring.load(wdv[:, half * 22:(half + 1) * 22, dp * 256:(dp + 1) * 256], 22, 256)
                            ada_step()
                            for j in range(2):
                                d = dp * 2 + j
                                for si, (start, n, isc) in enumerate(ST):
                                    off = offs[si]
                                    pd = P.ps()
                                    P.group("pe", [(lambda f=f: nc.tensor.matmul(
                                        pd[:, 0:n], dv[:, f, j * 128:(j + 1) * 128], act[:, f, off:off + n],
                                        start=(f == 0), stop=(f == 21))) for f in range(22)],
                                        reads=[ds_] + [(act, (f, si)) for f in range(22)], writes=[pd])
                                    b = cnt[0] % 2
                                    cnt[0] += 1
                                    P.dma("sp", xr[b][:, 0:n], xs.t.ap()[d, :, start:start + n], reads=[(xs, (d, start))],
                                          writes=[xr[b]])
                                    P.op("dve", lambda: nc.vector.scalar_tensor_tensor(
                                        xo[b][:, 0:n], pd[:, 0:n], modG[:, l, kn, d, isc:isc + 1], xr[b][:, 0:n],
                                        ALU.mult, ALU.add), reads=[pd, xr[b], (modG, (l, kn))], writes=[xo[b]])
                                    P.dma("sp", xs.t.ap()[d, :, start:start + n], xo[b][:, 0:n], reads=[xo[b]],
                                          writes=[(xs, (d, start))])

        def proj(ws, wv, j0, hT, seg, si):
            start, n, isc = seg
            p_ = P.ps()
            P.group("pe", [(lambda kc=kc: nc.tensor.matmul(p_[:, 0:n], wv[:, kc, j0:j0 + 128], hT[:, kc, start:start + n],
                                                           start=(kc == 0), stop=(kc == NCH - 1))) for kc in range(NCH)],
                    reads=[ws, (hT, si)], writes=[p_])
            return p_

        def mixer(l, need_ctx):
            KN = 1
            lam_init = 0.8 - 0.6 * math.exp(-0.3 * l)
            winv = w_in.t.ap()[l].rearrange("(kc p) n -> p kc n", p=128)
            mv = mscr.t.ap()
            with Phase(P) as phh:
                hT = phh.sbuf("hmix", [128, NCH, NT], BF16)
                with Phase(P) as ph:
                    xt = ph.sbuf("xt", [128, NCH, 512], F32)
                    sqb = [ph.sbuf("sq%d" % i, [128, 512], BF16) for i in range(2)]
                    tmpb = [ph.sbuf("tm%d" % i, [128, 512], F32) for i in range(2)]
                    rstd = ph.sbuf("rstd", [128, 512], F32)
                    for si, (start, n, isc) in enumerate(SEGS):
                        def consume(c, start=start, n=n, isc=isc, si=si):
                            tb = tmpb[c % 2]
                            P.op("dve", lambda: nc.vector.scalar_tensor_tensor(
                                tb[:, 0:n], xt[:, c, 0:n], modA[:, l, 1, c, isc:isc + 1], rstd[:, 0:n], ALU.mult, ALU.mult),
                                reads=[xt, rstd, (modA, (l, KN))], writes=[tb])
                            P.op("act", lambda: nc.scalar.activation(hT[:, c, start:start + n], tb[:, 0:n], AF.Identity,
                                                                     bias=shiftp(l, 1, c, isc), scale=1.0),
                                 reads=[tb, (modr, (l, KN))], writes=[(hT, si)])
                        norm_tile(ph, xt, sqb, rstd, start, n, consume, None)
                if stop_after == "h":
                    return
                with Phase(P) as ph:
                    zb = [ph.sbuf("zb%d" % i, [128, 512], F32) for i in range(3)]
                    cnt = 0
                    for zp in range(24):
                        ws, wv = ring.load(winv[:, :, OFF_Z + zp * 256:OFF_Z + (zp + 1) * 256], NCH, 256)
                        for j in range(2):
                            zc = zp * 2 + j
                            for si, seg in enumerate(SEGS):
                                start, n, isc = seg
                                if isc and not need_ctx:
                                    continue
                                p_ = proj(ws, wv, j * 128, hT, seg, si)
                                b = zb[cnt % 3]
                                cnt += 1
                                P.op("act", lambda: nc.scalar.activation(b[:, 0:n], p_[:, 0:n], AF.Sigmoid), reads=[p_], writes=[b])
                                P.dma("sp", zsig.t.ap()[zc, :, start:start + n], b[:, 0:n], reads=[b], writes=[(zsig, (zc, si))])
                with Phase(P) as ph:
                    xp = ph.sbuf("xp", [128, PL], F32)
                    u = ph.sbuf("u", [128, PL], F32)
                    ub = ph.sbuf("ub", [128, PL], BF16)
                    hf = ph.sbuf("hf", [128, PL], F32)
                    hb = ph.sbuf("hb", [128, PL], F32)
                    RB = ph.sbuf("RB", [128, NT], F32)
                    IB = ph.sbuf("IB", [128, NT], F32)
                    gy = [ph.sbuf("gy0", [128, 512], F32)] * 2
                    mo = [ph.sbuf("mo0", [128, 512], BF16)] * 2
                    P.op("dve", lambda: nc.vector.memset(xp[:], 0.0), writes=[xp])
                    rgv = rgwd.t.ap()[l]
                    for c in range(8):
                        wxs, wxv = ring.load(winv[:, :, OFF_X + c * 128:OFF_X + (c + 1) * 128], NCH, 128)
                        wys, wyv = ring.load(winv[:, :, OFF_Y + c * 128:OFF_Y + (c + 1) * 128], NCH, 128)
                        rgs, rgt = ring.load(rgv[:, c], 4, 128)
                        for si, seg in enumerate(SEGS):
                            start, n, isc = seg
                            p_ = proj(wxs, wxv, 0, hT, seg, si)
                            pp = ppos(start)
                            P.op("act", lambda: nc.scalar.copy(xp[:, pp:pp + n], p_[:, 0:n]), reads=[p_], writes=[(xp, si)])
                        lo, hi = PAD, PL - PAD
                        P.op("dve", lambda: nc.vector.tensor_scalar(u[:, lo:hi], xp[:, lo - 1:hi - 1], pr(l, "rcw", c * 4, 1),
                                                                    pr(l, "rcb", c, 1), ALU.mult, ALU.add),
                             reads=[xp, par], writes=[u])
                        for j in range(1, 4):
                            P.op("dve", lambda: nc.vector.scalar_tensor_tensor(
                                u[:, lo:hi], xp[:, lo + j - 1:hi + j - 1], pr(l, "rcw", c * 4 + j, 1), u[:, lo:hi],
                                ALU.mult, ALU.add), reads=[xp, par, u], writes=[u])
                        P.op("act", lambda: nc.scalar.copy(ub[:, lo:hi], u[:, lo:hi]), reads=[u], writes=[ub])
                        for d in range(2):
                            hbuf = hf if d == 0 else hb
                            order = list(range(5)) if d == 0 else [0, 4, 3, 2, 1]
                            for si in range(5):
                                start, n, isc = SEGS[si]
                                pp = ppos(start)
                                pr_ = P.ps()
                                pi_ = P.ps()
                                P.op("pe", lambda: nc.tensor.matmul(pr_[:, 0:n], rgt[:, d * 2 + 0, :], ub[:, pp:pp + n],
                                                                    start=True, stop=True), reads=[rgs, ub], writes=[pr_])
                                P.op("pe", lambda: nc.tensor.matmul(pi_[:, 0:n], rgt[:, d * 2 + 1, :], ub[:, pp:pp + n],
                                                                    start=True, stop=True), reads=[rgs, ub], writes=[pi_])
                                P.op("act", lambda: nc.scalar.activation(RB[:, start:start + n], pr_[:, 0:n], AF.Sigmoid,
                                                                         bias=pr(l, "rgbr", d * 8 + c, 1), scale=1.0),
                                     reads=[pr_, par], writes=[(RB, si)])
                                P.op("act", lambda: nc.scalar.activation(IB[:, start:start + n], pi_[:, 0:n], AF.Sigmoid,
                                                                         bias=pr(l, "rgbi", d * 8 + c, 1), scale=1.0),
                                     reads=[pi_, par], writes=[(IB, si)])
                            for si in range(5):
                                start, n, isc = SEGS[si]
                                P.op("act", lambda: nc.scalar.activation(xp[:, ppos(start):ppos(start) + n], RB[:, start:start + n], AF.Exp,
                                                                         scale=cneg[:, l, 1, d * 8 + c:d * 8 + c + 1]),
                                     reads=[(RB, si), cneg], writes=[(xp, si)])
                                P.op("act", lambda: nc.scalar.activation(RB[:, start:start + n], RB[:, start:start + n], AF.Exp,
                                                                         scale=cneg[:, l, 0, d * 8 + c:d * 8 + c + 1]),
                                     reads=[(RB, si), cneg], writes=[(RB, si)])
                            for si in range(5):
                                start, n, isc = SEGS[si]
                                P.op("act", lambda: nc.scalar.activation(xp[:, ppos(start):ppos(start) + n], xp[:, ppos(start):ppos(start) + n], AF.Sqrt,
                                                                         bias=1.0, scale=-1.0),
                                     reads=[(xp, si)], writes=[(xp, si)])
                            prev = None
                            for si in order:
                                start, n, isc = SEGS[si]
                                pp = ppos(start)
                                P.op("dve", lambda: nc.vector.tensor_tensor(IB[:, start:start + n], IB[:, start:start + n],
                                                                            xp[:, ppos(start):ppos(start) + n], ALU.mult),
                                     reads=[(IB, si), (xp, si)], writes=[(IB, si)])
                                P.op("dve", lambda: nc.vector.tensor_tensor(IB[:, start:start + n], IB[:, start:start + n],
                                                                            u[:, pp:pp + n], ALU.mult),
                                     reads=[(IB, si), u], writes=[(IB, si)])
                                if prev is None:
                                    init = 0.0
                                else:
                                    init = hbuf[:, prev:prev + 1]
                                if d == 0:
                                    P.op("dve", lambda: nc.vector.tensor_tensor_scan(hbuf[:, pp:pp + n], RB[:, start:start + n],
                                                                                     IB[:, start:start + n], init, ALU.mult, ALU.add),
                                         reads=[(RB, si), (IB, si), hbuf], writes=[hbuf])
                                    prev = pp + n - 1
                                else:
                                    P.op("dve", lambda: nc.vector.tensor_tensor_scan(
                                        hbuf[:, pp:pp + n][:, ::-1], RB[:, start:start + n][:, ::-1], IB[:, start:start + n][:, ::-1],
                                        init, ALU.mult, ALU.add), reads=[(RB, si), (IB, si), hbuf], writes=[hbuf])
                                    prev = pp
                        for si, seg in enumerate(SEGS):
                            start, n, isc = seg
                            if isc and not need_ctx:
                                continue
                            pp = ppos(start)
                            p_ = proj(wys, wyv, 0, hT, seg, si)
                            b = si % 2
                            P.op("act", lambda: nc.scalar.activation(gy[b][:, 0:n], p_[:, 0:n], AF.Gelu_apprx_tanh),
                                 reads=[p_], writes=[gy[b]])
                            P.op("dve", lambda: nc.vector.tensor_tensor(hf[:, pp:pp + n], hf[:, pp:pp + n], hb[:, pp:pp + n], ALU.add),
                                 reads=[hf, hb], writes=[hf])
                            P.op("dve", lambda: nc.vector.tensor_tensor(mo[b][:, 0:n], gy[b][:, 0:n], hf[:, pp:pp + n], ALU.mult),
                                 reads=[gy[b], hf], writes=[mo[b]])
                            P.dma("sp", mv[0, c, :, start:start + n], mo[b][:, 0:n], reads=[mo[b]], writes=[(mscr, (0, c, si))])
                if stop_after == "rnn":
                    return
                with Phase(P) as ph:
                    gpb = [ph.sbuf("gpb%d" % i, [128, PL], BF16) for i in range(2)]
                    sg = [ph.sbuf("sg%d" % i, [128, 512], F32) for i in range(2)]
                    co = [ph.sbuf("co%d" % i, [128, 512], F32) for i in range(2)]
                    for g_ in gpb:
                        P.op("dve", lambda: nc.vector.memset(g_[:], 0.0), writes=[g_])
                    cnt = 0
                    for c in range(8):
                        gp = gpb[c % 2]
                        was, wav = ring.load(winv[:, :, OFF_G + c * 128:OFF_G + (c + 1) * 128], NCH, 128)
                        wgs, wgv_ = ring.load(winv[:, :, OFF_G + 1024 + c * 128:OFF_G + 1024 + (c + 1) * 128], NCH, 128)
                        dgs, dgv = ring.load(cvdd.t.ap()[l][:, c], 31, 128)
                        for si, seg in enumerate(SEGS):
                            start, n, isc = seg
                            pp = ppos(start)
                            pa = proj(was, wav, 0, hT, seg, si)
                            pg = proj(wgs, wgv_, 0, hT, seg, si)
                            b = si % 2
                            P.op("act", lambda: nc.scalar.activation(sg[b][:, 0:n], pg[:, 0:n], AF.Sigmoid), reads=[pg], writes=[sg[b]])
                            P.op("dve", lambda: nc.vector.tensor_tensor(gp[:, pp:pp + n], sg[b][:, 0:n], pa[:, 0:n], ALU.mult),
                                 reads=[sg[b], pa], writes=[(gp, si)])
                        for si, seg in enumerate(SEGS):
                            start, n, isc = seg
                            if isc and not need_ctx:
                                continue
                            pp = ppos(start)
                            pc = P.ps()
                            P.group("pe", [(lambda j=j: nc.tensor.matmul(pc[:, 0:n], dgv[:, j, :], gp[:, pp + j - 15:pp + j - 15 + n],
                                                                         start=(j == 0), stop=(j == 30))) for j in range(31)],
                                    reads=[dgs, gp], writes=[pc])
                            b = cnt % 2
                            cnt += 1
                            P.op("act", lambda: nc.scalar.activation(co[b][:, 0:n], pc[:, 0:n], AF.Identity,
                                                                     bias=pr(l, "cvb", c, 1), scale=1.0),
                                 reads=[pc, par], writes=[co[b]])
                            P.dma("sp", cvo.t.ap()[c, :, start:start + n], co[b][:, 0:n], reads=[co[b]], writes=[(cvo, (c, si))])
                with Phase(P) as ph:
                    ct = ph.sbuf("ct", [128, 8, 512], F32)
                    sqb = [ph.sbuf("sq%d" % i, [128, 512], F32) for i in range(2)]
                    mean = ph.sbuf("mean", [128, 512], F32)
                    rstd = ph.sbuf("rstd", [128, 512], F32)
                    tb = [ph.sbuf("tb%d" % i, [128, 512], F32) for i in range(2)]
                    mo = [ph.sbuf("mo%d" % i, [128, 512], BF16) for i in range(2)]
                    cvv = cvo.t.ap().rearrange("c p t -> p c t")
                    for si, seg in enumerate(SEGS):
                        start, n, isc = seg
                        if isc and not need_ctx:
                            continue
                        P.dma("sp", ct[:, :, 0:n], cvv[:, :, start:start + n], reads=[cvo], writes=[ct])
                        pm = P.ps()
                        pq = P.ps()
                        for c in range(8):
                            q = sqb[c % 2]
                            P.op("act", lambda: nc.scalar.activation(q[:, 0:n], ct[:, c, 0:n], AF.Square), reads=[ct], writes=[q])
                            P.op("pe", lambda: nc.tensor.matmul(pm[:, 0:n], ones128, ct[:, c, 0:n], start=(c == 0), stop=(c == 7)),
                                 reads=[ct, cst], writes=[pm])
                            P.op("pe", lambda: nc.tensor.matmul(pq[:, 0:n], ones128, q[:, 0:n], start=(c == 0), stop=(c == 7)),
                                 reads=[q, cst], writes=[pq])
                        P.op("act", lambda: nc.scalar.mul(mean[:, 0:n], pm[:, 0:n], 0.125), reads=[pm], writes=[mean])
                        P.op("dve", lambda: nc.vector.tensor_tensor(rstd[:, 0:n], mean[:, 0:n], mean[:, 0:n], ALU.mult),
                             reads=[mean], writes=[rstd])
                        P.op("dve", lambda: nc.vector.scalar_tensor_tensor(rstd[:, 0:n], pq[:, 0:n], 0.125, rstd[:, 0:n],
                                                                           ALU.mult, ALU.subtract), reads=[pq, rstd], writes=[rstd])
                        rsqrt_eps(rstd, rstd, n)
                        for c in range(8):
                            b = c % 2
                            P.op("dve", lambda: nc.vector.tensor_tensor(tb[b][:, 0:n], ct[:, c, 0:n], mean[:, 0:n], ALU.subtract),
                                 reads=[ct, mean], writes=[tb[b]])
                            P.op("dve", lambda: nc.vector.tensor_tensor(tb[b][:, 0:n], tb[b][:, 0:n], rstd[:, 0:n], ALU.mult),
                                 reads=[tb[b], rstd], writes=[tb[b]])
                            P.op("act", lambda: nc.scalar.activation(mo[b][:, 0:n], tb[b][:, 0:n], AF.Silu,
                                                                     bias=pr(l, "cvbb", c, 1), scale=pr(l, "cvg", c, 1)),
                                 reads=[tb[b], par], writes=[mo[b]])
                            P.dma("sp", mv[1, c, :, start:start + n], mo[b][:, 0:n], reads=[mo[b]], writes=[(mscr, (1, c, si))])
                if stop_after == "conv":
                    return
                with Phase(P) as ph:
                    QTs = [ph.sbuf("QT%d" % i, [128, NT], BF16) for i in range(2)]
                    KTs = [ph.sbuf("KT%d" % i, [128, NT], BF16) for i in range(2)]
                    Vp = ph.sbuf("Vp", [128, 18, 256], BF16)
                    cs = [ph.sbuf("cs%d" % i, [128, 2, 512], F32) for i in range(2)]
                    qf = [ph.sbuf("qf%d" % i, [128, 512], F32) for i in range(2)]
                    t1 = [ph.sbuf("t10", [128, 512], F32)] * 2
                    t2 = [ph.sbuf("t20", [128, 512], F32)] * 2
                    Pt = [ph.sbuf("Pt%d" % i, [128, 512], BF16) for i in range(4)]
                    rz = [ph.sbuf("rz%d" % i, [128, 512], F32) for i in range(2)]
                    ob = [ph.sbuf("ob%d" % i, [128, 512], F32) for i in range(2)]
                    osq = ph.sbuf("osq", [128, 512], BF16)
                    orr = ph.sbuf("orr", [128, 512], F32)
                    mo = [ph.sbuf("mo%d" % i, [128, 512], BF16) for i in range(2)]
                    neglam = lamv[:, l, 0:1]
                    gsub = lamv[:, l, 1:2]
                    cnt = 0
                    def head_stage(h, stage):
                        nonlocal cnt
                        QT = QTs[h % 2]
                        KT = KTs[h % 2]
                        if stage == "v":
                            if h % 2 == 0:
                                wvs, wvv = ring.load(winv[:, :, OFF_V + h * 128:OFF_V + (h + 2) * 128], NCH, 256)
                                for tc in range(18):
                                    si = 0 if tc < 2 else 1 + (tc - 2) // 4
                                    p_ = P.ps()
                                    P.group("pe", [(lambda kc=kc: nc.tensor.matmul(p_[:, 0:256], hT[:, kc, tc * 128:(tc + 1) * 128],
                                                                                   wvv[:, kc, :], start=(kc == 0), stop=(kc == NCH - 1)))
                                                   for kc in range(NCH)], reads=[wvs, (hT, si)], writes=[p_])
                                    if tc % 2 == 0:
                                        P.op("act", lambda: nc.scalar.copy(Vp[:, tc, :], p_[:, 0:256]), reads=[p_], writes=[(Vp, tc)])
                                    else:
                                        P.op("dve", lambda: nc.vector.tensor_copy(Vp[:, tc, :], p_[:, 0:256]), reads=[p_], writes=[(Vp, tc)])
                        if stage == "qk":
                            wqs, wqv = ring.load(winv[:, :, OFF_Q + h * 128:OFF_Q + (h + 1) * 128], NCH, 128)
                            wks, wkv = ring.load(winv[:, :, OFF_K + h * 128:OFF_K + (h + 1) * 128], NCH, 128)
                            for si, seg in enumerate(SEGS):
                                start, n, isc = seg
                                if isc:
                                    if need_ctx:
                                        p_ = proj(wqs, wqv, 0, hT, seg, si)
                                        P.op("act", lambda: nc.scalar.copy(QT[:, start:start + n], p_[:, 0:n]), reads=[p_], writes=[(QT, si)])
                                    p_ = proj(wks, wkv, 0, hT, seg, si)
                                    P.op("act", lambda: nc.scalar.copy(KT[:, start:start + n], p_[:, 0:n]), reads=[p_], writes=[(KT, si)])
                                    continue
                                cb = cs[si % 2]
                                P.dma("sp", cb[:, 0, :], cosd.t.ap()[:, start - CTX:start - CTX + n], writes=[cb])
                                P.dma("sp", cb[:, 1, :], sind.t.ap()[:, start - CTX:start - CTX + n], writes=[cb])
                                for (ws_, wv_, dst) in ((wqs, wqv, QT), (wks, wkv, KT)):
                                    p_ = proj(ws_, wv_, 0, hT, seg, si)
                                    b = cnt % 2
                                    cnt += 1
                                    P.op("act", lambda: nc.scalar.copy(qf[b][:, 0:n], p_[:, 0:n]), reads=[p_], writes=[qf[b]])
                                    p2 = P.ps()
                                    P.op("pe", lambda: nc.tensor.matmul(p2[:, 0:n], rperm, qf[b][:, 0:n], start=True, stop=True),
                                         reads=[qf[b], cst], writes=[p2])
                                    P.op("dve", lambda: nc.vector.tensor_tensor(t1[b][:, 0:n], qf[b][:, 0:n], cb[:, 0, 0:n], ALU.mult),
                                         reads=[qf[b], cb], writes=[t1[b]])
                                    P.op("dve", lambda: nc.vector.tensor_tensor(t2[b][:, 0:n], p2[:, 0:n], cb[:, 1, 0:n], ALU.mult),
                                         reads=[p2, cb], writes=[t2[b]])
                                    P.op("dve", lambda: nc.vector.tensor_tensor(dst[:, start:start + n], t1[b][:, 0:n], t2[b][:, 0:n], ALU.add),
                                         reads=[t1[b], t2[b]], writes=[(dst, si)])
                        if stage == "core":
                            hoff = (h % 2) * 128
                            for si, seg in enumerate(SEGS):
                                qs, qn, isc = seg
                                if isc and not need_ctx:
                                    continue
                                keys = [0, 1] if isc else list(range(18))
                                nk = len(keys)
                                O = [P.psb[4], P.psb[5]]
                                Z = [P.psb[6], P.psb[7]]

                                def scores(kc):
                                    ksi = 0 if kc < 2 else 1 + (kc - 2) // 4
                                    out = []
                                    for comp in range(2):
                                        sc = P.ps()
                                        P.op("pe", lambda: nc.tensor.matmul(
                                            sc[:, 0:qn], KT[comp * 64:(comp + 1) * 64, kc * 128:(kc + 1) * 128],
                                            QT[comp * 64:(comp + 1) * 64, qs:qs + qn], start=True, stop=True),
                                            reads=[(KT, ksi), (QT, si)], writes=[sc])
                                        out.append(sc)
                                    return out
                                s_cur = scores(keys[0])
                                for i, kc in enumerate(keys):
                                    s_next = scores(keys[i + 1]) if i + 1 < nk else None
                                    for comp in range(2):
                                        pt = Pt[(i % 2) * 2 + comp]
                                        P.op("act", lambda: nc.scalar.activation(pt[:, 0:qn], s_cur[comp][:, 0:qn], AF.Exp, scale=0.125),
                                             reads=[s_cur[comp]], writes=[pt])
                                    for comp in range(2):
                                        pt = Pt[(i % 2) * 2 + comp]
                                        P.op("pe", lambda: nc.tensor.matmul(O[comp][:, 0:qn], Vp[:, kc, hoff:hoff + 128], pt[:, 0:qn],
                                                                            start=(i == 0), stop=(i == nk - 1)),
                                             reads=[(Vp, kc), pt], writes=[O[comp]])
                                        P.op("pe", lambda: nc.tensor.matmul(Z[comp][:, 0:qn], onesb[:], pt[:, 0:qn],
                                                                            start=(i == 0), stop=(i == nk - 1)),
                                             reads=[onesb, pt], writes=[Z[comp]])
                                    s_cur = s_next
                                for comp in range(2):
                                    P.op("act", lambda: nc.scalar.activation(rz[comp][:, 0:qn], Z[comp][:, 0:qn], AF.Ln), reads=[Z[comp]], writes=[rz[comp]])
                                    P.op("act", lambda: nc.scalar.activation(rz[comp][:, 0:qn], rz[comp][:, 0:qn], AF.Exp, scale=-1.0), reads=[rz[comp]], writes=[rz[comp]])
                                    P.op("dve", lambda: nc.vector.tensor_tensor(ob[comp][:, 0:qn], O[comp][:, 0:qn], rz[comp][:, 0:qn], ALU.mult),
                                         reads=[O[comp], rz[comp]], writes=[ob[comp]])
                                P.op("dve", lambda: nc.vector.scalar_tensor_tensor(ob[0][:, 0:qn], ob[1][:, 0:qn], neglam, ob[0][:, 0:qn],
                                                                                   ALU.mult, ALU.add), reads=[ob[0], ob[1], lamv], writes=[ob[0]])
                                P.op("act", lambda: nc.scalar.activation(osq[:, 0:qn], ob[0][:, 0:qn], AF.Square), reads=[ob[0]], writes=[osq])
                                pm = P.ps()
                                P.op("pe", lambda: nc.tensor.matmul(pm[:, 0:qn], ones128b[:], osq[:, 0:qn], start=True, stop=True),
                                     reads=[osq, ones128b], writes=[pm])
                                rsqrt_eps(orr, pm, qn)
                                P.op("dve", lambda: nc.vector.tensor_tensor(orr[:, 0:qn], orr[:, 0:qn], ob[0][:, 0:qn], ALU.mult),
                                     reads=[orr, ob[0]], writes=[orr])
                                b = si % 2
                                P.op("act", lambda: nc.scalar.activation(mo[b][:, 0:qn], orr[:, 0:qn], AF.Identity, bias=0.0, scale=gsub),
                                     reads=[orr, lamv], writes=[mo[b]])
                                P.dma("sp", mv[2, h, :, qs:qs + qn], mo[b][:, 0:qn], reads=[mo[b]], writes=[(mscr, (2, h, si))])
                    head_stage(0, "qk")
                    for h in range(8):
                        head_stage(h, "v")
                        if h + 1 < 8:
                            head_stage(h + 1, "qk")
                        head_stage(h, "core")
            if stop_after == "att":
                return
            bwv = [t.t.ap()[l].rearrange("(kc p) n -> p kc n", p=128) for t in (rnn_wo, cv_wo, da_wo)]
            wov = w_out.t.ap()[l].rearrange("(kc p) n -> p kc n", p=128)
            mvv = mscr.t.ap().rearrange("b c p t -> p b c t")
            with Phase(P) as ph:
                mt = ph.sbuf("mt", [128, 3, 8, 1024], BF16)
                mg = ph.sbuf("mg", [128, NCH, 1024], BF16)
                zt = [ph.sbuf("zt%d" % i, [128, 3, 512], F32) for i in range(2)]
                accb = [ph.sbuf("ac%d" % i, [128, 512], F32) for i in range(2)]
                tmb = [ph.sbuf("tmg%d" % i, [128, 512], F32) for i in range(2)]
                xr = [ph.sbuf("xr%d" % i, [128, 512], F32) for i in range(2)]
                xo = [ph.sbuf("xo%d" % i, [128, 512], F32) for i in range(2)]
                cnt = 0
                groups = [[0, 1], [2, 3], [4]] if need_ctx else [[1, 2], [3, 4]]
                for gi, grp in enumerate(groups):
                    goff = {}
                    o = 0
                    for si in grp:
                        goff[si] = o
                        o += SEGS[si][1]
                    for si in grp:
                        start, n, isc = SEGS[si]
                        off = goff[si]
                        for br in range(3):
                            P.dma("sp", mt[:, br, :, off:off + n], mvv[:, br, :, start:start + n], reads=[mscr],
                                  writes=[(mt, (br, si))])
                    for dp in range(8):
                        wts = []
                        for br in range(3):
                            wts.append(ring.load(bwv[br][:, :, dp * 256:(dp + 1) * 256], 8, 256))
                        for j in range(2):
                            d = dp * 2 + j
                            for si in grp:
                                start, n, isc = SEGS[si]
                                off = goff[si]
                                z_ = zt[cnt % 2]
                                a_ = accb[cnt % 2]
                                t_ = tmb[cnt % 2]
                                cnt += 1
                                for br in range(3):
                                    P.dma("sp", z_[:, br, 0:n], zsig.t.ap()[br * 16 + d, :, start:start + n], reads=[zsig],
                                          writes=[(z_, br)])
                                for br in range(3):
                                    ws, wv = wts[br]
                                    p_ = P.ps()
                                    P.group("pe", [(lambda kc=kc: nc.tensor.matmul(p_[:, 0:n], wv[:, kc, j * 128:(j + 1) * 128],
                                                                                   mt[:, br, kc, off:off + n], start=(kc == 0), stop=(kc == 7)))
                                                   for kc in range(8)], reads=[ws, (mt, (br, si))], writes=[p_])
                                    if br == 0:
                                        P.op("dve", lambda: nc.vector.tensor_tensor(a_[:, 0:n], p_[:, 0:n], z_[:, br, 0:n], ALU.mult),
                                             reads=[p_, (z_, br)], writes=[a_])
                                    else:
                                        P.op("dve", lambda: nc.vector.tensor_tensor(t_[:, 0:n], p_[:, 0:n], z_[:, br, 0:n], ALU.mult),
                                             reads=[p_, (z_, br)], writes=[t_])
                                        if br == 1:
                                            P.op("dve", lambda: nc.vector.tensor_tensor(a_[:, 0:n], a_[:, 0:n], t_[:, 0:n], ALU.add),
                                                 reads=[a_, t_], writes=[a_])
                                        else:
                                            P.op("dve", lambda: nc.vector.tensor_tensor(mg[:, d, off:off + n], a_[:, 0:n], t_[:, 0:n], ALU.add),
                                                 reads=[a_, t_], writes=[(mg, (d, si))])
                    for dp in range(8):
                        ws, wv = ring.load(wov[:, :, dp * 256:(dp + 1) * 256], NCH, 256)
                        for j in range(2):
                            d = dp * 2 + j
                            for si in grp:
                                start, n, isc = SEGS[si]
                                off = goff[si]
                                p_ = P.ps()
                                P.group("pe", [(lambda kc=kc: nc.tensor.matmul(p_[:, 0:n], wv[:, kc, j * 128:(j + 1) * 128],
                                                                               mg[:, kc, off:off + n], start=(kc == 0), stop=(kc == NCH - 1)))
                                               for kc in range(NCH)], reads=[ws] + [(mg, (kc, si)) for kc in range(NCH)], writes=[p_])
                                b = cnt % 2
                                cnt += 1
                                P.dma("sp", xr[b][:, 0:n], xs.t.ap()[d, :, start:start + n], reads=[(xs, (d, start))], writes=[xr[b]])
                                P.op("dve", lambda: nc.vector.scalar_tensor_tensor(
                                    xo[b][:, 0:n], p_[:, 0:n], modG[:, l, 1, d, isc:isc + 1], xr[b][:, 0:n], ALU.mult, ALU.add),
                                    reads=[p_, xr[b], (modG, (l, 1))], writes=[xo[b]])
                                P.dma("sp", xs.t.ap()[d, :, start:start + n], xo[b][:, 0:n], reads=[xo[b]], writes=[(xs, (d, start))])

        def final_norm():
            with Phase(P) as ph:
                xt = ph.sbuf("xt", [128, NCH, 512], F32)
                sqb = [ph.sbuf("sq%d" % i, [128, 512], BF16) for i in range(2)]
                rstd = ph.sbuf("rstd", [128, 512], F32)
                yo = [ph.sbuf("yo%d" % i, [128, 512], F32) for i in range(2)]
                for si, (start, n, isc) in enumerate(SEGS):
                    if isc:
                        continue

                    def consume(c, start=start, n=n):
                        y_ = yo[c % 2]
                        P.op("dve", lambda: nc.vector.scalar_tensor_tensor(
                            y_[:, 0:n], xt[:, c, 0:n], par[:, 0, _po["fing"] + c:_po["fing"] + c + 1], rstd[:, 0:n],
                            ALU.mult, ALU.mult), reads=[xt, rstd, par], writes=[y_])
                        P.dma("sp", yT.t.ap()[c, :, start - CTX:start - CTX + n], y_[:, 0:n], reads=[y_], writes=[(yT, (c, si))])
                    norm_tile(ph, xt, sqb, rstd, start, n, consume, None)

        def forward():
            for l in range(DEPTH):
                need_ctx = l < DEPTH - 1
                if l == 0:
                    pending.extend([ada_part(0, 1), ada_part(0, 2)])
                ffn(l, 0, True)
                ada_drain()
                dbgx("x_ffn1_%d" % l)
                if stop_after == "ffn1_%d" % l:
                    return
                mixer(l, need_ctx)
                dbgx("x_mix_%d" % l)
                if stop_after is not None and stop_after in ("h", "rnn", "conv", "att", "mix_%d" % l):
                    return
                if l == 0:
                    pending.extend([ada_part(1, 0), ada_part(1, 1), ada_part(1, 2)])
                ffn(l, 1, need_ctx)
                ada_drain()
                dbgx("x_ffn2_%d" % l)
                if stop_after == "ffn2_%d" % l:
                    return
            final_norm()

        forward()
        P.barrier(final=True)
    return nc, dbg


def _fm(v):
    v = np.asarray(v)
    return np.ascontiguousarray(v.reshape(-1, 128).T)


def _host_consts():
    inv = (np.float32(10000.0) ** (-np.arange(16, dtype=np.float32) * np.float32(2.0) / np.float32(32))).astype(np.float32)
    t = np.arange(SEQ)
    row = (t // 64).astype(np.float32)
    col = (t % 64).astype(np.float32)
    cosT = np.zeros((128, SEQ), np.float32)
    sinT = np.zeros((128, SEQ), np.float32)
    rperm = np.zeros((128, 128), np.float32)
    for p in range(128):
        d = p % 64
        a = d // 32
        half = (d % 32) // 16
        n = d % 16
        ang = ((row if a == 0 else col) * inv[n]).astype(np.float32)
        cosT[p] = np.cos(ang).astype(np.float32)
        sinT[p] = np.sin(ang).astype(np.float32)
        if half == 0:
            rperm[p + 16, p] = -1.0
        else:
            rperm[p - 16, p] = 1.0
    cst = np.zeros((128, 3, 128), np.float32)
    cst[:, 0, :] = 1.0 / 2048.0
    cst[:, 1, :] = 1.0 / 128.0
    cst[:, 2, :] = rperm
    return cosT, sinT, cst


def _prep_inputs(inp):
    f32 = np.float32
    cosT, sinT, cst = _host_consts()
    par = np.zeros((DEPTH, 128, NPAR), f32)
    rgw = np.zeros((DEPTH, 128, 8, 4, 128), f32)
    for l in range(DEPTH):
        def put(name, arr):
            arr = np.asarray(arr, f32)
            par[l, :, _po[name]:_po[name] + arr.shape[1]] = arr
        put("adab", _fm(inp["ada_b"][l]))
        put("ng", _fm(inp["norm_g"][l].reshape(-1)))
        put("rcw", inp["rnn_conv_w"][l].T.reshape(8, 128, 4).transpose(1, 0, 2).reshape(128, 32))
        put("rcb", _fm(inp["rnn_conv_b"][l]))
        put("rgbr", _fm(inp["rg_b_r"][l].reshape(-1)))
        put("rgbi", _fm(inp["rg_b_i"][l].reshape(-1)))
        put("rglam", _fm(inp["rg_lam"][l].reshape(-1)))
        put("cvw", inp["cv_dw_w"][l].T.reshape(8, 128, 31).transpose(1, 0, 2).reshape(128, 248))
        put("cvb", _fm(inp["cv_dw_b"][l]))
        put("cvg", _fm(inp["cv_ln_g"][l]))
        put("cvbb", _fm(inp["cv_ln_b"][l]))
        put("dalam", np.broadcast_to(inp["da_lam"][l].reshape(1, 256), (128, 256)))
        put("subg", inp["da_subln_g"][l].reshape(128, 1))
        put("fing", _fm(inp["final_g"]))
        for d in range(2):
            for g, nm in enumerate(("rg_w_r", "rg_w_i")):
                w = inp[nm][l, d]
                for c in range(8):
                    rgw[l, 0:64, c, d * 2 + g, 0:64] = w[2 * c]
                    rgw[l, 64:128, c, d * 2 + g, 64:128] = w[2 * c + 1]
    cvd = np.zeros((DEPTH, 128, 8, 31, 128), f32)
    ar = np.arange(128)
    for l in range(DEPTH):
        w = np.asarray(inp["cv_dw_w"][l], f32).T.reshape(8, 128, 31)
        for c in range(8):
            cvd[l, ar, c, :, ar] = w[c]
    shared = {
        "par": par, "rgw": rgw, "cvd": cvd, "cosT": cosT, "sinT": sinT, "cst": cst,
        "ada_w": np.ascontiguousarray(inp["ada_w"], dtype=f32),
        "ffn_w_gate": np.ascontiguousarray(inp["ffn_w_gate"], dtype=f32),
        "ffn_w_up": np.ascontiguousarray(inp["ffn_w_up"], dtype=f32),
        "ffn_w_down": np.ascontiguousarray(inp["ffn_w_down"], dtype=f32),
        "w_in": np.ascontiguousarray(inp["w_in"], dtype=f32),
        "rnn_w_out": np.ascontiguousarray(inp["rnn_w_out"], dtype=f32),
        "cv_w_out": np.ascontiguousarray(inp["cv_w_out"], dtype=f32),
        "da_w_o": np.ascontiguousarray(inp["da_w_o"], dtype=f32),
        "w_out": np.ascontiguousarray(inp["w_out"], dtype=f32),
    }
    maps = []
    B = inp["x"].shape[0]
    for b in range(B):
        xt = np.concatenate([inp["ctx"][b], inp["x"][b]], axis=0).astype(f32)
        xT = np.ascontiguousarray(xt.T).reshape(NCH, 128, NT)
        cc = np.stack([_fm(inp["c"][b]), _fm(inp["c_ctx"])], axis=-1).astype(f32)
        m = dict(shared)
        m["xT"] = xT
        m["cc"] = np.ascontiguousarray(cc)
        maps.append(m)
    return maps


_CACHE = {}


def kernel(**inputs):
    inp = {k: np.asarray(v) for k, v in inputs.items()}
    maps = _prep_inputs(inp)
    if "nc" not in _CACHE:
        _CACHE["nc"] = build_program()[0]
    nc = _CACHE["nc"]
    res = run_bass_kernel_spmd(nc, maps, core_ids=list(range(len(maps))))
    outs = []
    for r in res.results:
        yT = np.asarray(r["yT"]).reshape(D, SEQ)
        outs.append(np.ascontiguousarray(yT.T))
    return np.stack(outs, axis=0).astype(np.float32)
```

```python
from contextlib import ExitStack
import math
import numpy as np
import concourse.bass as bass
import concourse.mybir as mybir
from concourse.bass_utils import run_bass_kernel_spmd

F32 = mybir.dt.float32
BF16 = mybir.dt.bfloat16
AF = mybir.ActivationFunctionType
ALU = mybir.AluOpType
AX = mybir.AxisListType

D = 2048
NCH = 16
SEQ = 2048
CTX = 256
NT = SEQ + CTX
DFF = 5632
NFF = 44
DIN = 13312
EPS = 1e-6
DEPTH = 2
OFF_X, OFF_Y, OFF_G, OFF_Q, OFF_K, OFF_V, OFF_Z = 0, 1024, 2048, 4096, 5120, 6144, 7168
SEGS = [(0, 256, 1), (256, 512, 0), (768, 512, 0), (1280, 512, 0), (1792, 512, 0)]
PAD = 16
PL = PAD + CTX + PAD + SEQ + PAD


def ppos(t):
    return t + PAD if t < CTX else t + 2 * PAD


_po = {}
_n = 0
for _name, _w in (("adab", 144), ("ng", 48), ("rcw", 32), ("rcb", 8), ("rgbr", 16), ("rgbi", 16), ("rglam", 16),
                  ("cvw", 248), ("cvb", 8), ("cvg", 8), ("cvbb", 8), ("dalam", 256), ("subg", 1), ("fing", 16)):
    _po[_name] = _n
    _n += _w
NPAR = _n

WHOLE = "__whole__"


class _St:
    __slots__ = ("w", "r")

    def __init__(self):
        self.w = None
        self.r = []


class Buf:
    def __init__(self, t, name):
        self.t = t
        self.name = name
        self.st = {}

    def __getitem__(self, idx):
        return self.t[idx]


class Tok:
    __slots__ = ("eng", "seq", "sem", "val")

    def __init__(self, eng, seq=None, sem=None, val=None):
        self.eng = eng
        self.seq = seq
        self.sem = sem
        self.val = val


class Prog:
    ENGS = ("pe", "act", "dve", "pool", "sp")
    NDMA = 8

    def __init__(self, nc, es):
        self.nc = nc
        self.es = es
        self.e = {"pe": nc.tensor, "act": nc.scalar, "dve": nc.vector, "pool": nc.gpsimd, "sp": nc.sync}
        self.sem = {k: es.enter_context(nc.semaphore("s_" + k)) for k in self.ENGS}
        self.nseq = {k: 0 for k in self.ENGS}
        self.ninc = {k: 0 for k in self.ENGS}
        self.last_ins = {k: None for k in self.ENGS}
        self.incs = {k: [] for k in self.ENGS}
        self.recent = {k: [] for k in self.ENGS}
        self.seen = {k: {} for k in self.ENGS}
        self.dsem = {q: [es.enter_context(nc.semaphore("d_%s%d" % (q, i))) for i in range(self.NDMA)]
                     for q in ("sp", "pool")}
        self.dcnt = {q: 0 for q in ("sp", "pool")}
        self.dlast = {q: [0] * self.NDMA for q in ("sp", "pool")}
        self.uid = 0
        self.psb = None
        self.psi = 0
        self.psn = 7

    def sbuf(self, name, shape, dt):
        return Buf(self.es.enter_context(self.nc.sbuf_tensor("sb_" + name, list(shape), dt)), name)

    def dram(self, name, shape, dt, kind="Internal"):
        return Buf(self.nc.dram_tensor(name, list(shape), dt, kind=kind), name)

    def init_psum(self):
        self.psb = [Buf(self.es.enter_context(self.nc.psum_tensor("psb%d" % i, [128, 512], F32)), "ps%d" % i)
                    for i in range(8)]

    def ps(self):
        b = self.psb[self.psi % self.psn]
        self.psi += 1
        return b

    def _resolve(self, tok):
        if tok.eng == "dma":
            return tok.sem, tok.val
        eng = tok.eng
        lst = self.incs[eng]
        lo, hi = 0, len(lst)
        while lo < hi:
            mid = (lo + hi) // 2
            if lst[mid][0] >= tok.seq:
                hi = mid
            else:
                lo = mid + 1
        if lo < len(lst):
            return self.sem[eng], lst[lo][1]
        rec = self.recent[eng]
        k = 0
        while rec[k][0] < tok.seq:
            k += 1
        sq, ins = rec[k]
        del rec[:k + 1]
        self.ninc[eng] += 1
        ins.then_inc(self.sem[eng], 1)
        lst.append((sq, self.ninc[eng]))
        return self.sem[eng], self.ninc[eng]

    def _wait(self, eng, tok):
        if tok is None:
            return
        if tok.eng == "pe" and eng == "pe":
            return
        sem, val = self._resolve(tok)
        key = id(sem)
        if self.seen[eng].get(key, 0) >= val:
            return
        self.seen[eng][key] = val
        self.e[eng].wait_ge(sem, val)

    @staticmethod
    def _norm(x):
        if isinstance(x, Buf):
            return x, WHOLE
        return x

    @staticmethod
    def _states(buf, key):
        if key == WHOLE:
            if WHOLE not in buf.st:
                buf.st[WHOLE] = _St()
            return list(buf.st.values())
        if key not in buf.st:
            buf.st[key] = _St()
        out = [buf.st[key]]
        if WHOLE in buf.st:
            out.append(buf.st[WHOLE])
        return out

    def deps(self, eng, reads, writes):
        for x in reads:
            buf, key = self._norm(x)
            for st in self._states(buf, key):
                self._wait(eng, st.w)
        for x in writes:
            buf, key = self._norm(x)
            for st in self._states(buf, key):
                self._wait(eng, st.w)
                for r in st.r:
                    self._wait(eng, r)

    def record(self, tok, reads, writes):
        for x in reads:
            buf, key = self._norm(x)
            sts = self._states(buf, key) if key == WHOLE else [self._states(buf, key)[0]]
            for st in sts:
                if tok.eng != "dma":
                    st.r = [r for r in st.r if r.eng != tok.eng]
                st.r.append(tok)
        for x in writes:
            buf, key = self._norm(x)
            if key == WHOLE:
                buf.st = {WHOLE: _St()}
                buf.st[WHOLE].w = tok
            else:
                st = self._states(buf, key)[0]
                st.w = tok
                st.r = []

    def op(self, eng, fn, reads=(), writes=()):
        self.deps(eng, reads, writes)
        ins = fn()
        self.nseq[eng] += 1
        self.last_ins[eng] = ins
        rec = self.recent[eng]
        rec.append((self.nseq[eng], ins))
        if len(rec) > 96:
            del rec[:32]
        self.record(Tok(eng, seq=self.nseq[eng]), reads, writes)
        return ins

    def group(self, eng, fns, reads=(), writes=()):
        self.deps(eng, reads, writes)
        ins = None
        for fn in fns:
            ins = fn()
            self.nseq[eng] += 1
        self.last_ins[eng] = ins
        rec = self.recent[eng]
        rec.append((self.nseq[eng], ins))
        if len(rec) > 96:
            del rec[:32]
        self.record(Tok(eng, seq=self.nseq[eng]), reads, writes)

    def dma(self, q, out, in_, reads=(), writes=(), **kw):
        self.deps(q, reads, writes)
        j = self.dcnt[q]
        self.dcnt[q] += 1
        i = j % self.NDMA
        s = self.dsem[q][i]
        rnd = j // self.NDMA
        if rnd > 0:
            key = id(s)
            if self.seen[q].get(key, 0) < 16 * rnd:
                self.seen[q][key] = 16 * rnd
                self.e[q].wait_ge(s, 16 * rnd)
        ins = self.e[q].dma_start(out=out, in_=in_, **kw)
        ins.then_inc(s, 16)
        self.dlast[q][i] = 16 * (rnd + 1)
        tok = Tok("dma", sem=s, val=16 * (rnd + 1))
        self.record(tok, reads, writes)
        return tok

    def barrier(self, final=False):
        toks = [Tok(e, seq=self.nseq[e]) for e in ("pe", "act", "dve") if self.nseq[e] > 0]
        dt = [Tok("dma", sem=self.dsem["sp"][i], val=self.dlast["sp"][i]) for i in range(self.NDMA)
              if self.dlast["sp"][i] > 0]
        if final:
            dt += [Tok("dma", sem=self.dsem["pool"][i], val=self.dlast["pool"][i]) for i in range(self.NDMA)
                   if self.dlast["pool"][i] > 0]
        for e in ("pe", "act", "dve", "sp"):
            for t in toks:
                if t.eng != e:
                    self._wait(e, t)
            for t in dt:
                self._wait(e, t)


class Phase:
    def __init__(self, P):
        self.P = P
        self.es = ExitStack()

    def __enter__(self):
        self.es.__enter__()
        return self

    def sbuf(self, name, shape, dt):
        self.P.uid += 1
        return Buf(self.es.enter_context(self.P.nc.sbuf_tensor("sp_%s_%d" % (name, self.P.uid), list(shape), dt)), name)

    def __exit__(self, *a):
        if a[0] is None:
            self.P.barrier()
        return self.es.__exit__(*a)


class Ring:
    def __init__(self, P, nslots, elems, tag=""):
        self.P = P
        self.slots = [P.sbuf("wring%s%d" % (tag, i), [128, elems], BF16) for i in range(nslots)]
        self.i = 0

    def load(self, src, a, b):
        s = self.slots[self.i % len(self.slots)]
        self.i += 1
        view = s.t[:, 0:a * b].rearrange("p (a b) -> p a b", a=a)
        self.P.dma("pool", view, src, writes=[s])
        return s, view


def build_program(debug=(), stop_after=None):
    nc = bass.Bass("TRN2", target_bir_lowering=False)
    dbg = {}
    with ExitStack() as es:
        P = Prog(nc, es)
        P.init_psum()
        IN = "ExternalInput"
        xT = P.dram("xT", [NCH, 128, NT], F32, kind=IN)
        ccd = P.dram("cc", [128, NCH, 2], F32, kind=IN)
        pard = P.dram("par", [DEPTH, 128, NPAR], F32, kind=IN)
        rgwd = P.dram("rgw", [DEPTH, 128, 8, 4, 128], F32, kind=IN)
        cosd = P.dram("cosT", [128, SEQ], F32, kind=IN)
        sind = P.dram("sinT", [128, SEQ], F32, kind=IN)
        cstd = P.dram("cst", [128, 3, 128], F32, kind=IN)
        cvdd = P.dram("cvd", [DEPTH, 128, 8, 31, 128], F32, kind=IN)
        ada_w = P.dram("ada_w", [DEPTH, D, 9 * D], F32, kind=IN)
        w_gate = P.dram("ffn_w_gate", [DEPTH, 2, D, DFF], F32, kind=IN)
        w_up = P.dram("ffn_w_up", [DEPTH, 2, D, DFF], F32, kind=IN)
        w_down = P.dram("ffn_w_down", [DEPTH, 2, DFF, D], F32, kind=IN)
        w_in = P.dram("w_in", [DEPTH, D, DIN], F32, kind=IN)
        rnn_wo = P.dram("rnn_w_out", [DEPTH, 1024, D], F32, kind=IN)
        cv_wo = P.dram("cv_w_out", [DEPTH, 1024, D], F32, kind=IN)
        da_wo = P.dram("da_w_o", [DEPTH, 1024, D], F32, kind=IN)
        w_out = P.dram("w_out", [DEPTH, D, D], F32, kind=IN)
        yT = P.dram("yT", [NCH, 128, SEQ], F32, kind="ExternalOutput")
        xs = P.dram("xs", [NCH, 128, NT], F32)
        mscr = P.dram("mscr", [3, 8, 128, NT], BF16, kind="ExternalOutput" if "m" in debug else "Internal")
        zsig = P.dram("zsig", [48, 128, NT], F32)
        cvo = P.dram("cvo", [8, 128, NT], F32)

        def dbgx(name):
            if name in debug:
                t = P.dram("dbg_" + name, [NCH, 128, NT], F32, kind="ExternalOutput")
                P.dma("sp", t.t.ap(), xs.t.ap(), reads=[xs], writes=[t])
                dbg[name] = t

        xsv = xs.t.ap().rearrange("c p t -> p c t")

        ring = Ring(P, 5, 6144)
        par = P.sbuf("par", [128, DEPTH, NPAR], F32)
        cst = P.sbuf("cst", [128, 3, 128], F32)
        onesb = P.sbuf("onesb", [128, 128], BF16)
        modr = P.sbuf("modr", [128, DEPTH, 144, 2], F32)
        modA = P.sbuf("modA", [128, DEPTH, 3, NCH, 2], F32)
        modG = P.sbuf("modG", [128, DEPTH, 3, NCH, 2], F32)
        cneg = P.sbuf("cneg", [128, DEPTH, 2, 16], F32)
        lamv = P.sbuf("lamv", [128, DEPTH, 4], F32)
        P.dma("sp", par[:], pard.t.ap().rearrange("l p n -> p l n"), writes=[par])
        P.dma("sp", cst[:], cstd.t.ap(), writes=[cst])
        P.op("dve", lambda: nc.vector.memset(onesb[:], 1.0), writes=[onesb])
        ones128b = P.sbuf("ones128b", [128, 128], BF16)
        P.op("dve", lambda: nc.vector.memset(ones128b[:], 1.0 / 128.0), writes=[ones128b])
        ones2048b = P.sbuf("ones2048b", [128, 128], BF16)
        P.op("dve", lambda: nc.vector.memset(ones2048b[:], 1.0 / 2048.0), writes=[ones2048b])
        epsc = P.sbuf("epsc", [128, 2], F32)
        P.op("dve", lambda: nc.vector.memset(epsc[:], EPS), writes=[epsc])
        ones2048 = cst[:, 0, :]
        ones128 = cst[:, 1, :]
        rperm = cst[:, 2, :]
        ones1024 = None

        def pr(l, name, i0=0, n=1):
            o = _po[name] + i0
            return par[:, l, o:o + n]

        for c in range(NCH):
            P.dma("sp", xs.t.ap()[c], xT.t.ap()[c], writes=[(xs, c)])
        P.barrier()

        scb = P.sbuf("scb", [128, NCH, 2], BF16)
        with Phase(P) as ph:
            ccs = ph.sbuf("ccs", [128, NCH, 2], F32)
            tmp = ph.sbuf("tmpm", [128, 64], F32)
            tmp2 = ph.sbuf("tmpm2", [128, 64], F32)
            P.dma("sp", ccs[:], ccd.t.ap(), writes=[ccs])
            P.op("act", lambda: nc.scalar.activation(scb[:], ccs[:], AF.Silu), reads=[ccs], writes=[scb])
            for l in range(DEPTH):
                lam_init = 0.8 - 0.6 * math.exp(-0.3 * l)
                P.op("act", lambda: nc.scalar.activation(tmp[:, 0:16], pr(l, "rglam", 0, 16), AF.Exp, scale=-1.0),
                     reads=[par], writes=[tmp])
                P.op("act", lambda: nc.scalar.activation(tmp2[:, 0:16], tmp[:, 0:16], AF.Ln, bias=1.0, scale=1.0),
                     reads=[tmp], writes=[tmp2])
                P.op("dve", lambda: nc.vector.tensor_scalar_mul(cneg[:, l, 0, :], tmp2[:, 0:16], -8.0),
                     reads=[tmp2], writes=[(cneg, (l, 0))])
                P.op("dve", lambda: nc.vector.tensor_scalar_mul(cneg[:, l, 1, :], tmp2[:, 0:16], -16.0),
                     reads=[tmp2], writes=[(cneg, (l, 1))])
                dl = _po["dalam"]
                P.op("dve", lambda: nc.vector.tensor_tensor(tmp[:, 0:64], par[:, l, dl:dl + 64], par[:, l, dl + 64:dl + 128],
                                                            ALU.mult), reads=[par, tmp2], writes=[tmp])
                P.op("dve", lambda: nc.vector.reduce_sum(tmp2[:, 0:1], tmp[:, 0:64], AX.X), reads=[tmp], writes=[tmp2])
                P.op("dve", lambda: nc.vector.tensor_tensor(tmp[:, 0:64], par[:, l, dl + 128:dl + 192],
                                                            par[:, l, dl + 192:dl + 256], ALU.mult),
                     reads=[par, tmp2], writes=[tmp])
                P.op("dve", lambda: nc.vector.reduce_sum(tmp2[:, 1:2], tmp[:, 0:64], AX.X), reads=[tmp], writes=[tmp2])
                P.op("act", lambda: nc.scalar.activation(tmp[:, 0:2], tmp2[:, 0:2], AF.Exp), reads=[tmp2], writes=[tmp])
                P.op("dve", lambda: nc.vector.scalar_tensor_tensor(lamv[:, l, 0:1], tmp[:, 1:2], -lam_init, tmp[:, 0:1],
                                                                   ALU.add, ALU.subtract), reads=[tmp], writes=[(lamv, (l, 0))])
                P.op("dve", lambda: nc.vector.tensor_scalar_mul(lamv[:, l, 1:2], pr(l, "subg"), 1.0 - lam_init),
                     reads=[par], writes=[(lamv, (l, 1))])

        mps = P.psb[7]

        def ada_part(l, k):
            awv = ada_w.t.ap()[l].rearrange("(kc p) n -> p kc n", p=128)
            def mods_mm(nb, ws, wv):
                for j in range(2):
                    n = nb * 2 + j
                    P.group("pe", [
                        (lambda kc=kc: nc.tensor.matmul(mps[:, 2 * n:2 * n + 2], wv[:, kc, j * 128:(j + 1) * 128],
                                                        scb[:, kc, :], start=(kc == 0), stop=(kc == NCH - 1)))
                        for kc in range(NCH)], reads=[ws, scb], writes=[(mps, k)])
            prev = None
            for nb in range(24 * k, 24 * (k + 1)):
                ws, wv = ring.load(awv[:, :, nb * 256:(nb + 1) * 256], NCH, 256)
                if prev is not None:
                    mods_mm(*prev)
                prev = (nb, ws, wv)
                yield
            mods_mm(*prev)
            mv = mps[:, 0:288].rearrange("p (n j) -> p n j", j=2)
            for j in range(2):
                P.op("dve", lambda: nc.vector.tensor_tensor(modr[:, l, 48 * k:48 * (k + 1), j], mv[:, 48 * k:48 * (k + 1), j],
                                                            pr(l, "adab", 48 * k, 48), ALU.add),
                     reads=[(mps, k), par], writes=[(modr, (l, k))])
            for j in range(2):
                P.op("dve", lambda: nc.vector.scalar_tensor_tensor(
                    modA[:, l, k, :, j], modr[:, l, (3 * k + 1) * 16:(3 * k + 2) * 16, j], 1.0,
                    pr(l, "ng", k * 16, 16), ALU.add, ALU.mult), reads=[(modr, (l, k)), par], writes=[(modA, (l, k))])
                P.op("dve", lambda: nc.vector.tensor_scalar_mul(
                    modG[:, l, k, :, j], modr[:, l, (3 * k + 2) * 16:(3 * k + 3) * 16, j], 1.0 if k == 1 else 0.5),
                    reads=[(modr, (l, k))], writes=[(modG, (l, k))])

        pending = []

        def ada_step():
            while pending:
                try:
                    next(pending[0])
                    return
                except StopIteration:
                    pending.pop(0)

        def ada_drain():
            while pending:
                ada_step()

        pending.append(ada_part(0, 0))
        ada_drain()

        def shiftp(l, k, c, j):
            return modr[:, l, (3 * k) * 16 + c, j:j + 1]

        def rsqrt_eps(dst, src, n):
            P.op("act", lambda: nc.scalar.activation(dst[:, 0:n], src[:, 0:n], AF.Ln, bias=epsc[:, 0:1], scale=1.0),
                 reads=[src, epsc], writes=[dst])
            P.op("act", lambda: nc.scalar.activation(dst[:, 0:n], dst[:, 0:n], AF.Exp, scale=-0.5), reads=[dst], writes=[dst])

        def norm_tile(ph, xt, sqb, rstd, start, n, consume, gains):
            P.dma("sp", xt[:, :, 0:n], xsv[:, :, start:start + n], reads=[xs], writes=[xt])
            sp_ = P.ps()
            for c in range(NCH):
                q = sqb[c % 2]
                P.op("act", lambda: nc.scalar.activation(q[:, 0:n], xt[:, c, 0:n], AF.Square), reads=[xt], writes=[q])
                P.op("pe", lambda: nc.tensor.matmul(sp_[:, 0:n], ones2048b[:], q[:, 0:n], start=(c == 0), stop=(c == NCH - 1)),
                     reads=[q, ones2048b], writes=[sp_])
            rsqrt_eps(rstd, sp_, n)
            for c in range(NCH):
                consume(c)

        def subtiles(with_ctx):
            if with_ctx:
                return [[(0, 256, 1), (256, 512, 0), (768, 512, 0)], [(1280, 512, 0), (1792, 512, 0)]]
            return [[(256, 512, 0), (768, 512, 0)], [(1280, 512, 0), (1792, 512, 0)]]

        def ffn(l, k, with_ctx):
            kn = 0 if k == 0 else 2
            KN = kn
            wgv = w_gate.t.ap()[l, k].rearrange("(kc p) n -> p kc n", p=128)
            wuv = w_up.t.ap()[l, k].rearrange("(kc p) n -> p kc n", p=128)
            wdv = w_down.t.ap()[l, k].rearrange("(f p) n -> p f n", p=128)
            for ST in subtiles(with_ctx):
                T = sum(s[1] for s in ST)
                offs = []
                o = 0
                for s in ST:
                    offs.append(o)
                    o += s[1]
                with Phase(P) as ph:
                    hT = ph.sbuf("hT", [128, NCH, T], BF16)
                    act = ph.sbuf("act", [128, 22, T], BF16)
                    xt = ph.sbuf("xt", [128, NCH, 256], F32)
                    sqb = [ph.sbuf("sq%d" % i, [128, 512], BF16) for i in range(2)]
                    tmpb = [ph.sbuf("tm%d" % i, [128, 512], F32) for i in range(2)]
                    rstd = ph.sbuf("rstd", [128, 512], F32)
                    xr = [ph.sbuf("xr%d" % i, [128, 512], F32) for i in range(2)]
                    xo = [ph.sbuf("xo%d" % i, [128, 512], F32) for i in range(2)]
                    cnt = [0]
                    halves = []
                    for si, (start, n, isc) in enumerate(ST):
                        for h0 in range(0, n, 256):
                            halves.append((si, start + h0, min(256, n - h0), isc, offs[si] + h0))
                    for (si, start, n, isc, off) in halves:

                        def consume(c, start=start, n=n, isc=isc, off=off, si=si):
                            tb = tmpb[c % 2]
                            P.op("dve", lambda: nc.vector.scalar_tensor_tensor(
                                tb[:, 0:n], xt[:, c, 0:n], modA[:, l, kn, c, isc:isc + 1], rstd[:, 0:n], ALU.mult, ALU.mult),
                                reads=[xt, rstd, (modA, (l, KN))], writes=[tb])
                            P.op("act", lambda: nc.scalar.activation(hT[:, c, off:off + n], tb[:, 0:n], AF.Identity,
                                                                     bias=shiftp(l, kn, c, isc), scale=1.0),
                                 reads=[tb, (modr, (l, KN))], writes=[(hT, si)])
                        norm_tile(ph, xt, sqb, rstd, start, n, consume, None)
                    for half in range(2):
                        for fp in range(11):
                            c0 = (half * 22 + fp * 2) * 128
                            gs, gv = ring.load(wgv[:, :, c0:c0 + 256], NCH, 256)
                            us, uv = ring.load(wuv[:, :, c0:c0 + 256], NCH, 256)
                            ada_step()
                            for j in range(2):
                                f = fp * 2 + j
                                for si, (start, n, isc) in enumerate(ST):
                                    off = offs[si]
                                    pg = P.ps()
                                    pu = P.ps()
                                    P.group("pe", [(lambda kc=kc: nc.tensor.matmul(
                                        pg[:, 0:n], gv[:, kc, j * 128:(j + 1) * 128], hT[:, kc, off:off + n],
                                        start=(kc == 0), stop=(kc == NCH - 1))) for kc in range(NCH)],
                                        reads=[gs, (hT, si)], writes=[pg])
                                    P.group("pe", [(lambda kc=kc: nc.tensor.matmul(
                                        pu[:, 0:n], uv[:, kc, j * 128:(j + 1) * 128], hT[:, kc, off:off + n],
                                        start=(kc == 0), stop=(kc == NCH - 1))) for kc in range(NCH)],
                                        reads=[us, (hT, si)], writes=[pu])
                                    tb = tmpb[cnt[0] % 2]
                                    cnt[0] += 1
                                    P.op("act", lambda: nc.scalar.activation(tb[:, 0:n], pg[:, 0:n], AF.Silu),
                                         reads=[pg], writes=[tb])
                                    P.op("dve", lambda: nc.vector.tensor_tensor(act[:, f, off:off + n], tb[:, 0:n], pu[:, 0:n],
                                                                                ALU.mult),
                                         reads=[tb, pu], writes=[(act, (f, si))])
                        for dp in range(8):
                            ds_, dv = ring.load(wdv[:, half * 22:(half + 1) * 22, dp * 256:(dp + 1) * 256], 22, 256)
                            ada_step()
                            for j in range(2):
                                d = dp * 2 + j
                                for si, (start, n, isc) in enumerate(ST):
                                    off = offs[si]
                                    pd = P.ps()
                                    P.group("pe", [(lambda f=f: nc.tensor.matmul(
                                        pd[:, 0:n], dv[:, f, j * 128:(j + 1) * 128], act[:, f, off:off + n],
                                        start=(f == 0), stop=(f == 21))) for f in range(22)],
                                        reads=[ds_] + [(act, (f, si)) for f in range(22)], writes=[pd])
                                    b = cnt[0] % 2
                                    cnt[0] += 1
                                    P.dma("sp", xr[b][:, 0:n], xs.t.ap()[d, :, start:start + n], reads=[(xs, (d, start))],
                                          writes=[xr[b]])
                                    P.op("dve", lambda: nc.vector.scalar_tensor_tensor(
                                        xo[b][:, 0:n], pd[:, 0:n], modG[:, l, kn, d, isc:isc + 1], xr[b][:, 0:n],
                                        ALU.mult, ALU.add), reads=[pd, xr[b], (modG, (l, kn))], writes=[xo[b]])
                                    P.dma("sp", xs.t.ap()[d, :, start:start + n], xo[b][:, 0:n], reads=[xo[b]],
                                          writes=[(xs, (d, start))])

        def proj(ws, wv, j0, hT, seg, si):
            start, n, isc = seg
            p_ = P.ps()
            P.group("pe", [(lambda kc=kc: nc.tensor.matmul(p_[:, 0:n], wv[:, kc, j0:j0 + 128], hT[:, kc, start:start + n],
                                                           start=(kc == 0), stop=(kc == NCH - 1))) for kc in range(NCH)],
                    reads=[ws, (hT, si)], writes=[p_])
            return p_

        def mixer(l, need_ctx):
            KN = 1
            lam_init = 0.8 - 0.6 * math.exp(-0.3 * l)
            winv = w_in.t.ap()[l].rearrange("(kc p) n -> p kc n", p=128)
            mv = mscr.t.ap()
            with Phase(P) as phh:
                hT = phh.sbuf("hmix", [128, NCH, NT], BF16)
                with Phase(P) as ph:
                    xt = ph.sbuf("xt", [128, NCH, 512], F32)
                    sqb = [ph.sbuf("sq%d" % i, [128, 512], BF16) for i in range(2)]
                    tmpb = [ph.sbuf("tm%d" % i, [128, 512], F32) for i in range(2)]
                    rstd = ph.sbuf("rstd", [128, 512], F32)
                    for si, (start, n, isc) in enumerate(SEGS):
                        def consume(c, start=start, n=n, isc=isc, si=si):
                            tb = tmpb[c % 2]
                            P.op("dve", lambda: nc.vector.scalar_tensor_tensor(
                                tb[:, 0:n], xt[:, c, 0:n], modA[:, l, 1, c, isc:isc + 1], rstd[:, 0:n], ALU.mult, ALU.mult),
                                reads=[xt, rstd, (modA, (l, KN))], writes=[tb])
                            P.op("act", lambda: nc.scalar.activation(hT[:, c, start:start + n], tb[:, 0:n], AF.Identity,
                                                                     bias=shiftp(l, 1, c, isc), scale=1.0),
                                 reads=[tb, (modr, (l, KN))], writes=[(hT, si)])
                        norm_tile(ph, xt, sqb, rstd, start, n, consume, None)
                if stop_after == "h":
                    return
                with Phase(P) as ph:
                    zb = [ph.sbuf("zb%d" % i, [128, 512], F32) for i in range(3)]
                    cnt = 0
                    for zp in range(24):
                        ws, wv = ring.load(winv[:, :, OFF_Z + zp * 256:OFF_Z + (zp + 1) * 256], NCH, 256)
                        for j in range(2):
                            zc = zp * 2 + j
                            for si, seg in enumerate(SEGS):
                                start, n, isc = seg
                                if isc and not need_ctx:
                                    continue
                                p_ = proj(ws, wv, j * 128, hT, seg, si)
                                b = zb[cnt % 3]
                                cnt += 1
                                P.op("act", lambda: nc.scalar.activation(b[:, 0:n], p_[:, 0:n], AF.Sigmoid), reads=[p_], writes=[b])
                                P.dma("sp", zsig.t.ap()[zc, :, start:start + n], b[:, 0:n], reads=[b], writes=[(zsig, (zc, si))])
                with Phase(P) as ph:
                    xp = ph.sbuf("xp", [128, PL], F32)
                    u = ph.sbuf("u", [128, PL], F32)
                    ub = ph.sbuf("ub", [128, PL], BF16)
                    hf = ph.sbuf("hf", [128, PL], F32)
                    hb = ph.sbuf("hb", [128, PL], F32)
                    RB = ph.sbuf("RB", [128, NT], F32)
                    IB = ph.sbuf("IB", [128, NT], F32)
                    gy = [ph.sbuf("gy0", [128, 512], F32)] * 2
                    mo = [ph.sbuf("mo0", [128, 512], BF16)] * 2
                    P.op("dve", lambda: nc.vector.memset(xp[:], 0.0), writes=[xp])
                    rgv = rgwd.t.ap()[l]
                    for c in range(8):
                        wxs, wxv = ring.load(winv[:, :, OFF_X + c * 128:OFF_X + (c + 1) * 128], NCH, 128)
                        wys, wyv = ring.load(winv[:, :, OFF_Y + c * 128:OFF_Y + (c + 1) * 128], NCH, 128)
                        rgs, rgt = ring.load(rgv[:, c], 4, 128)
                        for si, seg in enumerate(SEGS):
                            start, n, isc = seg
                            p_ = proj(wxs, wxv, 0, hT, seg, si)
                            pp = ppos(start)
                            P.op("act", lambda: nc.scalar.copy(xp[:, pp:pp + n], p_[:, 0:n]), reads=[p_], writes=[(xp, si)])
                        lo, hi = PAD, PL - PAD
                        P.op("dve", lambda: nc.vector.tensor_scalar(u[:, lo:hi], xp[:, lo - 1:hi - 1], pr(l, "rcw", c * 4, 1),
                                                                    pr(l, "rcb", c, 1), ALU.mult, ALU.add),
                             reads=[xp, par], writes=[u])
                        for j in range(1, 4):
                            P.op("dve", lambda: nc.vector.scalar_tensor_tensor(
                                u[:, lo:hi], xp[:, lo + j - 1:hi + j - 1], pr(l, "rcw", c * 4 + j, 1), u[:, lo:hi],
                                ALU.mult, ALU.add), reads=[xp, par, u], writes=[u])
                        P.op("act", lambda: nc.scalar.copy(ub[:, lo:hi], u[:, lo:hi]), reads=[u], writes=[ub])
                        for d in range(2):
                            hbuf = hf if d == 0 else hb
                            order = list(range(5)) if d == 0 else [0, 4, 3, 2, 1]
                            for si in range(5):
                                start, n, isc = SEGS[si]
                                pp = ppos(start)
                                pr_ = P.ps()
                                pi_ = P.ps()
                                P.op("pe", lambda: nc.tensor.matmul(pr_[:, 0:n], rgt[:, d * 2 + 0, :], ub[:, pp:pp + n],
                                                                    start=True, stop=True), reads=[rgs, ub], writes=[pr_])
                                P.op("pe", lambda: nc.tensor.matmul(pi_[:, 0:n], rgt[:, d * 2 + 1, :], ub[:, pp:pp + n],
                                                                    start=True, stop=True), reads=[rgs, ub], writes=[pi_])
                                P.op("act", lambda: nc.scalar.activation(RB[:, start:start + n], pr_[:, 0:n], AF.Sigmoid,
                                                                         bias=pr(l, "rgbr", d * 8 + c, 1), scale=1.0),
                                     reads=[pr_, par], writes=[(RB, si)])
                                P.op("act", lambda: nc.scalar.activation(IB[:, start:start + n], pi_[:, 0:n], AF.Sigmoid,
                                                                         bias=pr(l, "rgbi", d * 8 + c, 1), scale=1.0),
                                     reads=[pi_, par], writes=[(IB, si)])
                            for si in range(5):
                                start, n, isc = SEGS[si]
                                P.op("act", lambda: nc.scalar.activation(xp[:, ppos(start):ppos(start) + n], RB[:, start:start + n], AF.Exp,
                                                                         scale=cneg[:, l, 1, d * 8 + c:d * 8 + c + 1]),
                                     reads=[(RB, si), cneg], writes=[(xp, si)])
                                P.op("act", lambda: nc.scalar.activation(RB[:, start:start + n], RB[:, start:start + n], AF.Exp,
                                                                         scale=cneg[:, l, 0, d * 8 + c:d * 8 + c + 1]),
                                     reads=[(RB, si), cneg], writes=[(RB, si)])
                            for si in range(5):
                                start, n, isc = SEGS[si]
                                P.op("act", lambda: nc.scalar.activation(xp[:, ppos(start):ppos(start) + n], xp[:, ppos(start):ppos(start) + n], AF.Sqrt,
                                                                         bias=1.0, scale=-1.0),
                                     reads=[(xp, si)], writes=[(xp, si)])
                            prev = None
                            for si in order:
                                start, n, isc = SEGS[si]
                                pp = ppos(start)
                                P.op("dve", lambda: nc.vector.tensor_tensor(IB[:, start:start + n], IB[:, start:start + n],
                                                                            xp[:, ppos(start):ppos(start) + n], ALU.mult),
                                     reads=[(IB, si), (xp, si)], writes=[(IB, si)])
                                P.op("dve", lambda: nc.vector.tensor_tensor(IB[:, start:start + n], IB[:, start:start + n],
                                                                            u[:, pp:pp + n], ALU.mult),
                                     reads=[(IB, si), u], writes=[(IB, si)])
                                if prev is None:
                                    init = 0.0
                                else:
                                    init = hbuf[:, prev:prev + 1]
                                if d == 0:
                                    P.op("dve", lambda: nc.vector.tensor_tensor_scan(hbuf[:, pp:pp + n], RB[:, start:start + n],
                                                                                     IB[:, start:start + n], init, ALU.mult, ALU.add),
                                         reads=[(RB, si), (IB, si), hbuf], writes=[hbuf])
                                    prev = pp + n - 1
                                else:
                                    P.op("dve", lambda: nc.vector.tensor_tensor_scan(
                                        hbuf[:, pp:pp + n][:, ::-1], RB[:, start:start + n][:, ::-1], IB[:, start:start + n][:, ::-1],
                                        init, ALU.mult, ALU.add), reads=[(RB, si), (IB, si), hbuf], writes=[hbuf])
                                    prev = pp
                        for si, seg in enumerate(SEGS):
                            start, n, isc = seg
                            if isc and not need_ctx:
                                continue
                            pp = ppos(start)
                            p_ = proj(wys, wyv, 0, hT, seg, si)
                            b = si % 2
                            P.op("act", lambda: nc.scalar.activation(gy[b][:, 0:n], p_[:, 0:n], AF.Gelu_apprx_tanh),
                                 reads=[p_], writes=[gy[b]])
                            P.op("dve", lambda: nc.vector.tensor_tensor(hf[:, pp:pp + n], hf[:, pp:pp + n], hb[:, pp:pp + n], ALU.add),
                                 reads=[hf, hb], writes=[hf])
                            P.op("dve", lambda: nc.vector.tensor_tensor(mo[b][:, 0:n], gy[b][:, 0:n], hf[:, pp:pp + n], ALU.mult),
                                 reads=[gy[b], hf], writes=[mo[b]])
                            P.dma("sp", mv[0, c, :, start:start + n], mo[b][:, 0:n], reads=[mo[b]], writes=[(mscr, (0, c, si))])
                if stop_after == "rnn":
                    return
                with Phase(P) as ph:
                    gpb = [ph.sbuf("gpb%d" % i, [128, PL], BF16) for i in range(2)]
                    sg = [ph.sbuf("sg%d" % i, [128, 512], F32) for i in range(2)]
                    co = [ph.sbuf("co%d" % i, [128, 512], F32) for i in range(2)]
                    for g_ in gpb:
                        P.op("dve", lambda: nc.vector.memset(g_[:], 0.0), writes=[g_])
                    cnt = 0
                    for c in range(8):
                        gp = gpb[c % 2]
                        was, wav = ring.load(winv[:, :, OFF_G + c * 128:OFF_G + (c + 1) * 128], NCH, 128)
                        wgs, wgv_ = ring.load(winv[:, :, OFF_G + 1024 + c * 128:OFF_G + 1024 + (c + 1) * 128], NCH, 128)
                        dgs, dgv = ring.load(cvdd.t.ap()[l][:, c], 31, 128)
                        for si, seg in enumerate(SEGS):
                            start, n, isc = seg
                            pp = ppos(start)
                            pa = proj(was, wav, 0, hT, seg, si)
                            pg = proj(wgs, wgv_, 0, hT, seg, si)
                            b = si % 2
                            P.op("act", lambda: nc.scalar.activation(sg[b][:, 0:n], pg[:, 0:n], AF.Sigmoid), reads=[pg], writes=[sg[b]])
                            P.op("dve", lambda: nc.vector.tensor_tensor(gp[:, pp:pp + n], sg[b][:, 0:n], pa[:, 0:n], ALU.mult),
                                 reads=[sg[b], pa], writes=[(gp, si)])
                        for si, seg in enumerate(SEGS):
                            start, n, isc = seg
                            if isc and not need_ctx:
                                continue
                            pp = ppos(start)
                            pc = P.ps()
                            P.group("pe", [(lambda j=j: nc.tensor.matmul(pc[:, 0:n], dgv[:, j, :], gp[:, pp + j - 15:pp + j - 15 + n],
                                                                         start=(j == 0), stop=(j == 30))) for j in range(31)],
                                    reads=[dgs, gp], writes=[pc])
                            b = cnt % 2
                            cnt += 1
                            P.op("act", lambda: nc.scalar.activation(co[b][:, 0:n], pc[:, 0:n], AF.Identity,
                                                                     bias=pr(l, "cvb", c, 1), scale=1.0),
                                 reads=[pc, par], writes=[co[b]])
                            P.dma("sp", cvo.t.ap()[c, :, start:start + n], co[b][:, 0:n], reads=[co[b]], writes=[(cvo, (c, si))])
                with Phase(P) as ph:
                    ct = ph.sbuf("ct", [128, 8, 512], F32)
                    sqb = [ph.sbuf("sq%d" % i, [128, 512], F32) for i in range(2)]
                    mean = ph.sbuf("mean", [128, 512], F32)
                    rstd = ph.sbuf("rstd", [128, 512], F32)
                    tb = [ph.sbuf("tb%d" % i, [128, 512], F32) for i in range(2)]
                    mo = [ph.sbuf("mo%d" % i, [128, 512], BF16) for i in range(2)]
                    cvv = cvo.t.ap().rearrange("c p t -> p c t")
                    for si, seg in enumerate(SEGS):
                        start, n, isc = seg
                        if isc and not need_ctx:
                            continue
                        P.dma("sp", ct[:, :, 0:n], cvv[:, :, start:start + n], reads=[cvo], writes=[ct])
                        pm = P.ps()
                        pq = P.ps()
                        for c in range(8):
                            q = sqb[c % 2]
                            P.op("act", lambda: nc.scalar.activation(q[:, 0:n], ct[:, c, 0:n], AF.Square), reads=[ct], writes=[q])
                            P.op("pe", lambda: nc.tensor.matmul(pm[:, 0:n], ones128, ct[:, c, 0:n], start=(c == 0), stop=(c == 7)),
                                 reads=[ct, cst], writes=[pm])
                            P.op("pe", lambda: nc.tensor.matmul(pq[:, 0:n], ones128, q[:, 0:n], start=(c == 0), stop=(c == 7)),
                                 reads=[q, cst], writes=[pq])
                        P.op("act", lambda: nc.scalar.mul(mean[:, 0:n], pm[:, 0:n], 0.125), reads=[pm], writes=[mean])
                        P.op("dve", lambda: nc.vector.tensor_tensor(rstd[:, 0:n], mean[:, 0:n], mean[:, 0:n], ALU.mult),
                             reads=[mean], writes=[rstd])
                        P.op("dve", lambda: nc.vector.scalar_tensor_tensor(rstd[:, 0:n], pq[:, 0:n], 0.125, rstd[:, 0:n],
                                                                           ALU.mult, ALU.subtract), reads=[pq, rstd], writes=[rstd])
                        rsqrt_eps(rstd, rstd, n)
                        for c in range(8):
                            b = c % 2
                            P.op("dve", lambda: nc.vector.tensor_tensor(tb[b][:, 0:n], ct[:, c, 0:n], mean[:, 0:n], ALU.subtract),
                                 reads=[ct, mean], writes=[tb[b]])
                            P.op("dve", lambda: nc.vector.tensor_tensor(tb[b][:, 0:n], tb[b][:, 0:n], rstd[:, 0:n], ALU.mult),
                                 reads=[tb[b], rstd], writes=[tb[b]])
                            P.op("act", lambda: nc.scalar.activation(mo[b][:, 0:n], tb[b][:, 0:n], AF.Silu,
                                                                     bias=pr(l, "cvbb", c, 1), scale=pr(l, "cvg", c, 1)),
                                 reads=[tb[b], par], writes=[mo[b]])
                            P.dma("sp", mv[1, c, :, start:start + n], mo[b][:, 0:n], reads=[mo[b]], writes=[(mscr, (1, c, si))])
                if stop_after == "conv":
                    return
                with Phase(P) as ph:
                    QT = ph.sbuf("QT", [128, NT], BF16)
                    KT = ph.sbuf("KT", [128, NT], BF16)
                    Vp = ph.sbuf("Vp", [128, 18, 256], BF16)
                    cs = [ph.sbuf("cs%d" % i, [128, 2, 512], F32) for i in range(2)]
                    qf = [ph.sbuf("qf%d" % i, [128, 512], F32) for i in range(2)]
                    t1 = [ph.sbuf("t1%d" % i, [128, 512], F32) for i in range(2)]
                    t2 = [ph.sbuf("t2%d" % i, [128, 512], F32) for i in range(2)]
                    Pt = [ph.sbuf("Pt%d" % i, [128, 512], BF16) for i in range(4)]
                    rz = [ph.sbuf("rz%d" % i, [128, 512], F32) for i in range(2)]
                    ob = [ph.sbuf("ob%d" % i, [128, 512], F32) for i in range(2)]
                    osq = ph.sbuf("osq", [128, 512], BF16)
                    orr = ph.sbuf("orr", [128, 512], F32)
                    mo = [ph.sbuf("mo%d" % i, [128, 512], BF16) for i in range(2)]
                    neglam = lamv[:, l, 0:1]
                    gsub = lamv[:, l, 1:2]
                    cnt = 0
                    P.psn = 4
                    for h in range(8):
                        if h % 2 == 0:
                            wvs, wvv = ring.load(winv[:, :, OFF_V + h * 128:OFF_V + (h + 2) * 128], NCH, 256)
                            for tc in range(18):
                                si = 0 if tc < 2 else 1 + (tc - 2) // 4
                                p_ = P.ps()
                                P.group("pe", [(lambda kc=kc: nc.tensor.matmul(p_[:, 0:256], hT[:, kc, tc * 128:(tc + 1) * 128],
                                                                               wvv[:, kc, :], start=(kc == 0), stop=(kc == NCH - 1)))
                                               for kc in range(NCH)], reads=[wvs, (hT, si)], writes=[p_])
                                if tc % 2 == 0:
                                    P.op("act", lambda: nc.scalar.copy(Vp[:, tc, :], p_[:, 0:256]), reads=[p_], writes=[(Vp, tc)])
                                else:
                                    P.op("dve", lambda: nc.vector.tensor_copy(Vp[:, tc, :], p_[:, 0:256]), reads=[p_], writes=[(Vp, tc)])
                        wqs, wqv = ring.load(winv[:, :, OFF_Q + h * 128:OFF_Q + (h + 1) * 128], NCH, 128)
                        wks, wkv = ring.load(winv[:, :, OFF_K + h * 128:OFF_K + (h + 1) * 128], NCH, 128)
                        for si, seg in enumerate(SEGS):
                            start, n, isc = seg
                            if isc:
                                if need_ctx:
                                    p_ = proj(wqs, wqv, 0, hT, seg, si)
                                    P.op("act", lambda: nc.scalar.copy(QT[:, start:start + n], p_[:, 0:n]), reads=[p_], writes=[(QT, si)])
                                p_ = proj(wks, wkv, 0, hT, seg, si)
                                P.op("act", lambda: nc.scalar.copy(KT[:, start:start + n], p_[:, 0:n]), reads=[p_], writes=[(KT, si)])
                                continue
                            cb = cs[si % 2]
                            P.dma("sp", cb[:, 0, :], cosd.t.ap()[:, start - CTX:start - CTX + n], writes=[cb])
                            P.dma("sp", cb[:, 1, :], sind.t.ap()[:, start - CTX:start - CTX + n], writes=[cb])
                            for (ws_, wv_, dst) in ((wqs, wqv, QT), (wks, wkv, KT)):
                                p_ = proj(ws_, wv_, 0, hT, seg, si)
                                b = cnt % 2
                                cnt += 1
                                P.op("act", lambda: nc.scalar.copy(qf[b][:, 0:n], p_[:, 0:n]), reads=[p_], writes=[qf[b]])
                                p2 = P.ps()
                                P.op("pe", lambda: nc.tensor.matmul(p2[:, 0:n], rperm, qf[b][:, 0:n], start=True, stop=True),
                                     reads=[qf[b], cst], writes=[p2])
                                P.op("dve", lambda: nc.vector.tensor_tensor(t1[b][:, 0:n], qf[b][:, 0:n], cb[:, 0, 0:n], ALU.mult),
                                     reads=[qf[b], cb], writes=[t1[b]])
                                P.op("dve", lambda: nc.vector.tensor_tensor(t2[b][:, 0:n], p2[:, 0:n], cb[:, 1, 0:n], ALU.mult),
                                     reads=[p2, cb], writes=[t2[b]])
                                P.op("dve", lambda: nc.vector.tensor_tensor(dst[:, start:start + n], t1[b][:, 0:n], t2[b][:, 0:n], ALU.add),
                                     reads=[t1[b], t2[b]], writes=[(dst, si)])
                        hoff = (h % 2) * 128
                        for si, seg in enumerate(SEGS):
                            qs, qn, isc = seg
                            if isc and not need_ctx:
                                continue
                            keys = [0, 1] if isc else list(range(18))
                            nk = len(keys)
                            O = [P.psb[4], P.psb[5]]
                            Z = [P.psb[6], P.psb[7]]

                            def scores(kc):
                                ksi = 0 if kc < 2 else 1 + (kc - 2) // 4
                                out = []
                                for comp in range(2):
                                    sc = P.ps()
                                    P.op("pe", lambda: nc.tensor.matmul(
                                        sc[:, 0:qn], KT[comp * 64:(comp + 1) * 64, kc * 128:(kc + 1) * 128],
                                        QT[comp * 64:(comp + 1) * 64, qs:qs + qn], start=True, stop=True),
                                        reads=[(KT, ksi), (QT, si)], writes=[sc])
                                    out.append(sc)
                                return out
                            s_cur = scores(keys[0])
                            for i, kc in enumerate(keys):
                                s_next = scores(keys[i + 1]) if i + 1 < nk else None
                                for comp in range(2):
                                    pt = Pt[(i % 2) * 2 + comp]
                                    P.op("act", lambda: nc.scalar.activation(pt[:, 0:qn], s_cur[comp][:, 0:qn], AF.Exp, scale=0.125),
                                         reads=[s_cur[comp]], writes=[pt])
                                for comp in range(2):
                                    pt = Pt[(i % 2) * 2 + comp]
                                    P.op("pe", lambda: nc.tensor.matmul(O[comp][:, 0:qn], Vp[:, kc, hoff:hoff + 128], pt[:, 0:qn],
                                                                        start=(i == 0), stop=(i == nk - 1)),
                                         reads=[(Vp, kc), pt], writes=[O[comp]])
                                    P.op("pe", lambda: nc.tensor.matmul(Z[comp][:, 0:qn], onesb[:], pt[:, 0:qn],
                                                                        start=(i == 0), stop=(i == nk - 1)),
                                         reads=[onesb, pt], writes=[Z[comp]])
                                s_cur = s_next
                            for comp in range(2):
                                P.op("act", lambda: nc.scalar.activation(rz[comp][:, 0:qn], Z[comp][:, 0:qn], AF.Ln), reads=[Z[comp]], writes=[rz[comp]])
                                P.op("act", lambda: nc.scalar.activation(rz[comp][:, 0:qn], rz[comp][:, 0:qn], AF.Exp, scale=-1.0), reads=[rz[comp]], writes=[rz[comp]])
                                P.op("dve", lambda: nc.vector.tensor_tensor(ob[comp][:, 0:qn], O[comp][:, 0:qn], rz[comp][:, 0:qn], ALU.mult),
                                     reads=[O[comp], rz[comp]], writes=[ob[comp]])
                            P.op("dve", lambda: nc.vector.scalar_tensor_tensor(ob[0][:, 0:qn], ob[1][:, 0:qn], neglam, ob[0][:, 0:qn],
                                                                               ALU.mult, ALU.add), reads=[ob[0], ob[1], lamv], writes=[ob[0]])
                            P.op("act", lambda: nc.scalar.activation(osq[:, 0:qn], ob[0][:, 0:qn], AF.Square), reads=[ob[0]], writes=[osq])
                            pm = P.ps()
                            P.op("pe", lambda: nc.tensor.matmul(pm[:, 0:qn], ones128b[:], osq[:, 0:qn], start=True, stop=True),
                                 reads=[osq, ones128b], writes=[pm])
                            rsqrt_eps(orr, pm, qn)
                            P.op("dve", lambda: nc.vector.tensor_tensor(orr[:, 0:qn], orr[:, 0:qn], ob[0][:, 0:qn], ALU.mult),
                                 reads=[orr, ob[0]], writes=[orr])
                            b = si % 2
                            P.op("act", lambda: nc.scalar.activation(mo[b][:, 0:qn], orr[:, 0:qn], AF.Identity, bias=0.0, scale=gsub),
                                 reads=[orr, lamv], writes=[mo[b]])
                            P.dma("sp", mv[2, h, :, qs:qs + qn], mo[b][:, 0:qn], reads=[mo[b]], writes=[(mscr, (2, h, si))])
            P.psn = 7
            if stop_after == "att":
                return
            bwv = [t.t.ap()[l].rearrange("(kc p) n -> p kc n", p=128) for t in (rnn_wo, cv_wo, da_wo)]
            wov = w_out.t.ap()[l].rearrange("(kc p) n -> p kc n", p=128)
            mvv = mscr.t.ap().rearrange("b c p t -> p b c t")
            with Phase(P) as ph:
                mt = ph.sbuf("mt", [128, 3, 8, 1024], BF16)
                mg = ph.sbuf("mg", [128, NCH, 1024], BF16)
                zt = [ph.sbuf("zt%d" % i, [128, 3, 512], F32) for i in range(2)]
                accb = [ph.sbuf("ac%d" % i, [128, 512], F32) for i in range(2)]
                tmb = [ph.sbuf("tmg%d" % i, [128, 512], F32) for i in range(2)]
                xr = [ph.sbuf("xr%d" % i, [128, 512], F32) for i in range(2)]
                xo = [ph.sbuf("xo%d" % i, [128, 512], F32) for i in range(2)]
                cnt = 0
                groups = [[0, 1], [2, 3], [4]] if need_ctx else [[1, 2], [3, 4]]
                for gi, grp in enumerate(groups):
                    goff = {}
                    o = 0
                    for si in grp:
                        goff[si] = o
                        o += SEGS[si][1]
                    for si in grp:
                        start, n, isc = SEGS[si]
                        off = goff[si]
                        for br in range(3):
                            P.dma("sp", mt[:, br, :, off:off + n], mvv[:, br, :, start:start + n], reads=[mscr],
                                  writes=[(mt, (br, si))])
                    for dp in range(8):
                        wts = []
                        for br in range(3):
                            wts.append(ring.load(bwv[br][:, :, dp * 256:(dp + 1) * 256], 8, 256))
                        for j in range(2):
                            d = dp * 2 + j
                            for si in grp:
                                start, n, isc = SEGS[si]
                                off = goff[si]
                                z_ = zt[cnt % 2]
                                a_ = accb[cnt % 2]
                                t_ = tmb[cnt % 2]
                                cnt += 1
                                for br in range(3):
                                    P.dma("sp", z_[:, br, 0:n], zsig.t.ap()[br * 16 + d, :, start:start + n], reads=[zsig],
                                          writes=[(z_, br)])
                                for br in range(3):
                                    ws, wv = wts[br]
                                    p_ = P.ps()
                                    P.group("pe", [(lambda kc=kc: nc.tensor.matmul(p_[:, 0:n], wv[:, kc, j * 128:(j + 1) * 128],
                                                                                   mt[:, br, kc, off:off + n], start=(kc == 0), stop=(kc == 7)))
                                                   for kc in range(8)], reads=[ws, (mt, (br, si))], writes=[p_])
                                    if br == 0:
                                        P.op("dve", lambda: nc.vector.tensor_tensor(a_[:, 0:n], p_[:, 0:n], z_[:, br, 0:n], ALU.mult),
                                             reads=[p_, (z_, br)], writes=[a_])
                                    else:
                                        P.op("dve", lambda: nc.vector.tensor_tensor(t_[:, 0:n], p_[:, 0:n], z_[:, br, 0:n], ALU.mult),
                                             reads=[p_, (z_, br)], writes=[t_])
                                        if br == 1:
                                            P.op("dve", lambda: nc.vector.tensor_tensor(a_[:, 0:n], a_[:, 0:n], t_[:, 0:n], ALU.add),
                                                 reads=[a_, t_], writes=[a_])
                                        else:
                                            P.op("dve", lambda: nc.vector.tensor_tensor(mg[:, d, off:off + n], a_[:, 0:n], t_[:, 0:n], ALU.add),
                                                 reads=[a_, t_], writes=[(mg, (d, si))])
                    for dp in range(8):
                        ws, wv = ring.load(wov[:, :, dp * 256:(dp + 1) * 256], NCH, 256)
                        for j in range(2):
                            d = dp * 2 + j
                            for si in grp:
                                start, n, isc = SEGS[si]
                                off = goff[si]
                                p_ = P.ps()
                                P.group("pe", [(lambda kc=kc: nc.tensor.matmul(p_[:, 0:n], wv[:, kc, j * 128:(j + 1) * 128],
                                                                               mg[:, kc, off:off + n], start=(kc == 0), stop=(kc == NCH - 1)))
                                               for kc in range(NCH)], reads=[ws] + [(mg, (kc, si)) for kc in range(NCH)], writes=[p_])
                                b = cnt % 2
                                cnt += 1
                                P.dma("sp", xr[b][:, 0:n], xs.t.ap()[d, :, start:start + n], reads=[(xs, (d, start))], writes=[xr[b]])
                                P.op("dve", lambda: nc.vector.scalar_tensor_tensor(
                                    xo[b][:, 0:n], p_[:, 0:n], modG[:, l, 1, d, isc:isc + 1], xr[b][:, 0:n], ALU.mult, ALU.add),
                                    reads=[p_, xr[b], (modG, (l, 1))], writes=[xo[b]])
                                P.dma("sp", xs.t.ap()[d, :, start:start + n], xo[b][:, 0:n], reads=[xo[b]], writes=[(xs, (d, start))])

        def final_norm():
            with Phase(P) as ph:
                xt = ph.sbuf("xt", [128, NCH, 512], F32)
                sqb = [ph.sbuf("sq%d" % i, [128, 512], BF16) for i in range(2)]
                rstd = ph.sbuf("rstd", [128, 512], F32)
                yo = [ph.sbuf("yo%d" % i, [128, 512], F32) for i in range(2)]
                for si, (start, n, isc) in enumerate(SEGS):
                    if isc:
                        continue

                    def consume(c, start=start, n=n):
                        y_ = yo[c % 2]
                        P.op("dve", lambda: nc.vector.scalar_tensor_tensor(
                            y_[:, 0:n], xt[:, c, 0:n], par[:, 0, _po["fing"] + c:_po["fing"] + c + 1], rstd[:, 0:n],
                            ALU.mult, ALU.mult), reads=[xt, rstd, par], writes=[y_])
                        P.dma("sp", yT.t.ap()[c, :, start - CTX:start - CTX + n], y_[:, 0:n], reads=[y_], writes=[(yT, (c, si))])
                    norm_tile(ph, xt, sqb, rstd, start, n, consume, None)

        def forward():
            for l in range(DEPTH):
                need_ctx = l < DEPTH - 1
                if l == 0:
                    pending.extend([ada_part(0, 1), ada_part(0, 2)])
                ffn(l, 0, True)
                ada_drain()
                dbgx("x_ffn1_%d" % l)
                if stop_after == "ffn1_%d" % l:
                    return
                mixer(l, need_ctx)
                dbgx("x_mix_%d" % l)
                if stop_after is not None and stop_after in ("h", "rnn", "conv", "att", "mix_%d" % l):
                    return
                if l == 0:
                    pending.extend([ada_part(1, 0), ada_part(1, 1), ada_part(1, 2)])
                ffn(l, 1, need_ctx)
                ada_drain()
                dbgx("x_ffn2_%d" % l)
                if stop_after == "ffn2_%d" % l:
                    return
            final_norm()

        forward()
        P.barrier(final=True)
    return nc, dbg


def _fm(v):
    v = np.asarray(v)
    return np.ascontiguousarray(v.reshape(-1, 128).T)


def _host_consts():
    inv = (np.float32(10000.0) ** (-np.arange(16, dtype=np.float32) * np.float32(2.0) / np.float32(32))).astype(np.float32)
    t = np.arange(SEQ)
    row = (t // 64).astype(np.float32)
    col = (t % 64).astype(np.float32)
    cosT = np.zeros((128, SEQ), np.float32)
    sinT = np.zeros((128, SEQ), np.float32)
    rperm = np.zeros((128, 128), np.float32)
    for p in range(128):
        d = p % 64
        a = d // 32
        half = (d % 32) // 16
        n = d % 16
        ang = ((row if a == 0 else col) * inv[n]).astype(np.float32)
        cosT[p] = np.cos(ang).astype(np.float32)
        sinT[p] = np.sin(ang).astype(np.float32)
        if half == 0:
            rperm[p + 16, p] = -1.0
        else:
            rperm[p - 16, p] = 1.0
    cst = np.zeros((128, 3, 128), np.float32)
    cst[:, 0, :] = 1.0 / 2048.0
    cst[:, 1, :] = 1.0 / 128.0
    cst[:, 2, :] = rperm
    return cosT, sinT, cst


def _prep_inputs(inp):
    f32 = np.float32
    cosT, sinT, cst = _host_consts()
    par = np.zeros((DEPTH, 128, NPAR), f32)
    rgw = np.zeros((DEPTH, 128, 8, 4, 128), f32)
    for l in range(DEPTH):
        def put(name, arr):
            arr = np.asarray(arr, f32)
            par[l, :, _po[name]:_po[name] + arr.shape[1]] = arr
        put("adab", _fm(inp["ada_b"][l]))
        put("ng", _fm(inp["norm_g"][l].reshape(-1)))
        put("rcw", inp["rnn_conv_w"][l].T.reshape(8, 128, 4).transpose(1, 0, 2).reshape(128, 32))
        put("rcb", _fm(inp["rnn_conv_b"][l]))
        put("rgbr", _fm(inp["rg_b_r"][l].reshape(-1)))
        put("rgbi", _fm(inp["rg_b_i"][l].reshape(-1)))
        put("rglam", _fm(inp["rg_lam"][l].reshape(-1)))
        put("cvw", inp["cv_dw_w"][l].T.reshape(8, 128, 31).transpose(1, 0, 2).reshape(128, 248))
        put("cvb", _fm(inp["cv_dw_b"][l]))
        put("cvg", _fm(inp["cv_ln_g"][l]))
        put("cvbb", _fm(inp["cv_ln_b"][l]))
        put("dalam", np.broadcast_to(inp["da_lam"][l].reshape(1, 256), (128, 256)))
        put("subg", inp["da_subln_g"][l].reshape(128, 1))
        put("fing", _fm(inp["final_g"]))
        for d in range(2):
            for g, nm in enumerate(("rg_w_r", "rg_w_i")):
                w = inp[nm][l, d]
                for c in range(8):
                    rgw[l, 0:64, c, d * 2 + g, 0:64] = w[2 * c]
                    rgw[l, 64:128, c, d * 2 + g, 64:128] = w[2 * c + 1]
    cvd = np.zeros((DEPTH, 128, 8, 31, 128), f32)
    ar = np.arange(128)
    for l in range(DEPTH):
        w = np.asarray(inp["cv_dw_w"][l], f32).T.reshape(8, 128, 31)
        for c in range(8):
            cvd[l, ar, c, :, ar] = w[c]
    shared = {
        "par": par, "rgw": rgw, "cvd": cvd, "cosT": cosT, "sinT": sinT, "cst": cst,
        "ada_w": np.ascontiguousarray(inp["ada_w"], dtype=f32),
        "ffn_w_gate": np.ascontiguousarray(inp["ffn_w_gate"], dtype=f32),
        "ffn_w_up": np.ascontiguousarray(inp["ffn_w_up"], dtype=f32),
        "ffn_w_down": np.ascontiguousarray(inp["ffn_w_down"], dtype=f32),
        "w_in": np.ascontiguousarray(inp["w_in"], dtype=f32),
        "rnn_w_out": np.ascontiguousarray(inp["rnn_w_out"], dtype=f32),
        "cv_w_out": np.ascontiguousarray(inp["cv_w_out"], dtype=f32),
        "da_w_o": np.ascontiguousarray(inp["da_w_o"], dtype=f32),
        "w_out": np.ascontiguousarray(inp["w_out"], dtype=f32),
    }
    maps = []
    B = inp["x"].shape[0]
    for b in range(B):
        xt = np.concatenate([inp["ctx"][b], inp["x"][b]], axis=0).astype(f32)
        xT = np.ascontiguousarray(xt.T).reshape(NCH, 128, NT)
        cc = np.stack([_fm(inp["c"][b]), _fm(inp["c_ctx"])], axis=-1).astype(f32)
        m = dict(shared)
        m["xT"] = xT
        m["cc"] = np.ascontiguousarray(cc)
        maps.append(m)
    return maps


_CACHE = {}


def kernel(**inputs):
    inp = {k: np.asarray(v) for k, v in inputs.items()}
    maps = _prep_inputs(inp)
    if "nc" not in _CACHE:
        _CACHE["nc"] = build_program()[0]
    nc = _CACHE["nc"]
    res = run_bass_kernel_spmd(nc, maps, core_ids=list(range(len(maps))))
    outs = []
    for r in res.results:
        yT = np.asarray(r["yT"]).reshape(D, SEQ)
        outs.append(np.ascontiguousarray(yT.T))
    return np.stack(outs, axis=0).astype(np.float32)
```
